# Optimizing a Trainium2 kernel written in Bass

```python
import math
import jax
import jax.numpy as jnp
from jax import lax
import numpy as np

D_MODEL = 2048
BATCH = 4
SEQ = 2048
DEPTH = 2
DEC_BATCH = 128
DEC_SEQ = 8
PAST_LEN = 16384
PAGE_SIZE = 128

S5_WIDTH = D_MODEL // 4
S5_GROUP = 16
S5_GROUPS = S5_WIDTH // S5_GROUP
S5_STATE = 64
S5_DT_MIN = 0.001
S5_DT_MAX = 0.1
RWKV_HEAD_DIM = 64
RWKV_WIDTH = D_MODEL // 4
RWKV_HEADS = RWKV_WIDTH // RWKV_HEAD_DIM
RWKV_DECAY_LORA = 96
RWKV_ICLR_LORA = 96
RWKV_GATE_LORA = 256
RWKV_COLS = 3 * RWKV_WIDTH + RWKV_DECAY_LORA + RWKV_ICLR_LORA + RWKV_GATE_LORA
RWKV_GN_EPS = 64e-5
GDN_HEAD_DIM = 128
GDN_WIDTH = D_MODEL // 2
GDN_HEADS = GDN_WIDTH // GDN_HEAD_DIM
GDN_CONV = 4
GDN_CHUNK = 64
N_BRANCH = 3
IN_SIZES = (S5_WIDTH, RWKV_COLS, 3 * GDN_WIDTH, GDN_WIDTH, GDN_HEADS, GDN_HEADS, D_MODEL, D_MODEL, D_MODEL)
IN_COLS = S5_WIDTH + RWKV_COLS + 4 * GDN_WIDTH + 2 * GDN_HEADS + N_BRANCH * D_MODEL
D_FF = 5632
FFN_CONV = 3
NORM_EPS = 1e-6

kernel_name = "hybrid_s5_rwkv7_gdn_convffn_step"


def split_cols(x, sizes):
    idx = np.cumsum(np.asarray(sizes))[:-1].tolist()
    return jnp.split(x, idx, axis=-1)


def rmsnorm(x, g):
    xf = x.astype(jnp.float32)
    ms = jnp.mean(xf * xf, axis=-1, keepdims=True)
    return (xf * lax.rsqrt(ms + NORM_EPS) * g.astype(jnp.float32)).astype(x.dtype)


def l2norm(x):
    return x * lax.rsqrt(jnp.sum(x * x, axis=-1, keepdims=True) + 1e-6)


def causal_dwconv(u, buf, w):
    width, ch = w.shape
    ext = jnp.concatenate([buf.astype(u.dtype), u], axis=1)
    out = lax.conv_general_dilated(ext, w.astype(u.dtype)[:, None, :], window_strides=(1,), padding='VALID',
                                   dimension_numbers=('NWC', 'WIO', 'NWC'), feature_group_count=ch)
    return out, ext[:, ext.shape[1] - (width - 1):]


def _linear_combine(left, right):
    a1, b1 = left
    a2, b2 = right
    return a1 * a2, a2 * b1 + b2


def s5_mixer(u, x0_re, x0_im, lam_re, lam_im, b_re, b_im, c_re, c_im, d, log_step, w_glu):
    f32 = jnp.float32
    bsz, L, _ = u.shape
    uf = u.astype(f32)
    ug = uf.reshape(bsz, L, S5_GROUPS, S5_GROUP)
    lam = lax.complex(lam_re.astype(f32), lam_im.astype(f32))
    step = jnp.exp(log_step.astype(f32))[:, None]
    lam_bar = jnp.exp(lam * step)
    b_bar = ((lam_bar - 1.0) / lam)[..., None] * lax.complex(b_re.astype(f32), b_im.astype(f32))
    bu = lax.complex(jnp.einsum('gpc,blgc->blgp', b_bar.real, ug),
                     jnp.einsum('gpc,blgc->blgp', b_bar.imag, ug))
    x0 = lax.complex(x0_re.astype(f32), x0_im.astype(f32))
    bu = bu.at[:, 0].add(lam_bar * x0)
    a = jnp.broadcast_to(lam_bar, bu.shape)
    _, xs = lax.associative_scan(_linear_combine, (a, bu), axis=1)
    y = (jnp.einsum('gcp,blgp->blgc', c_re.astype(f32), xs.real)
         - jnp.einsum('gcp,blgp->blgc', c_im.astype(f32), xs.imag))
    y = y.reshape(bsz, L, S5_WIDTH) + d.astype(f32) * uf
    y = jax.nn.gelu(y)
    y = y * jax.nn.sigmoid(y @ w_glu.astype(f32))
    x_last = xs[:, -1]
    return y.astype(u.dtype), x_last.real, x_last.imag


def _rwkv7_step(S, inp):
    r_t, w_t, k_t, v_t, a_t, b_t = inp
    sa = jnp.einsum('bhvk,bhk->bhv', S, a_t)
    S = S * w_t[:, :, None, :] + sa[..., None] * b_t[:, :, None, :] + v_t[..., None] * k_t[:, :, None, :]
    return S, jnp.einsum('bhvk,bhk->bhv', S, r_t)


def rwkv7_mixer(h, shift0, wkv0, mu, w0, w2, a0, a2, g2, k_k, k_a, r_k, ln_w, ln_b):
    f32 = jnp.float32
    bsz, L, _ = h.shape
    hf = h.astype(f32)
    prev = jnp.concatenate([shift0.astype(f32)[:, None], hf[:, :-1]], axis=1)
    hs = hf + (prev - hf) * mu.astype(f32)
    r, k, v, wd, ad, gd = split_cols(hs, (RWKV_WIDTH, RWKV_WIDTH, RWKV_WIDTH,
                                          RWKV_DECAY_LORA, RWKV_ICLR_LORA, RWKV_GATE_LORA))
    w_log = -jax.nn.softplus(-(w0.astype(f32) + jnp.tanh(wd) @ w2.astype(f32))) - 0.5
    decay = jnp.exp(-jnp.exp(w_log))
    a = jax.nn.sigmoid(a0.astype(f32) + ad @ a2.astype(f32))
    g = jax.nn.sigmoid(gd) @ g2.astype(f32)
    heads = lambda t: t.reshape(bsz, L, RWKV_HEADS, RWKV_HEAD_DIM)
    kk = l2norm(heads(k * k_k.astype(f32)))
    k = k * (1.0 + (a - 1.0) * k_a.astype(f32))
    r, decay, k, v, a = heads(r), heads(decay), heads(k), heads(v), heads(a)
    seq = tuple(jnp.moveaxis(t, 1, 0) for t in (r, decay, k, v, -kk, kk * a))
    wkv_last, ys = lax.scan(_rwkv7_step, wkv0.astype(f32), seq)
    y = jnp.moveaxis(ys, 0, 1)
    mean = jnp.mean(y, axis=-1, keepdims=True)
    var = jnp.mean(jnp.square(y - mean), axis=-1, keepdims=True)
    y = ((y - mean) * lax.rsqrt(var + RWKV_GN_EPS)).reshape(bsz, L, RWKV_WIDTH)
    y = y * ln_w.astype(f32) + ln_b.astype(f32)
    bonus = jnp.sum(r * k * r_k.astype(f32), axis=-1, keepdims=True) * v
    y = (y + bonus.reshape(bsz, L, RWKV_WIDTH)) * g
    return y.astype(h.dtype), hf[:, -1], wkv_last


def chunk_gated_delta_rule(q, k, v, g, beta, s0):
    bsz, L, H, dk = q.shape
    dv = v.shape[-1]
    c = min(GDN_CHUNK, L)
    n = -(-L // c)
    pad = n * c - L
    if pad:
        pw = ((0, 0), (0, pad), (0, 0), (0, 0))
        q, k, v = jnp.pad(q, pw), jnp.pad(k, pw), jnp.pad(v, pw)
        g, beta = jnp.pad(g, pw[:3]), jnp.pad(beta, pw[:3])

    def chunks(t):
        t = t.reshape((bsz, n, c, H) + t.shape[3:])
        return jnp.moveaxis(jnp.moveaxis(t, 1, 0), 3, 2)

    qc, kc, vc = chunks(q), chunks(k), chunks(v)
    gc = jnp.cumsum(chunks(g), axis=-1)
    bc = chunks(beta)
    kb, vb = kc * bc[..., None], vc * bc[..., None]
    causal = jnp.tril(jnp.ones((c, c), dtype=bool))
    strict = jnp.tril(jnp.ones((c, c), dtype=bool), -1)
    decay = jnp.exp(jnp.where(causal, gc[..., :, None] - gc[..., None, :], -jnp.inf))
    lower = jnp.where(strict, jnp.einsum('nbhid,nbhjd->nbhij', kb, kc) * decay, 0.0)
    rhs = jnp.concatenate([vb, kb * jnp.exp(gc)[..., None]], axis=-1)
    sol = lax.linalg.triangular_solve(lower + jnp.eye(c, dtype=q.dtype), rhs, left_side=True,
                                      lower=True, unit_diagonal=True)
    u, w = sol[..., :dv], sol[..., dv:]
    intra = jnp.einsum('nbhid,nbhjd->nbhij', qc, kc) * decay

    def body(S, xs):
        q_i, k_i, u_i, w_i, g_i, a_i = xs
        v_new = u_i - jnp.einsum('bhcd,bhde->bhce', w_i, S)
        o_i = (jnp.einsum('bhcd,bhde->bhce', q_i * jnp.exp(g_i)[..., None], S)
               + jnp.einsum('bhij,bhje->bhie', a_i, v_new))
        g_last = g_i[..., -1:]
        S = (S * jnp.exp(g_last)[..., None]
             + jnp.einsum('bhcd,bhce->bhde', k_i * jnp.exp(g_last - g_i)[..., None], v_new))
        return S, o_i

    s_last, o = lax.scan(body, s0, (qc, kc, u, w, gc, intra))
    o = o.transpose(1, 0, 3, 2, 4).reshape(bsz, n * c, H, dv)[:, :L]
    return o, s_last


def gdn_mixer(qkv_raw, z, beta_raw, a_raw, conv0, s0, conv_w, a_log, dt_bias, norm_g):
    f32 = jnp.float32
    bsz, L, _ = qkv_raw.shape
    qkv, conv_last = causal_dwconv(qkv_raw.astype(f32), conv0, conv_w.astype(f32))
    qkv = jax.nn.silu(qkv)
    q, k, v = [t.reshape(bsz, L, GDN_HEADS, GDN_HEAD_DIM) for t in jnp.split(qkv, 3, axis=-1)]
    q = l2norm(q) * (GDN_HEAD_DIM ** -0.5)
    k = l2norm(k)
    beta = jax.nn.sigmoid(beta_raw.astype(f32))
    g = -jnp.exp(a_log.astype(f32)) * jax.nn.softplus(a_raw.astype(f32) + dt_bias.astype(f32))
    o, s_last = chunk_gated_delta_rule(q, k, v, g, beta, s0.astype(f32))
    o = o * lax.rsqrt(jnp.mean(o * o, axis=-1, keepdims=True) + NORM_EPS) * norm_g.astype(f32)
    o = o.reshape(bsz, L, GDN_WIDTH) * jax.nn.silu(z.astype(f32))
    return o.astype(z.dtype), conv_last, s_last


def decoder_layer(x, st, p):
    s5_re, s5_im, rw_shift, rw_wkv, gdn_conv, gdn_state, ffn_conv = st
    hn = rmsnorm(x, p['norm1'])
    proj = hn @ p['w_in']
    u_s5, h_rw, qkv, z, b_raw, a_raw, gt_s5, gt_rw, gt_gd = split_cols(proj, IN_SIZES)
    y_s5, n_s5_re, n_s5_im = s5_mixer(u_s5, s5_re, s5_im, p['s5_lambda_re'], p['s5_lambda_im'], p['s5_b_re'],
                                      p['s5_b_im'], p['s5_c_re'], p['s5_c_im'], p['s5_d'], p['s5_log_step'],
                                      p['s5_w_glu'])
    y_rw, n_shift, n_wkv = rwkv7_mixer(h_rw, rw_shift, rw_wkv, p['rwkv_mu'], p['rwkv_w0'], p['rwkv_w2'],
                                       p['rwkv_a0'], p['rwkv_a2'], p['rwkv_g2'], p['rwkv_k_k'], p['rwkv_k_a'],
                                       p['rwkv_r_k'], p['rwkv_ln_w'], p['rwkv_ln_b'])
    y_gd, n_gconv, n_gstate = gdn_mixer(qkv, z, b_raw, a_raw, gdn_conv, gdn_state, p['gdn_conv_w'],
                                        p['gdn_a_log'], p['gdn_dt_bias'], p['gdn_norm_g'])
    merged = (jax.nn.sigmoid(gt_s5) * (y_s5 @ p['w_br_s5'])
              + jax.nn.sigmoid(gt_rw) * (y_rw @ p['w_br_rwkv'])
              + jax.nn.sigmoid(gt_gd) * (y_gd @ p['w_br_gdn']))
    x = x + merged @ p['w_out']
    hn2 = rmsnorm(x, p['norm2'])
    up = hn2 @ p['ffn_w_up']
    up_c, n_ffn = causal_dwconv(up, ffn_conv, p['ffn_conv_w'])
    gate, val = jnp.split(up_c + p['ffn_conv_b'], 2, axis=-1)
    x = x + (jax.nn.silu(gate) * val) @ p['ffn_w_down']
    return x, (n_s5_re, n_s5_im, n_shift, n_wkv, n_gconv, n_gstate, n_ffn)


def run_trunk(x, states, layer_params, final_g):
    new = []
    for l in range(DEPTH):
        x, ns = decoder_layer(x, tuple(s[l] for s in states), layer_params[l])
        new.append(ns)
    new_states = tuple(jnp.stack([ns[i] for ns in new]).astype(states[i].dtype) for i in range(len(states)))
    return rmsnorm(x, final_g), new_states


def setup_inputs(seed: int = 0) -> dict:
    key = jax.random.key(seed)
    ks = jax.random.split(key, 64)
    counter = [0]
    f32 = jnp.float32

    def nk():
        counter[0] += 1
        return ks[counter[0] - 1]

    def nrm(shape, scale):
        return scale * jax.random.normal(nk(), shape, f32)

    def unif(shape, lo, hi):
        return jax.random.uniform(nk(), shape, f32, lo, hi)

    D, G, P, GS = D_MODEL, S5_GROUPS, S5_STATE, S5_GROUP
    dt = jnp.exp(unif((DEPTH, GDN_HEADS), math.log(0.001), math.log(0.1)))
    return {
        'x_prompt': nrm((BATCH, SEQ, D), 1.0),
        'x_sample': nrm((DEC_BATCH, DEC_SEQ, D), 1.0),
        'state_s5_re': nrm((DEPTH, DEC_BATCH, G, P), 0.1),
        'state_s5_im': nrm((DEPTH, DEC_BATCH, G, P), 0.1),
        'state_rwkv_shift': nrm((DEPTH, DEC_BATCH, RWKV_COLS), 1.0),
        'state_rwkv_wkv': nrm((DEPTH, DEC_BATCH, RWKV_HEADS, RWKV_HEAD_DIM, RWKV_HEAD_DIM), 0.3),
        'state_gdn_conv': nrm((DEPTH, DEC_BATCH, GDN_CONV - 1, 3 * GDN_WIDTH), 1.0),
        'state_gdn': nrm((DEPTH, DEC_BATCH, GDN_HEADS, GDN_HEAD_DIM, GDN_HEAD_DIM), 0.1),
        'state_ffn_conv': nrm((DEPTH, DEC_BATCH, FFN_CONV - 1, 2 * D_FF), 1.0),
        'norm1_g': 1.0 + nrm((DEPTH, D), 0.02),
        'norm2_g': 1.0 + nrm((DEPTH, D), 0.02),
        'final_norm_g': 1.0 + nrm((D,), 0.02),
        'w_in': nrm((DEPTH, D, IN_COLS), D ** -0.5),
        's5_lambda_re': -0.5 + nrm((DEPTH, G, P), 0.01),
        's5_lambda_im': math.pi * jnp.arange(P, dtype=f32) + nrm((DEPTH, G, P), 0.01),
        's5_b_re': nrm((DEPTH, G, P, GS), (2 * GS) ** -0.5),
        's5_b_im': nrm((DEPTH, G, P, GS), (2 * GS) ** -0.5),
        's5_c_re': nrm((DEPTH, G, GS, P), (2 * P) ** -0.5),
        's5_c_im': nrm((DEPTH, G, GS, P), (2 * P) ** -0.5),
        's5_d': nrm((DEPTH, S5_WIDTH), 1.0),
        's5_log_step': unif((DEPTH, G), math.log(S5_DT_MIN), math.log(S5_DT_MAX)),
        's5_w_glu': nrm((DEPTH, S5_WIDTH, S5_WIDTH), S5_WIDTH ** -0.5),
        'rwkv_mu': unif((DEPTH, RWKV_COLS), 0.0, 1.0),
        'rwkv_w0': unif((DEPTH, RWKV_WIDTH), -6.0, -1.0),
        'rwkv_w2': nrm((DEPTH, RWKV_DECAY_LORA, RWKV_WIDTH), 0.1 * RWKV_DECAY_LORA ** -0.5),
        'rwkv_a0': nrm((DEPTH, RWKV_WIDTH), 0.1),
        'rwkv_a2': nrm((DEPTH, RWKV_ICLR_LORA, RWKV_WIDTH), RWKV_ICLR_LORA ** -0.5),
        'rwkv_g2': nrm((DEPTH, RWKV_GATE_LORA, RWKV_WIDTH), RWKV_GATE_LORA ** -0.5),
        'rwkv_k_k': 0.85 + nrm((DEPTH, RWKV_WIDTH), 0.02),
        'rwkv_k_a': 1.0 + nrm((DEPTH, RWKV_WIDTH), 0.02),
        'rwkv_r_k': nrm((DEPTH, RWKV_HEADS, RWKV_HEAD_DIM), 0.1),
        'rwkv_ln_w': 1.0 + nrm((DEPTH, RWKV_WIDTH), 0.02),
        'rwkv_ln_b': nrm((DEPTH, RWKV_WIDTH), 0.02),
        'gdn_conv_w': nrm((DEPTH, GDN_CONV, 3 * GDN_WIDTH), GDN_CONV ** -0.5),
        'gdn_a_log': jnp.log(unif((DEPTH, GDN_HEADS), 1.0, 16.0)),
        'gdn_dt_bias': dt + jnp.log(-jnp.expm1(-dt)),
        'gdn_norm_g': 1.0 + nrm((DEPTH, GDN_HEAD_DIM), 0.02),
        'w_br_s5': nrm((DEPTH, S5_WIDTH, D), S5_WIDTH ** -0.5),
        'w_br_rwkv': nrm((DEPTH, RWKV_WIDTH, D), RWKV_WIDTH ** -0.5),
        'w_br_gdn': nrm((DEPTH, GDN_WIDTH, D), GDN_WIDTH ** -0.5),
        'w_out': nrm((DEPTH, D, D), D ** -0.5),
        'ffn_w_up': nrm((DEPTH, D, 2 * D_FF), D ** -0.5),
        'ffn_conv_w': nrm((DEPTH, FFN_CONV, 2 * D_FF), FFN_CONV ** -0.5),
        'ffn_conv_b': nrm((DEPTH, 2 * D_FF), 0.02),
        'ffn_w_down': nrm((DEPTH, D_FF, D), D_FF ** -0.5),
    }


def reference(x_prompt, x_sample, state_s5_re, state_s5_im, state_rwkv_shift, state_rwkv_wkv, state_gdn_conv,
              state_gdn, state_ffn_conv, norm1_g, norm2_g, final_norm_g, w_in, s5_lambda_re, s5_lambda_im,
              s5_b_re, s5_b_im, s5_c_re, s5_c_im, s5_d, s5_log_step, s5_w_glu, rwkv_mu, rwkv_w0, rwkv_w2,
              rwkv_a0, rwkv_a2, rwkv_g2, rwkv_k_k, rwkv_k_a, rwkv_r_k, rwkv_ln_w, rwkv_ln_b, gdn_conv_w,
              gdn_a_log, gdn_dt_bias, gdn_norm_g, w_br_s5, w_br_rwkv, w_br_gdn, w_out, ffn_w_up, ffn_conv_w,
              ffn_conv_b, ffn_w_down):
    layer_params = [dict(norm1=norm1_g[l], norm2=norm2_g[l], w_in=w_in[l],
                         s5_lambda_re=s5_lambda_re[l], s5_lambda_im=s5_lambda_im[l], s5_b_re=s5_b_re[l],
                         s5_b_im=s5_b_im[l], s5_c_re=s5_c_re[l], s5_c_im=s5_c_im[l], s5_d=s5_d[l],
                         s5_log_step=s5_log_step[l], s5_w_glu=s5_w_glu[l],
                         rwkv_mu=rwkv_mu[l], rwkv_w0=rwkv_w0[l], rwkv_w2=rwkv_w2[l], rwkv_a0=rwkv_a0[l],
                         rwkv_a2=rwkv_a2[l], rwkv_g2=rwkv_g2[l], rwkv_k_k=rwkv_k_k[l], rwkv_k_a=rwkv_k_a[l],
                         rwkv_r_k=rwkv_r_k[l], rwkv_ln_w=rwkv_ln_w[l], rwkv_ln_b=rwkv_ln_b[l],
                         gdn_conv_w=gdn_conv_w[l], gdn_a_log=gdn_a_log[l], gdn_dt_bias=gdn_dt_bias[l],
                         gdn_norm_g=gdn_norm_g[l], w_br_s5=w_br_s5[l], w_br_rwkv=w_br_rwkv[l],
                         w_br_gdn=w_br_gdn[l], w_out=w_out[l], ffn_w_up=ffn_w_up[l],
                         ffn_conv_w=ffn_conv_w[l], ffn_conv_b=ffn_conv_b[l], ffn_w_down=ffn_w_down[l])
                    for l in range(DEPTH)]
    sample_states = (state_s5_re, state_s5_im, state_rwkv_shift, state_rwkv_wkv, state_gdn_conv, state_gdn,
                     state_ffn_conv)
    n_prompt = x_prompt.shape[0]
    prompt_states = tuple(jnp.zeros((DEPTH, n_prompt) + s.shape[2:], s.dtype) for s in sample_states)
    y_prompt, (p_s5_re, p_s5_im, p_rwkv_shift, p_rwkv_wkv, p_gdn_conv, p_gdn, p_ffn_conv) = run_trunk(
        x_prompt, prompt_states, layer_params, final_norm_g)
    y_sample, (s_s5_re, s_s5_im, s_rwkv_shift, s_rwkv_wkv, s_gdn_conv, s_gdn, s_ffn_conv) = run_trunk(
        x_sample, sample_states, layer_params, final_norm_g)
    return (y_prompt, y_sample, p_s5_re, p_s5_im, p_rwkv_shift, p_rwkv_wkv, p_gdn_conv, p_gdn, p_ffn_conv,
            s_s5_re, s_s5_im, s_rwkv_shift, s_rwkv_wkv, s_gdn_conv, s_gdn, s_ffn_conv)
```

```python
import numpy as np
from contextlib import ExitStack
import concourse.bass as bass
import concourse.mybir as mybir
from concourse.bass_utils import run_bass_kernel_spmd
from concourse.alu_op_type import AluOpType as ALU

AF = mybir.ActivationFunctionType
F32 = mybir.dt.float32
BF16 = mybir.dt.bfloat16
I32 = mybir.dt.int32
F32R = mybir.dt.float32r

D = 2048
DEPTH = 2
SEQ = 2048
NSEQ_S = 16
LS = 8
IN_COLS = 12752
DFF = 5632
C_S5, C_RW, C_QKV, C_Z, C_B, C_A, C_G = 0, 512, 2496, 5568, 6592, 6600, 6608
RW_BLOCKS = [(i * 128, 128) for i in range(12)] + [(1536, 96), (1632, 96), (1728, 128), (1856, 128)]
BIG = 30000.0
TWO_PI = 6.283185307179586
PI = 3.141592653589793


class Sched:
    ENGS = ("pe", "act", "dve", "pool", "sp")
    ND = 24
    EPOCH = 30000
    NEP = 6

    def __init__(self):
        self.prog = {e: [] for e in self.ENGS}
        self.cnt = {e: 0 for e in self.ENGS}
        self.waited = {e: {} for e in self.ENGS}
        self.lastw = {}
        self.readers = {}
        self.dma_cnt = [0] * self.ND
        self.dma_rr = 0
        self.semh = None
        self.n_instr = 0
        self.pending = {e: {} for e in self.ENGS}
        self.log = None
        self.sfx = ""
        self.sfx2 = ""
        self.private2 = set()
        self._stack = []
        self.f32r = False
        self.alias = {}
        self.private = set()
        self._rec = None

    def _deps(self, eng, reads, writes, px=()):
        need = {}

        def add(tok):
            sk, v = tok
            if eng == "pe" and sk[0] == "pe":
                return
            if v > need.get(sk, 0):
                need[sk] = v
        if self.pending[eng]:
            for sk, v in self.pending[eng].items():
                add((sk, v))
            self.pending[eng] = {}
        for r in tuple(reads) + tuple(writes):
            if r in self.lastw:
                add(self.lastw[r])
        for r in writes:
            for sk, v in self.readers.get(r, {}).items():
                add((sk, v))
        for r in px:
            for sk, v in self.readers.get(r, {}).items():
                if sk[0] != eng:
                    add((sk, v))
        out = []
        for sk, v in need.items():
            if v > self.waited[eng].get(sk, 0):
                self.waited[eng][sk] = v
                out.append((sk, v))
        return out

    def barrier(self):
        snap = {}
        for e in ("pe", "act", "dve", "pool"):
            if self.cnt[e]:
                sk, v = self._tok(e, self.cnt[e])
                snap[sk] = v
        for i in range(self.ND):
            if self.dma_cnt[i]:
                snap[("dma", i)] = self.dma_cnt[i]
        for e in self.ENGS:
            self.pending[e] = dict(snap)

    def _tok(self, eng, c):
        return ((eng, (c - 1) // self.EPOCH), (c - 1) % self.EPOCH + 1)

    def _commit(self, tok, reads, writes):
        for r in writes:
            self.lastw[r] = tok
            self.readers[r] = {}
        for r in reads:
            if r not in writes:
                d = self.readers.setdefault(r, {})
                if tok[1] > d.get(tok[0], 0):
                    d[tok[0]] = tok[1]

    def _nm(self, names):
        if not self.sfx and not self.sfx2:
            return list(names)
        al = self.alias
        out = []
        for r in names:
            if r in self.private:
                out.append(al.get(r, r) + self.sfx2 + self.sfx)
            elif r in self.private2:
                out.append(r + self.sfx2)
            else:
                out.append(r)
        return out

    def thread_begin(self):
        self._stack.append(self._rec)
        self._rec = []

    def thread_end(self):
        r = self._rec
        self._rec = self._stack.pop()
        return r

    def replay(self, threads):
        idx = [0] * len(threads)
        sfx, self.sfx = self.sfx, ""
        sfx2, self.sfx2 = self.sfx2, ""
        while any(idx[i] < len(t) for i, t in enumerate(threads)):
            for i, t in enumerate(threads):
                if idx[i] < len(t):
                    kind, a, kw = t[idx[i]]
                    idx[i] += 1
                    (self.op if kind == "op" else self.dma)(*a, **kw)
        self.sfx = sfx
        self.sfx2 = sfx2

    def op(self, eng, meth, *args, reads=(), writes=(), inc=True, **kw):
        reads, writes = self._nm(reads), self._nm(writes)
        if not inc:
            assert eng == "pe"
            if self._rec is not None:
                self._rec.append(("op", (eng, meth) + tuple(args), dict(reads=reads, writes=writes, inc=False, **kw)))
                return None
            fn0 = lambda e: getattr(e, meth)(*args, **kw)
            px0 = [r for r in reads if r.startswith("PS")]
            waits0 = self._deps(eng, reads, writes, px0)
            tok0 = self._tok(eng, self.cnt[eng] + 1)
            self.n_instr += 1

            def run0(e, waits=waits0, fn=fn0):
                for sk, wv in waits:
                    e.wait_ge(self.semh[sk], wv)
                fn(e)
            self.prog[eng].append(run0)
            self._commit(tok0, reads, writes)
            return tok0
        if meth == "matmul" and self.f32r and args[1].dtype == F32 and args[2].dtype == F32 \
                and args[0].base_partition() == 0 and args[2].shape[-1] % 2 == 0:
            args = (args[0], args[1].bitcast(F32R), args[2].bitcast(F32R)) + tuple(args[3:])
        if self._rec is not None:
            self._rec.append(("op", (eng, meth) + tuple(args), dict(reads=reads, writes=writes, **kw)))
            return None
        fn = lambda e: getattr(e, meth)(*args, **kw)
        px = [r for r in reads if r.startswith("PS")]
        waits = self._deps(eng, reads, writes, px)
        self.cnt[eng] += 1
        tok = self._tok(eng, self.cnt[eng])
        self.n_instr += 1

        def run(e, waits=waits, fn=fn, mysem=tok[0]):
            for sk, wv in waits:
                e.wait_ge(self.semh[sk], wv)
            fn(e).then_inc(self.semh[mysem], 1)
        self.prog[eng].append(run)
        self._commit(tok, reads, writes)
        if self.log is not None:
            self.log.append((eng, meth, tok, list(waits), list(reads), list(writes)))
        return tok

    def dma(self, queue, out, in_, reads=(), writes=()):
        reads, writes = self._nm(reads), self._nm(writes)
        if self._rec is not None:
            self._rec.append(("dma", (queue,), dict(out=out, in_=in_, reads=reads, writes=writes)))
            return None
        fn = lambda e: e.dma_start(out=out, in_=in_)
        waits = self._deps(queue, reads, writes)
        i = self.dma_rr
        self.dma_rr = (i + 1) % self.ND
        sk = ("dma", i)
        prev = self.dma_cnt[i]
        if prev > self.waited[queue].get(sk, 0):
            self.waited[queue][sk] = prev
            waits = waits + [(sk, prev)]
        self.dma_cnt[i] += 16
        tok = (sk, self.dma_cnt[i])
        self.n_instr += 1

        def run(e, waits=waits, fn=fn, sk=sk):
            for s2, wv in waits:
                e.wait_ge(self.semh[s2], wv)
            fn(e).then_inc(self.semh[sk], 16)
        self.prog[queue].append(run)
        self._commit(tok, reads, writes)
        if self.log is not None:
            self.log.append((queue, "dma", tok, list(waits), list(reads), list(writes)))
        return tok

    def final_wait(self, queue):
        sems = dict(self.semh)

        def run(e):
            for i in range(self.ND):
                if self.dma_cnt[i]:
                    e.wait_ge(sems[("dma", i)], self.dma_cnt[i])
            for g in ("pe", "act", "dve", "pool"):
                if self.cnt[g]:
                    sk, v = self._tok(g, self.cnt[g])
                    e.wait_ge(sems[sk], v)
        self.prog[queue].append(run)


def _col(v, width=128):
    v = np.asarray(v, np.float32)
    n = v.shape[-1] // width
    return np.ascontiguousarray(v.reshape(v.shape[:-1] + (n, width)).swapaxes(-1, -2))


def _masks(kind):
    idx = np.arange(128)
    seq = idx // 8 if kind == "s" else np.zeros(128, np.int64)
    same = (seq[:, None] == seq[None, :])
    p, f = idx[:, None], idx[None, :]
    m = {}
    m["MUi"] = (same & (p <= f)).astype(np.float32)
    m["MUs"] = (same & (p < f)).astype(np.float32)
    m["MLs"] = (same & (p > f)).astype(np.float32)
    m["NEGU"] = np.where(same & (p <= f), 0.0, -BIG).astype(np.float32)
    m["POSL"] = np.where(same & (p >= f), 0.0, BIG).astype(np.float32)
    return m


class Ctx:
    pass


def build_program(stage=99):
    nc = bass.Bass("TRN2", target_bir_lowering=False)
    S = Sched()
    import os as _os0
    if _os0.environ.get("MK_LOG"):
        S.log = []
    S.f32r = _os0.environ.get("MK_F32R", "0") == "1"
    es = ExitStack()
    K = Ctx()

    def din(name, shape):
        return nc.dram_tensor(name, list(shape), F32, kind="ExternalInput").ap()

    def dout(name, shape):
        return nc.dram_tensor(name, list(shape), F32, kind="ExternalOutput").ap()

    xp = din("xp", [SEQ, D]); xs = din("xs", [128, D])
    w_in = din("w_in", [DEPTH, D, IN_COLS])
    w_brs = [din("w_br_s5", [DEPTH, 512, D]), din("w_br_rwkv", [DEPTH, 512, D]), din("w_br_gdn", [DEPTH, 1024, D])]
    w_out = din("w_out", [DEPTH, D, D])
    w_glu = din("s5_w_glu", [DEPTH, 512, 512])
    cstk = din("cstk", [2, 6, 128, 128])
    NWC = 520
    WCT = [nc.dram_tensor("wcache%d" % i_, [130, 128, 4096], BF16, kind="Internal").ap() for i_ in range(4)]

    class _WC:
        def __getitem__(self, key):
            sl_i = key[0]
            return WCT[sl_i // 130][(sl_i % 130,) + tuple(key[1:])]
    WC = _WC()
    wc_slots = {}
    w_lora = [din("rwkv_w2", [DEPTH, 96, 512]), din("rwkv_a2", [DEPTH, 96, 512]), din("rwkv_g2", [DEPTH, 256, 512])]
    st_wkv = din("st_wkv", [DEPTH, 4, 128, NSEQ_S, 64])
    o_wkv = {"p": dout("o_wkv_p", [DEPTH, 4, 128, 1, 64]), "s": dout("o_wkv_s", [DEPTH, 4, 128, NSEQ_S, 64])}
    prow = din("prow", [DEPTH, 128, 16])
    st_gdn = din("st_gdn", [DEPTH, 8, 128, NSEQ_S, 128])
    o_gdn = {"p": dout("o_gdn_p", [DEPTH, 8, 128, 1, 128]), "s": dout("o_gdn_s", [DEPTH, 8, 128, NSEQ_S, 128])}
    s5B = din("s5B", [DEPTH, 2, 512, 2048])
    s5C = din("s5C", [DEPTH, 2, 2048, 512])
    st_s5 = din("st_s5", [DEPTH, 128, 2, 16, NSEQ_S])
    o_s5 = {"p": dout("o_s5_p", [DEPTH, 128, 2, 16, 1]), "s": dout("o_s5_s", [DEPTH, 128, 2, 16, NSEQ_S])}
    w_up = din("ffn_w_up", [DEPTH, D, 2 * DFF])
    w_down = din("ffn_w_down", [DEPTH, DFF, D])
    st_fconv = din("st_fconv", [DEPTH, 128, 88, NSEQ_S, 2])
    o_fconv = {"p": dout("o_fconv_p", [DEPTH, 128, 88, 1, 2]), "s": dout("o_fconv_s", [DEPTH, 128, 88, NSEQ_S, 2])}
    pcol = din("pcol", [DEPTH, 128, NPC])
    cst = din("cst", [NCST, 128, 128])
    st_shift = din("st_shift", [DEPTH, 128, 16, NSEQ_S, 1])
    st_gconv = din("st_gconv", [DEPTH, 128, 24, NSEQ_S, 3])
    o_y = {"p": dout("y_p", [SEQ, D]), "s": dout("y_s", [128, D])}
    o_shift = {"p": dout("o_shift_p", [DEPTH, 128, 16, 1, 1]), "s": dout("o_shift_s", [DEPTH, 128, 16, NSEQ_S, 1])}
    o_gconv = {"p": dout("o_gconv_p", [DEPTH, 128, 24, 1, 3]), "s": dout("o_gconv_s", [DEPTH, 128, 24, NSEQ_S, 3])}

    with es:
        def sb(name, shape, dt=F32):
            return es.enter_context(nc.sbuf_tensor(name, list(shape), dt))

        def ps(name):
            return es.enter_context(nc.psum_tensor(name, [128, 512], F32))

        X = sb("X", [128, 16, 512])
        HN = sb("HN", [128, 16, 512], BF16)
        WS = [sb("WS%d" % i, [128, 4096]) for i in range(2)]
        WB = [sb("WB%d" % i, [128, 4096], BF16) for i in range(2)]
        PC = sb("PC", [128, DEPTH, NPC])
        CS = sb("CS", [128, NCST, 128])
        NA = 14144
        AR = sb("AR", [128, NA])

        class View:
            def __init__(self, off, size, name):
                self.off, self.size, self.name = off, size, name

            def __getitem__(self, key):
                return AR[:, self.off:self.off + self.size][key]
        EXT = [View(i * 520, 520, "EXT%d" % i) for i in range(8)]
        STT = View(4160, 2816, "STT")
        GT = [View(6976 + i * 512, 512, "GT%d" % i) for i in range(6)]
        ACC = [View(10048 + i * 512, 512, "ACC%d" % i) for i in range(2)]
        TMP = [View(11072 + i * 512, 512, "TMP%d" % i) for i in range(2)]
        XT = View(12096, 2048, "XT")
        SQ = [View(10048 + i * 512, 512, "SQ%d" % i) for i in range(2)]
        RSTD = View(11072, 512, "RSTD")
        YBR = sb("YBR", [128, 16, 512], BF16)
        MRG = sb("MRG", [128, 16, 512], BF16)
        AJ = MRG[:, 12:16, :]
        TI = sb("TI", [128, 512], I32)
        CK = sb("CK", [128, 6, 128])
        WBA = sb("WBA", [128, 16, 16], BF16)
        PR = sb("PR", [128, DEPTH, 16])
        S5ST = sb("S5ST", [128, 2 * 16 * NSEQ_S])
        S5P = sb("S5P", [128, DEPTH, 6, 16])
        SB5 = [sb("SB5_%d" % i, [128, 2, 128]) for i in range(2)]
        PSD = [ps("PSD0"), ps("PSD1")]
        PSM = [ps("PSM%d" % i) for i in range(6)]
        semh = {}
        for e in ("pe", "act", "dve", "pool"):
            for ep in range(S.NEP if e != "pool" else 1):
                semh[(e, ep)] = es.enter_context(nc.semaphore("s_%s%d" % (e, ep)))
        for i in range(S.ND):
            semh[("dma", i)] = es.enter_context(nc.semaphore("s_dma%d" % i))
        S.semh = semh
        ident = CS[:, CI["ident"], :]
        ones = CS[:, CI["ones"], :]

        S.dma("sp", out=PC[:], in_=pcol.rearrange("l p c -> p l c"), writes=["PC"])
        S.dma("sp", out=CS[:], in_=cst.rearrange("k p f -> p k f"), writes=["CS"])
        S.dma("sp", out=PR[:], in_=prow.rearrange("l p c -> p l c"), writes=["PR"])


        def wrap_sin(dst, src, n_, tA, tB, eng="dve", srcn=(), dstn=()):
            S.op(eng, "tensor_scalar", tA, src, 1.0 / TWO_PI, None, ALU.mult, reads=list(srcn) + ["TA", "TB"], writes=["wrapA", "TA"])
            S.op(eng, "tensor_copy", TI[:, 0:n_], tA, reads=["wrapA"], writes=["TI"])
            S.op(eng, "tensor_copy", tA, TI[:, 0:n_], reads=["TI"], writes=["wrapA"])
            S.op(eng, "scalar_tensor_tensor", tB, tA, -TWO_PI, src, ALU.mult, ALU.add, reads=["wrapA"] + list(srcn), writes=["wrapB", "TB"])
            S.op(eng, "tensor_scalar", tB, tB, -PI, PI, ALU.max, ALU.min, reads=["wrapB"], writes=["wrapB"])
            S.op("act", "activation", dst, tB, AF.Sin, reads=["wrapB"], writes=["wrapD"] + list(dstn))

        S.barrier()
        for l in range(DEPTH):
            lre = PC[:, l, PO["lre"]:PO["lre"] + 16]
            lim = PC[:, l, PO["lim"]:PO["lim"] + 16]
            lst = PC[:, l, PO["lst"]:PO["lst"] + 16]
            t = [AR[:, i * 16:(i + 1) * 16] for i in range(12)]
            rho, th, cr, ci, ncr = (S5P[:, l, i, :] for i in range(5))
            S.op("act", "activation", t[0], lst, AF.Exp, reads=["PC"], writes=["p5"])
            S.op("dve", "tensor_tensor", t[1], lre, t[0], ALU.mult, reads=["p5", "PC"], writes=["p5"])
            S.op("act", "activation", rho, t[1], AF.Exp, reads=["p5"], writes=["S5P"])
            S.op("dve", "tensor_tensor", th, lim, t[0], ALU.mult, reads=["p5", "PC"], writes=["S5P"])
            S.barrier()
            wrap_sin(t[2], th, 16, t[8], t[9])
            S.op("dve", "tensor_scalar", t[3], th, PI / 2, None, ALU.add, reads=["S5P"], writes=["p5"])
            S.barrier()
            wrap_sin(t[4], t[3], 16, t[8], t[9])
            S.barrier()
            S.op("dve", "tensor_tensor", t[5], rho, t[4], ALU.mult, writes=["p5"])
            S.op("dve", "tensor_scalar", t[5], t[5], -1.0, None, ALU.add, writes=["p5"])
            S.op("dve", "tensor_tensor", t[6], rho, t[2], ALU.mult, writes=["p5"])
            S.op("dve", "tensor_tensor", t[7], lre, lre, ALU.mult, writes=["p5"])
            S.op("dve", "tensor_tensor", t[10], lim, lim, ALU.mult, writes=["p5"])
            S.op("dve", "tensor_tensor", t[7], t[7], t[10], ALU.add, writes=["p5"])
            S.op("dve", "reciprocal", t[7], t[7], writes=["p5"])
            S.op("dve", "tensor_tensor", t[10], t[5], lre, ALU.mult, writes=["p5"])
            S.op("dve", "tensor_tensor", t[11], t[6], lim, ALU.mult, writes=["p5"])
            S.op("dve", "tensor_tensor", t[10], t[10], t[11], ALU.add, writes=["p5"])
            S.op("dve", "tensor_tensor", cr, t[10], t[7], ALU.mult, writes=["p5", "S5P"])
            S.op("dve", "tensor_tensor", t[10], t[6], lre, ALU.mult, writes=["p5"])
            S.op("dve", "tensor_tensor", t[11], t[5], lim, ALU.mult, writes=["p5"])
            S.op("dve", "tensor_tensor", t[10], t[10], t[11], ALU.subtract, writes=["p5"])
            S.op("dve", "tensor_tensor", ci, t[10], t[7], ALU.mult, writes=["p5", "S5P"])
            S.op("dve", "tensor_scalar", ncr, cr, -1.0, None, ALU.mult, writes=["p5", "S5P"])
            S.barrier()

        dense_rr = [0]
        w_rr = [0]
        ws_busy = [False]
        deep_pf = [False]
        wx_rr = [0]
        WBX = [(WB[0], "WB0"), (WB[1], "WB1")]
        for i_ in range(2):
            v_ = WS[i_][:, :].bitcast(BF16)
            WBX += [(v_[:, 0:4096], "WS%da" % i_), (v_[:, 4096:8192], "WS%db" % i_)]

        def linear(wd, row0, nrows, groups, act_t, n, consume, kc0=0, aname=None, wk=None):
            KC = max(1, nrows // 128)
            kp = min(nrows, 128)
            for (col0, ncols, subs) in groups:
                i = w_rr[0] % 2
                w_rr[0] += 1
                ws, wb = WS[i], WB[i]
                m_ = KC * ncols
                ckey = (wk, row0, col0, ncols) if wk is not None else None
                wbn = "WB%d" % i
                if ckey is not None and ckey in wc_slots:
                    sl_i = wc_slots[ckey]
                    if deep_pf[0] and not ws_busy[0]:
                        wb, wbn = WBX[wx_rr[0] % 6]
                        wx_rr[0] += 1
                    S.dma("sp", out=wb[0:kp, 0:m_], in_=WC[sl_i, 0:kp, 0:m_], reads=["WC%d" % sl_i], writes=[wbn])
                else:
                    assert not ws_busy[0], "uncached weight block while staging buffers host mixer slots"
                    src = wd[row0:row0 + nrows, col0:col0 + ncols].rearrange("(kc p) n -> p kc n", p=kp)
                    dst = ws[0:kp, 0:m_].rearrange("p (kc n) -> p kc n", kc=KC)
                    S.dma("sp", out=dst, in_=src, writes=["WS%d" % i])
                    assert not deep_pf[0]
                    S.op("act", "activation", wb[0:kp, 0:m_], ws[0:kp, 0:m_], AF.Copy,
                         reads=["WS%d" % i], writes=[wbn])
                    if ckey is not None and len(wc_slots) < NWC:
                        sl_i = len(wc_slots)
                        wc_slots[ckey] = sl_i
                        S.dma("act", out=WC[sl_i, 0:kp, 0:m_], in_=wb[0:kp, 0:m_], reads=[wbn], writes=["WC%d" % sl_i])
                wv = wb[:, 0:KC * ncols].rearrange("p (kc n) -> p kc n", kc=KC)
                for (off, w) in subs:
                    j = dense_rr[0] % 2
                    dense_rr[0] += 1
                    pt = PSD[j]
                    for kc in range(KC):
                        S.op("pe", "matmul",
                            pt[0:w, 0:n], wv[0:kp, kc, off:off + w], act_t[0:kp, kc0 + kc, 0:n], start=(kc == 0), stop=(kc == KC - 1),
                            inc=(kc == KC - 1),
                            reads=[wbn, aname or act_t.name], writes=["PSD%d" % j])
                    consume(col0 + off, w, pt[0:w, 0:n], "PSD%d" % j)

        def rmsnorm(gcol, n, dst, dstname):
            pn = PSM[0]
            S.barrier()
            for c in range(16):
                q = SQ[c % 2]
                S.op("act", "activation", q[:, 0:n], X[:, c, 0:n], AF.Square,
                     reads=["X"], writes=["SQ%d" % (c % 2)])
                S.op("pe", "matmul", pn[:, 0:n], ones, q[:, 0:n], start=(c == 0), stop=(c == 15),
                     reads=["SQ%d" % (c % 2), "CS"], writes=["PSM0"])
            S.op("act", "activation", SQ[0][:, 0:n], pn[:, 0:n], AF.Sqrt, bias=1e-6, scale=1.0 / D,
                 reads=["PSM0"], writes=["SQ0"])
            S.op("dve", "reciprocal", RSTD[:, 0:n], SQ[0][:, 0:n], reads=["SQ0"], writes=["RSTD"])
            for c in range(16):
                S.op("dve", "scalar_tensor_tensor", dst[:, c, 0:n], X[:, c, 0:n], gcol[:, c:c + 1],
                                                                   RSTD[:, 0:n], ALU.mult, ALU.mult,
                     reads=["X", "RSTD", "PC"], writes=[dstname])
            S.barrier()

        def final_out(kind, ti, n):
            pn = PSM[0]
            S.barrier()
            for c in range(16):
                q = SQ[c % 2]
                S.op("act", "activation", q[:, 0:n], X[:, c, 0:n], AF.Square, reads=["X"], writes=["SQ%d" % (c % 2)])
                S.op("pe", "matmul", pn[:, 0:n], ones, q[:, 0:n], start=(c == 0), stop=(c == 15),
                     reads=["SQ%d" % (c % 2), "CS"], writes=["PSM0"])
            S.op("act", "activation", SQ[0][:, 0:n], pn[:, 0:n], AF.Sqrt, bias=1e-6, scale=1.0 / D, reads=["PSM0"], writes=["SQ0"])
            S.op("dve", "reciprocal", RSTD[:, 0:n], SQ[0][:, 0:n], reads=["SQ0"], writes=["RSTD"])
            gf = PC[:, 0, PO["gf"]:PO["gf"] + 16]
            for c in range(16):
                S.op("dve", "scalar_tensor_tensor", X[:, c, 0:n], X[:, c, 0:n], gf[:, c:c + 1], RSTD[:, 0:n], ALU.mult, ALU.mult,
                     reads=["X", "RSTD", "PC"], writes=["X"])
            S.barrier()
            ydst = o_y[kind]
            for s_ in range(n // 128):
                for cg in range(4):
                    pt = PSM[1 + cg % 2]
                    for c4 in range(4):
                        c = cg * 4 + c4
                        S.op("pe", "transpose", pt[:, c4 * 128:(c4 + 1) * 128], X[:, c, s_ * 128:(s_ + 1) * 128], ident,
                             reads=["X", "CS"], writes=["PSM%d" % (1 + cg % 2)])
                    S.op("act", "activation", XT[:, cg * 512:(cg + 1) * 512], pt[:, :], AF.Copy,
                         reads=["PSM%d" % (1 + cg % 2)], writes=["XT"])
                r0 = ti * 512 + s_ * 128
                S.dma("act", out=ydst[r0:r0 + 128, :], in_=XT[:], reads=["XT"], writes=["o_y"])

        tiles = [("p", t) for t in range(4)] + [("s", 0)]
        import os
        if os.environ.get("MK_TILES"):
            tiles = [tiles[int(i)] for i in os.environ["MK_TILES"].split(",")]
        for (kind, ti) in tiles:
            n = 512 if kind == "p" else 128
            nb = 1 if kind == "p" else NSEQ_S
            L = n // nb
            xsrc = xp[ti * 512:(ti + 1) * 512, :] if kind == "p" else xs
            if kind == "s" or ti > 0:
                S.barrier()
                deep_pf[0] = True
            S.barrier()
            S.dma("sp", out=CK[:], in_=cstk[0 if kind == "p" else 1].rearrange("k p f -> p k f"), writes=["CK"])
            for s_ in range(n // 128):
                S.dma("sp", out=XT[:], in_=xsrc[s_ * 128:(s_ + 1) * 128, :], writes=["XT"])
                for cg in range(4):
                    pt = PSM[1 + cg % 2]
                    for c4 in range(4):
                        c = cg * 4 + c4
                        S.op("pe", "transpose", pt[:, c4 * 128:(c4 + 1) * 128],
                                                                           XT[:, c * 128:(c + 1) * 128], ident,
                             reads=["XT", "CS"], writes=["PSM%d" % (1 + cg % 2)])
                    S.op("dve", "tensor_copy",
                        X[:, cg * 4:cg * 4 + 4, s_ * 128:(s_ + 1) * 128], pt[:, :].rearrange("p (c t) -> p c t", c=4),
                        reads=["PSM%d" % (1 + cg % 2)], writes=["X"])
            for l in range(DEPTH):
                rmsnorm(PC[:, l, PO["g1"]:PO["g1"] + 16], n, HN, "HN")
                first = (kind == "p" and ti == 0)

                def halo_proj(wd, col_base, blocks, H, st_in, st_out, oname, after, nbuf=2, sblk=None, wk=None):
                    nblk = len(blocks) if sblk is None else sblk[1]
                    sidx = list(range(len(blocks))) if sblk is None else (
                        sblk[0] if isinstance(sblk[0], (list, tuple)) else [sblk[0] + i_ for i_ in range(len(blocks))])
                    sv = STT[:, 0:nblk * nb * H].rearrange("p (c b h) -> p c b h", c=nblk, b=nb)
                    if st_in is None:
                        pass
                    elif first:
                        S.op("dve", "memset", STT[:, 0:nblk * nb * H], 0.0, writes=["STT"])
                    else:
                        S.dma("act", out=sv, in_=st_in, reads=[oname], writes=["STT"])
                    blk_idx = {col_base + o: i for i, (o, w) in enumerate(blocks)}
                    nbl = len(blocks)

                    def consume(col, w, pap, pres):
                        bi = blk_idx[col]
                        ex = EXT[bi % nbuf]
                        ev = ex[:, 0:nb * (H + L)].rearrange("p (b t) -> p b t", b=nb)
                        S.op("dve", "tensor_copy", ev[0:w, :, 0:H], sv[0:w, sidx[bi], :, :],
                             reads=["STT"], writes=["EXT%d" % (bi % nbuf)])
                        S.op("act", "activation", ev[0:w, :, H:H + L], pap.rearrange("p (b t) -> p b t", b=nb), AF.Copy,
                             reads=[pres], writes=["EXT%d" % (bi % nbuf)])
                        S.op("dve", "tensor_copy", sv[0:w, sidx[bi], :, :], ev[0:w, :, L:L + H],
                             reads=["EXT%d" % (bi % nbuf)], writes=["STT"])
                        after(bi, w, ev, "EXT%d" % (bi % nbuf))
                    groups = []
                    i = 0
                    while i < nbl:
                        o0, w0 = blocks[i]
                        if i + 1 < nbl and blocks[i + 1][0] == o0 + w0:
                            o1, w1 = blocks[i + 1]
                            groups.append((col_base + o0, w0 + w1, [(0, w0), (w0, w1)]))
                            i += 2
                        else:
                            groups.append((col_base + o0, w0, [(0, w0)]))
                            i += 1
                    linear(wd, 0, D, groups, HN, n, consume, wk=wk)
                    if st_out is not None:
                        S.dma("act", out=st_out, in_=sv, reads=["STT"], writes=[oname])


                S.barrier()
                U = AR[:, 0:2048].rearrange("p (c t) -> p c t", c=4)
                T5 = [AR[:, 2048 + i * 512:2048 + (i + 1) * 512] for i in range(12)]
                Y1F = AR[:, 8192:10240].rearrange("p (c t) -> p c t", c=4)
                iota = CS[:, CI["iota_p"]:CI["iota_p"] + 4, :] if kind == "p" else CS[:, CI["iota_s"]:CI["iota_s"] + 1, :]
                rset = CS[:, CI["reset_p"]:CI["reset_p"] + 4, :] if kind == "p" else CS[:, CI["reset_s"]:CI["reset_s"] + 1, :]
                iota = iota.rearrange("p c t -> p (c t)")
                rset = rset.rearrange("p c t -> p (c t)")

                def cons_u(col, w, pap, pres):
                    S.op("act", "activation", U[:, col // 128, 0:n], pap, AF.Copy, reads=[pres], writes=["U"])
                linear(w_in[l], 0, D, [(0, 256, [(0, 128), (128, 128)]), (256, 256, [(0, 128), (128, 128)])], HN, n, cons_u, wk="in%d" % l)
                s5v = S5ST[:, 0:2 * 16 * nb].rearrange("p (r c b) -> p r c b", r=2, c=16)
                if first:
                    S.op("dve", "memset", S5ST[:, 0:2 * 16 * nb], 0.0, writes=["S5ST"])
                else:
                    S.dma("act", out=s5v, in_=(o_s5["p"][l] if kind == "p" else st_s5[l]), reads=["o_s5"], writes=["S5ST"])
                rho, th, cr, ci, ncr = (S5P[:, l, i, :] for i in range(5))
                psy = [PSM[4], PSM[5]]
                v3 = lambda ap: ap.rearrange("p (b t) -> p b t", b=nb)
                for sc in range(16):
                    kc, oc = sc // 4, sc // 4
                    sbt = SB5[sc % 2]
                    S.dma("sp", out=sbt[:], in_=s5B[l, :, kc * 128:(kc + 1) * 128, sc * 128:(sc + 1) * 128].rearrange("r p m -> p r m"),
                          writes=["SB5_%d" % (sc % 2)])
                    pb = [PSM[2], PSM[3]]
                    for r in range(2):
                        S.op("pe", "matmul", pb[r][:, 0:n], sbt[:, r, :], U[:, kc, 0:n], start=True, stop=True,
                             reads=["SB5_%d" % (sc % 2), "U"], writes=["PSM%d" % (2 + r)])
                    sl_ = slice(sc, sc + 1)
                    ANG, A2, SIN, COS, IR, II, RR, RI, ZR, ZI, TA, TB = (x[:, 0:n] for x in T5)
                    S.op("dve", "tensor_scalar", ANG, iota[:, 0:n], th[:, sl_], None, ALU.mult, reads=["CS", "S5P"], writes=["ANG"])
                    S.op("dve", "tensor_scalar", A2, ANG, PI / 2, None, ALU.add, reads=["ANG"], writes=["A2"])
                    wrap_sin(SIN, ANG, n, TA, TB, srcn=["ANG"], dstn=["SIN"])
                    S.barrier()
                    wrap_sin(COS, A2, n, TA, TB, srcn=["A2"], dstn=["COS"])
                    S.barrier()
                    S.op("dve", "tensor_scalar", IR, COS, cr[:, sl_], None, ALU.mult, writes=["IR"])
                    S.op("dve", "scalar_tensor_tensor", IR, SIN, ci[:, sl_], IR, ALU.mult, ALU.add, writes=["IR"])
                    S.op("dve", "tensor_scalar", II, COS, ci[:, sl_], None, ALU.mult, writes=["II"])
                    S.op("dve", "scalar_tensor_tensor", II, SIN, ncr[:, sl_], II, ALU.mult, ALU.add, reads=["II"], writes=["II"])
                    S.op("dve", "tensor_tensor", RR, IR, pb[0][:, 0:n], ALU.mult, reads=["IR", "PSM2"], writes=["RR"])
                    S.op("dve", "tensor_tensor", TA, II, pb[1][:, 0:n], ALU.mult, reads=["II", "PSM3"], writes=["TA"])
                    S.op("dve", "tensor_tensor", RR, RR, TA, ALU.subtract, reads=["TA", "RR"], writes=["RR"])
                    S.op("dve", "tensor_tensor", RI, IR, pb[1][:, 0:n], ALU.mult, reads=["IR", "PSM3"], writes=["RI"])
                    S.op("dve", "tensor_tensor", TB, II, pb[0][:, 0:n], ALU.mult, reads=["II", "PSM2"], writes=["TB"])
                    S.op("dve", "tensor_tensor", RI, RI, TB, ALU.add, reads=["TB", "RI"], writes=["RI"])
                    S.op("dve", "tensor_scalar", TA, rset[:, 0:n], rho[:, sl_], None, ALU.mult, reads=["TA", "CS", "S5P"], writes=["TA"])
                    for r, RX in ((0, RR), (1, RI)):
                        S.op("dve", "tensor_scalar", TB[:, 0:nb], s5v[:, r, sc, :], rho[:, sl_], None, ALU.mult,
                             reads=["S5ST", "TB"], writes=["TB"])
                        S.op("dve", "tensor_tensor", v3(RX)[:, :, 0], v3(RX)[:, :, 0], TB[:, 0:nb], ALU.add,
                             reads=["TB", "RR", "RI"], writes=["RR", "RI"])
                    S.op("dve", "tensor_tensor_scan", ZR, TA, RR, 0.0, ALU.mult, ALU.add, reads=["TA", "RR"], writes=["ZR"])
                    S.op("dve", "tensor_tensor_scan", ZI, TA, RI, 0.0, ALU.mult, ALU.add, reads=["TA", "RI"], writes=["ZI"])
                    S.barrier()
                    S.op("dve", "tensor_tensor", RR, COS, ZR, ALU.mult, writes=["RR"])
                    S.op("dve", "tensor_tensor", TA, SIN, ZI, ALU.mult, writes=["TA"])
                    S.op("dve", "tensor_tensor", RR, RR, TA, ALU.subtract, reads=["TA"], writes=["RR"])
                    S.op("dve", "tensor_tensor", RI, COS, ZI, ALU.mult, writes=["RI"])
                    S.op("dve", "tensor_tensor", TB, SIN, ZR, ALU.mult, writes=["TB"])
                    S.op("dve", "tensor_tensor", RI, RI, TB, ALU.add, reads=["TB"], writes=["RI"])
                    S.op("dve", "tensor_copy", s5v[:, 0, sc, :], v3(RR)[:, :, L - 1], reads=["RR"], writes=["S5ST"])
                    S.op("dve", "tensor_copy", s5v[:, 1, sc, :], v3(RI)[:, :, L - 1], reads=["RI"], writes=["S5ST"])
                    sct = SB5[sc % 2]
                    S.dma("sp", out=sct[:], in_=s5C[l, :, sc * 128:(sc + 1) * 128, oc * 128:(oc + 1) * 128].rearrange("r p m -> p r m"),
                          reads=[], writes=["SB5_%d" % (sc % 2)])
                    for r, RX, nm in ((0, RR, "RR"), (1, RI, "RI")):
                        S.op("pe", "matmul", psy[r][:, 0:n], sct[:, r, :], RX, start=(sc % 4 == 0), stop=(sc % 4 == 3),
                             reads=["SB5_%d" % (sc % 2), nm], writes=["PSM%d" % (4 + r)])
                    if sc % 4 == 3:
                        dcol = PC[:, l, PO["s5d"] + oc:PO["s5d"] + oc + 1]
                        Y = ZR
                        S.op("act", "activation", TA, psy[1][:, 0:n], AF.Copy, reads=["PSM5"], writes=["TA"])
                        S.op("dve", "tensor_tensor", Y, psy[0][:, 0:n], TA, ALU.subtract, reads=["PSM4", "TA"], writes=["ZR"])
                        S.op("dve", "scalar_tensor_tensor", Y, U[:, oc, 0:n], dcol, Y, ALU.mult, ALU.add, reads=["U", "PC"], writes=["ZR"])
                        S.op("dve", "tensor_tensor", TB, Y, Y, ALU.mult, reads=["ZR"], writes=["TB"])
                        S.op("dve", "tensor_scalar", TB, TB, 0.044715, 1.0, ALU.mult, ALU.add, writes=["TB"])
                        S.op("dve", "tensor_tensor", TB, TB, Y, ALU.mult, writes=["TB"])
                        S.op("act", "activation", TB, TB, AF.Sigmoid, scale=1.5957691216057308, reads=["TB"], writes=["TB"])
                        S.op("dve", "tensor_tensor", Y1F[:, oc, 0:n], Y, TB, ALU.mult, reads=["TB", "ZR"], writes=["Y1F"])
                        S.op("dve", "tensor_copy", AJ[:, oc, 0:n], Y1F[:, oc, 0:n], reads=["Y1F"], writes=["AJ"])
                        S.barrier()

                def cons_glu(col, w, pap, pres):
                    oc = col // 128
                    S.op("act", "activation", T5[0][:, 0:n], pap, AF.Sigmoid, reads=[pres], writes=["ANG"])
                    S.op("dve", "tensor_tensor", YBR[:, oc, 0:n], Y1F[:, oc, 0:n], T5[0][:, 0:n], ALU.mult,
                         reads=["ANG", "Y1F"], writes=["YBR"])
                linear(w_glu[l], 0, 512, [(0, 256, [(0, 128), (128, 128)]), (256, 256, [(0, 128), (128, 128)])], AJ, n, cons_glu, aname="AJ", wk="glu%d" % l)
                S.dma("act", out=o_s5[kind][l], in_=s5v, reads=["S5ST"], writes=["o_s5"])
                S.barrier()


                S.barrier()
                MUi, MUs, MLs, NEGU, POSL, SAME = (CK[:, i_, :] for i_ in range(6))
                rowmask = CS[:, CI["rowmask"], 0:NSEQ_S] if kind == "s" else CS[:, CI["ones"], 0:NSEQ_S]
                headblk = CS[:, CI["headblk"], :]
                nchunk = n // 128
                levels = 6 if kind == "p" else 2
                bump = [6976]

                def alloc(k):
                    o_ = bump[0]
                    bump[0] += k
                    assert bump[0] <= NA, bump[0]
                    return AR[:, o_:o_ + k]
                RKV = alloc(3 * n).rearrange("p (c t) -> p c t", c=3)
                AAb, BBb, LWb, Gb, BON = (alloc(n) for _ in range(5))
                PW = alloc(nb * 64).rearrange("p (b v) -> p b v", b=nb)
                TT = [alloc(128) for _ in range(20)]
                UU = alloc(128)
                KQM_flat = AR[:, 1040:1040 + 2176]
                KQM = KQM_flat[:, 0:2048].rearrange("p (b c) -> p b c", b=16)
                KQD = KQM_flat.rearrange("p (b x) -> p b x", x=136)[:, :, 0:8]
                if nb > 1:
                    KM1 = alloc(128)
                    KM2 = alloc(128)
                S.private = {"LTm", "Lm", "AKT", "RBT", "RKT", "y0", "y1", "P1T"}
                XT1 = [AR[:, 1040 + i_ * 128:1040 + (i_ + 1) * 128] for i_ in range(11)] if nb == 1 else None
                LORA = MRG[:, 8:12, :]
                rst = STT[:, 0:16 * nb * 1].rearrange("p (c b h) -> p c b h", c=16, b=nb)
                if first:
                    S.op("dve", "memset", STT[:, 0:16 * nb], 0.0, writes=["STT"])
                else:
                    S.dma("act", out=rst, in_=(o_shift["p"][l] if kind == "p" else st_shift[l]), reads=["o_shift"], writes=["STT"])
                mu = PC[:, l, PO["mu"]:PO["mu"] + 16]

                def shifted(bi_g, w, ev, eres, dst, dname):
                    hsv = dst.rearrange("p (b t) -> p b t", b=nb)
                    S.op("dve", "tensor_tensor", hsv[0:w], ev[0:w, :, 0:L], ev[0:w, :, 1:1 + L], ALU.subtract, reads=[eres], writes=[dname])
                    S.op("dve", "scalar_tensor_tensor", hsv[0:w], hsv[0:w], mu[0:w, bi_g:bi_g + 1], ev[0:w, :, 1:1 + L], ALU.mult, ALU.add,
                         reads=[eres, "PC", dname], writes=[dname])

                HS = TT[0:4]

                def after_lora(bi, w, ev, eres):
                    hs_t = AR[:, 6976 + 3 * n:6976 + 4 * n]
                    shifted(12 + bi, w, ev, eres, hs_t[:, 0:n], "AAb")
                    fn = (AF.Tanh, AF.Copy, AF.Sigmoid, AF.Sigmoid)[bi]
                    S.op("act", "activation", LORA[0:w, bi, 0:n], hs_t[0:w, 0:n], fn, reads=["AAb"], writes=["LORA"])
                halo_proj(w_in[l], C_RW, RW_BLOCKS[12:16], 1, None, None, "o_shift", after_lora, nbuf=2, sblk=([12, 13, 14, 15], 16), wk="in%d" % l)

                ps_rr = [0]

                def PS_():
                    i_ = ps_rr[0] % 6
                    ps_rr[0] += 1
                    return PSM[i_], "PSM%d" % i_

                S.private2 = {"RKV", "AAb", "BBb", "LWb", "Gb", "BON", "PW", "GL", "GAM", "GINV", "GPREV", "RT", "KT", "BT_", "AT",
                              "Vt", "KTt", "BTt", "YN", "UU", "P0x", "P0Tx", "P1x", "KM1", "KM2", "KQM"}
                NP = 2 if (kind == "p" and ti > 0) else 1
                BK = [(PSM[i_], "PSM%d" % i_) for i_ in range(6)] + [(PSD[0], "PSD0"), (PSD[1], "PSD1")]
                pslots = [dict(RKV=RKV, AR5=(AAb, BBb, LWb, Gb, BON), PW=PW, TT=TT, UU=UU, XT1=XT1, banks=BK[0:6] if NP == 1 else BK[0:4])]
                if NP == 2:
                    S.barrier()
                    ws_busy[0] = True
                    w0_, w1_ = WS[0], WS[1]
                    pslots.append(dict(RKV=w0_[:, 0:3 * n].rearrange("p (c t) -> p c t", c=3),
                                       AR5=tuple(w0_[:, 1536 + i_ * 512:1536 + (i_ + 1) * 512] for i_ in range(5)),
                                       PW=w1_[:, 0:64].rearrange("p (b v) -> p b v", b=1),
                                       TT=[w1_[:, 64 + i_ * 128:64 + (i_ + 1) * 128] for i_ in range(20)],
                                       UU=w1_[:, 2624:2752],
                                       XT1=[AR[:, 2448 + i_ * 128:2448 + (i_ + 1) * 128] for i_ in range(11)],
                                       banks=BK[4:8]))

                def pair_p(pr, sl):
                    RKV, (AAb, BBb, LWb, Gb, BON), PW = sl["RKV"], sl["AR5"], sl["PW"]

                    def after_rkv(bi, w, ev, eres):
                        shifted((pr, 4 + pr, 8 + pr)[bi], w, ev, eres, RKV[:, bi, 0:n], "RKV")
                    halo_proj(w_in[l], C_RW, [(pr * 128, 128), (512 + pr * 128, 128), (1024 + pr * 128, 128)], 1, None, None,
                              "o_shift", after_rkv, nbuf=2, sblk=([pr, 4 + pr, 8 + pr], 16), wk="in%d" % l)
                    rT_, kT_, vT_ = RKV[:, 0, 0:n], RKV[:, 1, 0:n], RKV[:, 2, 0:n]
                    pcol_ = lambda nm: PC[:, l, PO[nm] + pr:PO[nm] + pr + 1]
                    T1, T2 = BON[:, 0:n], Gb[:, 0:n]

                    def cons_w(col, w, pap, pres):
                        S.op("act", "activation", LWb[:, 0:n], pap, AF.Identity, bias=pcol_("nw0"), scale=1.0, reads=[pres, "PC"], writes=["LWb"])
                        S.op("act", "activation", LWb[:, 0:n], LWb[:, 0:n], AF.Exp, scale=-1.0, reads=["LWb"], writes=["LWb"])
                        S.op("act", "activation", LWb[:, 0:n], LWb[:, 0:n], AF.Ln, bias=1.0, scale=1.0, reads=["LWb"], writes=["LWb"])
                        S.op("act", "activation", LWb[:, 0:n], LWb[:, 0:n], AF.Exp, bias=-0.5, scale=-1.0, reads=["LWb"], writes=["LWb"])
                        S.op("dve", "tensor_scalar", LWb[:, 0:n], LWb[:, 0:n], -1.0, None, ALU.mult, reads=["LWb"], writes=["LWb"])
                    linear(w_lora[0][l], 0, 96, [(pr * 128, 128, [(0, 128)])], LORA, n, cons_w, kc0=0, aname="LORA", wk="lo0%d" % l)

                    def cons_a(col, w, pap, pres):
                        S.op("act", "activation", AAb[:, 0:n], pap, AF.Sigmoid, bias=pcol_("a0"), scale=1.0, reads=[pres, "PC"], writes=["AAb"])
                    linear(w_lora[1][l], 0, 96, [(pr * 128, 128, [(0, 128)])], LORA, n, cons_a, kc0=1, aname="LORA", wk="lo1%d" % l)
                    S.op("dve", "tensor_scalar", BBb[:, 0:n], kT_, pcol_("kk"), None, ALU.mult, reads=["RKV", "PC"], writes=["BBb"])
                    S.op("act", "activation", T1, BBb[:, 0:n], AF.Square, reads=["BBb"], writes=["BON"])
                    pt, pn_ = PS_()
                    S.op("pe", "matmul", pt[:, 0:n], headblk, T1, start=True, stop=True, reads=["BON", "CS"], writes=[pn_])
                    S.op("act", "activation", T1, pt[:, 0:n], AF.Sqrt, bias=1e-6, scale=1.0, reads=[pn_], writes=["BON"])
                    S.op("dve", "reciprocal", T1, T1, reads=["BON"], writes=["BON"])
                    S.op("dve", "tensor_tensor", BBb[:, 0:n], BBb[:, 0:n], T1, ALU.mult, reads=["BON", "BBb"], writes=["BBb"])
                    S.op("dve", "tensor_scalar", T1, AAb[:, 0:n], -1.0, pcol_("ka"), ALU.add, ALU.mult, reads=["AAb", "PC"], writes=["BON"])
                    S.op("dve", "tensor_scalar", T1, T1, 1.0, None, ALU.add, reads=["BON"], writes=["BON"])
                    S.op("dve", "tensor_tensor", kT_, kT_, T1, ALU.mult, reads=["BON", "RKV"], writes=["RKV"])
                    S.op("dve", "tensor_tensor", T1, BBb[:, 0:n], AAb[:, 0:n], ALU.mult, reads=["BBb", "AAb"], writes=["BON"])
                    S.op("dve", "tensor_scalar", AAb[:, 0:n], BBb[:, 0:n], -1.0, None, ALU.mult, reads=["BBb", "AAb"], writes=["AAb"])
                    S.op("dve", "tensor_copy", BBb[:, 0:n], T1, reads=["BON"], writes=["BBb"])
                    S.op("dve", "scalar_tensor_tensor", T1, rT_, pcol_("rk"), kT_, ALU.mult, ALU.mult, reads=["RKV", "PC", "BON"], writes=["BON"])
                    pt, pn_ = PS_()
                    S.op("pe", "matmul", pt[:, 0:n], headblk, T1, start=True, stop=True, reads=["BON", "CS"], writes=[pn_])
                    S.op("dve", "tensor_tensor", BON[:, 0:n], pt[:, 0:n], vT_, ALU.mult, reads=[pn_, "RKV", "BON"], writes=["BON"])

                    def cons_g(col, w, pap, pres):
                        S.op("act", "activation", Gb[:, 0:n], pap, AF.Copy, reads=[pres], writes=["Gb"])
                    linear(w_lora[2][l], 0, 256, [(pr * 128, 128, [(0, 128)])], LORA, n, cons_g, kc0=2, aname="LORA", wk="lo2%d" % l)
                    if first:
                        S.op("dve", "memset", PW[:, :, :], 0.0, writes=["PW"])
                    else:
                        S.dma("act", out=PW[:, :, :], in_=(o_wkv["p"][l, pr] if kind == "p" else st_wkv[l, pr]), reads=["o_wkv"], writes=["PW"])

                def pair_c(pr, sl):
                    RKV, (AAb, BBb, LWb, Gb, BON), PW = sl["RKV"], sl["AR5"], sl["PW"]
                    TT, UU, XT1 = sl["TT"], sl["UU"], sl["XT1"]
                    pcol_ = lambda nm: PC[:, l, PO[nm] + pr:PO[nm] + pr + 1]
                    prr_ = [0]

                    def PS_():
                        bk = sl["banks"][prr_[0] % len(sl["banks"])]
                        prr_[0] += 1
                        return bk
                    for c in range(nchunk):
                        if _os0.environ.get("MK_RSTOP"):
                            continue
                        cs_ = slice(c * 128, (c + 1) * 128)
                        (GL, GAM, GINV, GPREV, RT, KT, BT_, AT, Vt, KTt, BTt, LTm, Lm, AKT, RBT, RKT, y0, y1, YN, P1T) = TT
                        P0, P0T, P1 = GINV, GPREV, GL
                        rs_ = (CS[:, CI["reset_p"], :] if kind == "p" else CS[:, CI["reset_s"], :])
                        S.op("dve", "tensor_tensor_scan", GL, rs_, LWb[:, cs_], 0.0, ALU.mult, ALU.add, reads=["CS", "LWb"], writes=["GL"])
                        S.op("act", "activation", GAM, GL, AF.Exp, reads=["GL"], writes=["GAM"])
                        S.op("act", "activation", GINV, GL, AF.Exp, scale=-1.0, reads=["GL"], writes=["GINV"])
                        S.op("dve", "tensor_tensor", GPREV, GL, LWb[:, cs_], ALU.subtract, reads=["GL", "LWb"], writes=["GPREV"])
                        S.op("act", "activation", GPREV, GPREV, AF.Exp, reads=["GPREV"], writes=["GPREV"])
                        S.op("dve", "tensor_tensor", RT, RKV[:, 0, cs_], GAM, ALU.mult, reads=["RKV", "GAM"], writes=["RT"])
                        S.op("dve", "tensor_tensor", KT, RKV[:, 1, cs_], GINV, ALU.mult, reads=["RKV", "GINV"], writes=["KT"])
                        S.op("dve", "tensor_tensor", BT_, BBb[:, cs_], GINV, ALU.mult, reads=["BBb", "GINV"], writes=["BT_"])
                        S.op("dve", "tensor_tensor", AT, AAb[:, cs_], GPREV, ALU.mult, reads=["AAb", "GPREV"], writes=["AT"])
                        pt, pn_ = PS_()
                        S.op("pe", "transpose", pt[:, 0:128], RKV[:, 2, cs_], ident, reads=["RKV", "CS"], writes=[pn_])
                        S.op("pe", "transpose", pt[:, 128:256], KT, ident, reads=["KT", "CS"], writes=[pn_])
                        S.op("pe", "transpose", pt[:, 256:384], BT_, ident, reads=["BT_", "CS"], writes=[pn_])
                        S.op("act", "activation", Vt, pt[:, 0:128], AF.Copy, reads=[pn_], writes=["Vt"])
                        S.op("dve", "tensor_copy", KTt, pt[:, 128:256], reads=[pn_], writes=["KTt"])
                        S.op("act", "activation", BTt, pt[:, 256:384], AF.Copy, reads=[pn_], writes=["BTt"])
                        HB = [(LTm, Lm, AKT, RBT, RKT, y0, y1, P1T, P0, P0T, P1, "GINV", "GPREV", "GL")]
                        if nb == 1:
                            HB.append(tuple(XT1) + ("P0x", "P0Tx", "P1x"))

                        def head_chain(hh, threaded):
                            (LTm, Lm, AKT, RBT, RKT, y0, y1, P1T, P0, P0T, P1, nP0, nP0T, nP1) = HB[hh if threaded else 0]
                            nbk = len(sl["banks"]) // 2
                            banks = sl["banks"][hh * nbk:(hh + 1) * nbk] if threaded else sl["banks"]
                            rr_ = [0]

                            def PS_():
                                bk = banks[rr_[0] % len(banks)]
                                rr_[0] += 1
                                return bk
                            hp = slice(hh * 64, (hh + 1) * 64)
                            hv = slice(hh * 64, (hh + 1) * 64)
                            pa, pan = PS_()
                            S.op("pe", "matmul", pa[:, 0:128], BT_[hp, :], AT[hp, :], start=True, stop=True, reads=["BT_", "AT"], writes=[pan])
                            S.op("pe", "matmul", pa[:, 128:256], AT[hp, :], BT_[hp, :], start=True, stop=True, reads=["BT_", "AT"], writes=[pan])
                            S.op("pe", "matmul", pa[:, 256:384], KT[hp, :], AT[hp, :], start=True, stop=True, reads=["KT", "AT"], writes=[pan])
                            pb2, pbn = PS_()
                            S.op("pe", "matmul", pb2[:, 0:128], BT_[hp, :], RT[hp, :], start=True, stop=True, reads=["BT_", "RT"], writes=[pbn])
                            S.op("pe", "matmul", pb2[:, 128:256], KT[hp, :], RT[hp, :], start=True, stop=True, reads=["KT", "RT"], writes=[pbn])
                            S.op("dve", "tensor_tensor", LTm, pa[:, 0:128], MUs, ALU.mult, reads=[pan, "CK"], writes=["LTm"])
                            S.op("dve", "tensor_tensor", Lm, pa[:, 128:256], MLs, ALU.mult, reads=[pan, "CK"], writes=["Lm"])
                            S.op("dve", "tensor_tensor", AKT, pa[:, 256:384], MUs, ALU.mult, reads=[pan, "CK"], writes=["AKT"])
                            S.op("dve", "tensor_tensor", RBT, pb2[:, 0:128], MUi, ALU.mult, reads=[pbn, "CK"], writes=["RBT"])
                            S.op("dve", "tensor_tensor", RKT, pb2[:, 128:256], MUi, ALU.mult, reads=[pbn, "CK"], writes=["RKT"])

                            def state_acc(pt2, pn2, srcT, srcname, first_start):
                                if nb == 1:
                                    S.op("pe", "matmul", pt2[:, 0:64], srcT[hp, :], PW[hp, 0, :], start=first_start, stop=True,
                                         reads=[srcname, "PW"], writes=[pn2])
                                else:
                                    S.op("dve", "memset", KQM_flat[:, 0:2048], 0.0, writes=["KQM"])
                                    S.op("dve", "tensor_copy", KQD[hp], srcT[hp, :].rearrange("p (b t) -> p b t", b=16), reads=[srcname], writes=["KQM"])
                                    for b_ in range(nb):
                                        S.op("pe", "matmul", pt2[:, 0:64], KQM[hp, b_, :], PW[hp, b_, :], start=(first_start and b_ == 0),
                                             stop=(b_ == nb - 1), reads=["KQM", "PW"], writes=[pn2])
                            pr_, prn = PS_()
                            S.op("pe", "matmul", pr_[:, 0:64], AKT, Vt[:, hv], start=True, stop=False, reads=["AKT", "Vt"], writes=[prn])
                            state_acc(pr_, prn, AT, "AT", False)
                            S.op("act", "activation", y0[:, 0:64], pr_[:, 0:64], AF.Copy, reads=[prn], writes=["y0"])
                            pt, pn_ = PS_()
                            S.op("pe", "matmul", pt[:, 0:64], LTm, y0[:, 0:64], start=True, stop=True, reads=["LTm", "y0"], writes=[pn_])
                            S.op("dve", "tensor_tensor", y1[:, 0:64], y0[:, 0:64], pt[:, 0:64], ALU.add, reads=[pn_, "y0"], writes=["y1"])
                            ycur, yn_, yoth, yon_ = y1, "y1", y0, "y0"
                            Pc, PcT, Pcn, PcTn = Lm, LTm, "Lm", "LTm"
                            pw = [(P0, P0T, nP0, nP0T), (P1, P1T, nP1, "P1T")]
                            for lev in range(levels):
                                Pn, PnT, Pnn, PnTn = pw[lev % 2]
                                pt, pn_ = PS_()
                                S.op("pe", "matmul", pt[:, 0:128], Pc, PcT, start=True, stop=True, reads=[Pcn, PcTn], writes=[pn_])
                                if lev < levels - 1:
                                    S.op("pe", "matmul", pt[:, 128:256], PcT, Pc, start=True, stop=True, reads=[Pcn, PcTn], writes=[pn_])
                                S.op("act", "activation", PnT, pt[:, 0:128], AF.Copy, reads=[pn_], writes=[PnTn])
                                if lev < levels - 1:
                                    S.op("dve", "tensor_copy", Pn, pt[:, 128:256], reads=[pn_], writes=[Pnn])
                                pt2, pn2 = PS_()
                                S.op("pe", "matmul", pt2[:, 0:64], PnT, ycur[:, 0:64], start=True, stop=True, reads=[PnTn, yn_], writes=[pn2])
                                S.op("dve", "tensor_tensor", yoth[:, 0:64], ycur[:, 0:64], pt2[:, 0:64], ALU.add, reads=[pn2, yn_], writes=[yon_])
                                ycur, yn_, yoth, yon_ = yoth, yon_, ycur, yn_
                                Pc, PcT, Pcn, PcTn = Pn, PnT, Pnn, PnTn
                            Um, Un_ = ycur, yn_
                            py, pyn = PS_()
                            S.op("pe", "matmul", py[:, 0:64], RBT, Um[:, 0:64], start=True, stop=False, reads=["RBT", Un_], writes=[pyn])
                            S.op("pe", "matmul", py[:, 0:64], RKT, Vt[:, hv], start=False, stop=False, reads=["RKT", "Vt"], writes=[pyn])
                            state_acc(py, pyn, RT, "RT", False)
                            st6 = AKT[:, 0:6]
                            mv = AKT[:, 8:10]
                            S.op("act", "activation", RBT[:, 0:64], py[:, 0:64], AF.Copy, reads=[pyn], writes=["RBT"])
                            S.op("dve", "bn_stats", st6, RBT[:, 0:64], reads=["RBT"], writes=["AKT"])
                            S.op("dve", "bn_aggr", mv, st6, reads=["AKT"], writes=["AKT"])
                            S.op("act", "activation", mv[:, 1:2], mv[:, 1:2], AF.Sqrt, bias=64e-5, scale=1.0, reads=["AKT"], writes=["AKT"])
                            S.op("dve", "reciprocal", mv[:, 1:2], mv[:, 1:2], reads=["AKT"], writes=["AKT"])
                            S.op("dve", "tensor_scalar", YN[:, hv], RBT[:, 0:64], mv[:, 0:1], mv[:, 1:2], ALU.subtract, ALU.mult,
                                 reads=["RBT", "AKT"], writes=["YN"])
                            S.op("dve", "tensor_copy", UU[:, hv], Um[:, 0:64], reads=[Un_], writes=["UU"])
                        if nb == 1:
                            ths = []
                            for hh in range(2):
                                S.sfx = "@%d" % hh
                                S.thread_begin()
                                head_chain(hh, True)
                                ths.append(S.thread_end())
                            S.sfx = ""
                            S.replay(ths)
                        else:
                            for hh in range(2):
                                head_chain(hh, False)
                        for b_ in range(nb):
                            if nb > 1:
                                S.op("dve", "tensor_scalar", KM1, BTt, rowmask[:, b_:b_ + 1], None, ALU.mult, reads=["BTt", "CS"], writes=["KM1"])
                                S.op("dve", "tensor_scalar", KM2, KTt, rowmask[:, b_:b_ + 1], None, ALU.mult, reads=["KTt", "CS"], writes=["KM2"])
                                lb, lbn = (KM1, KM2), ("KM1", "KM2")
                            else:
                                lb, lbn = (BTt, KTt), ("BTt", "KTt")
                            lastc = (b_ + 1) * L - 1 if nb > 1 else 127
                            for hh in range(2):
                                hp = slice(hh * 64, (hh + 1) * 64)
                                hv = hp
                                pt, pn_ = PS_()
                                S.op("pe", "matmul", pt[hp, 0:64], lb[0][:, hv], UU[:, hv], start=True, stop=False, reads=[lbn[0], "UU"], writes=[pn_])
                                S.op("pe", "matmul", pt[hp, 0:64], lb[1][:, hv], Vt[:, hv], start=False, stop=True, reads=[lbn[1], "Vt"], writes=[pn_])
                                S.op("dve", "tensor_tensor", PW[hp, b_, :], PW[hp, b_, :], pt[hp, 0:64], ALU.add, reads=[pn_, "PW"], writes=["PW"])
                                S.op("dve", "tensor_scalar", PW[hp, b_, :], PW[hp, b_, :], GAM[hp, lastc:lastc + 1], None, ALU.mult,
                                     reads=["GAM", "PW"], writes=["PW"])
                        pt, pn_ = PS_()
                        S.op("pe", "transpose", pt[:, 0:128], YN, ident, reads=["YN", "CS"], writes=[pn_])
                        S.op("dve", "tensor_scalar", RT, pt[:, 0:128], pcol_("lnw"), pcol_("lnb"), ALU.mult, ALU.add, reads=[pn_, "PC"], writes=["RT"])
                        S.op("dve", "tensor_tensor", RT, RT, BON[:, cs_], ALU.add, reads=["RT", "BON"], writes=["RT"])
                        S.op("dve", "tensor_tensor", YBR[:, 4 + pr, cs_], RT, Gb[:, cs_], ALU.mult, reads=["RT", "Gb"], writes=["YBR"])
                    S.dma("act", out=o_wkv[kind][l, pr], in_=PW[:, :, :], reads=["PW"], writes=["o_wkv"])

                for pg in range(0, 4, NP):
                    for k_ in range(NP):
                        S.sfx2 = "#%d" % k_
                        pair_p(pg + k_, pslots[k_])
                    pths = []
                    for k_ in range(NP):
                        S.sfx2 = "#%d" % k_
                        S.thread_begin()
                        pair_c(pg + k_, pslots[k_])
                        pths.append(S.thread_end())
                    S.sfx2 = ""
                    S.replay(pths)
                if NP == 2:
                    S.barrier()
                    ws_busy[0] = False
                S.private2 = set()
                S.dma("act", out=o_shift[kind][l], in_=rst, reads=["STT"], writes=["o_shift"])
                S.barrier()

                S.barrier()
                MUi, MUs, MLs, NEGU, POSL, SAME = (CK[:, i_, :] for i_ in range(6))
                rowmask = CS[:, CI["rowmask"], 0:NSEQ_S] if kind == "s" else CS[:, CI["ones"], 0:NSEQ_S]
                nchunk = n // 128
                levels = 6 if kind == "p" else 2
                bump = [6976]

                def alloc(k):
                    o_ = bump[0]
                    bump[0] += k
                    assert bump[0] <= NA, bump[0]
                    return AR[:, o_:o_ + k]
                QKV = alloc(3 * n).rearrange("p (c t) -> p c t", c=3)
                SG = alloc(nb * 128).rearrange("p (b v) -> p b v", b=nb)
                TT = [alloc(128) for _ in range(22)]
                BTF = alloc(n)
                GCH = alloc(128)
                L2S = alloc(n)
                KQM_flat = AR[:, 1040:1040 + 2176]
                KQM = KQM_flat[:, 0:2048].rearrange("p (b c) -> p b c", b=16)
                KQD = KQM_flat.rearrange("p (b x) -> p b x", x=136)[:, :, 0:8]
                SC8c = [alloc(64) for _ in range(nchunk)]
                GCFc = [alloc(128) for _ in range(nchunk)]
                EGc = [alloc(nb * 8) for _ in range(nchunk)]
                GM = SB5[1][:, 1, :]
                ZS = MRG
                def cons_z(col, w, pap, pres):
                    S.op("act", "activation", ZS[:, (col - C_Z) // 128, 0:n], pap, AF.Silu, reads=[pres], writes=["ZS"])
                linear(w_in[l], 0, D, [(C_Z + i_ * 256, 256, [(0, 128), (128, 128)]) for i_ in range(4)], HN, n, cons_z, wk="in%d" % l)
                ws_ = SB5[0][:, :, :].rearrange("p a b -> p (a b)")
                S.dma("sp", out=ws_[:, 0:256].rearrange("p (kc n) -> p kc n", kc=16),
                      in_=w_in[l][:, C_B:C_B + 16].rearrange("(kc p) n -> p kc n", p=128), writes=["SB5_0"])
                S.op("dve", "tensor_copy", WBA[:, :, :], ws_[:, 0:256].rearrange("p (kc n) -> p kc n", kc=16),
                     reads=["SB5_0"], writes=["WBA"])
                wv_ = WBA
                pb_, pa_ = PSM[0], PSM[1]
                for (pp, c0, pn_) in ((pb_, 0, "PSM0"), (pa_, 8, "PSM1")):
                    for kc in range(16):
                        S.op("pe", "matmul", pp[0:8, 0:n], wv_[:, kc, c0:c0 + 8], HN[:, kc, 0:n], start=(kc == 0), stop=(kc == 15),
                             reads=["WBA", "HN"], writes=[pn_])
                S.op("act", "activation", BTF[0:8, 0:n], pb_[0:8, 0:n], AF.Sigmoid, reads=["PSM0"], writes=["BTF"])
                alog = PR[:, l, 0:8]
                dtb = PR[:, l, 8:16]
                NEA = alloc(8)
                S.op("act", "activation", NEA, alog, AF.Exp, reads=["PR"], writes=["NEA"])
                S.op("dve", "tensor_scalar", NEA, NEA, -1.0, None, ALU.mult, reads=["NEA"], writes=["NEA"])

                gst = STT[:, 0:24 * nb * 3].rearrange("p (c b h) -> p c b h", c=24, b=nb)
                if first:
                    S.op("dve", "memset", STT[:, 0:24 * nb * 3], 0.0, writes=["STT"])
                else:
                    S.dma("act", out=gst, in_=(o_gconv["p"][l] if kind == "p" else st_gconv[l]), reads=["o_gconv"], writes=["STT"])

                ps_rr = [0]

                def PS_():
                    i_ = ps_rr[0] % 6
                    ps_rr[0] += 1
                    return PSM[i_], "PSM%d" % i_

                import os as _os
                GSTOP = int(_os.environ.get("MK_GSTOP", "99"))
                for c in range(nchunk):
                    cs_ = slice(c * 128, (c + 1) * 128)
                    sc8 = SC8c[c]
                    BET, GTK, GCT, EGC, NBEG, NGC, EDEC, GLT = (sc8[:, i_ * 8:(i_ + 1) * 8] for i_ in range(8))
                    GCF = GCFc[c]
                    EG = EGc[c]
                    pt, pn_ = PS_()
                    for kc in range(16):
                        S.op("pe", "matmul", pt[:, 0:16], HN[:, kc, cs_], wv_[:, kc, 0:16], start=(kc == 0), stop=(kc == 15),
                             reads=["WBA", "HN"], writes=[pn_])
                    S.op("act", "activation", BET, pt[:, 0:8], AF.Sigmoid, reads=[pn_], writes=["sc8"])
                    S.op("dve", "tensor_tensor", GTK, pt[:, 8:16], dtb, ALU.add, reads=[pn_, "PR"], writes=["sc8"])
                    S.op("act", "activation", GTK, GTK, AF.Exp, reads=["sc8"], writes=["sc8"])
                    S.op("act", "activation", GTK, GTK, AF.Ln, bias=1.0, scale=1.0, reads=["sc8"], writes=["sc8"])
                    S.op("dve", "tensor_tensor", GTK, GTK, NEA, ALU.mult, reads=["sc8", "NEA"], writes=["sc8"])
                    pt, pn_ = PS_()
                    S.op("pe", "matmul", pt[:, 0:8], MUi, GTK, start=True, stop=True, reads=["CK", "sc8"], writes=[pn_])
                    S.op("pe", "matmul", pt[:, 8:16], SAME, GTK, start=True, stop=True, reads=["CK", "sc8"], writes=[pn_])
                    S.op("pe", "matmul", pt[0:8, 128:256], GTK, MUi, start=True, stop=True, reads=["CK", "sc8"], writes=[pn_])
                    S.op("dve", "tensor_copy", GCT, pt[:, 0:8], reads=[pn_], writes=["sc8"])
                    S.op("dve", "tensor_copy", GLT, pt[:, 8:16], reads=[pn_], writes=["sc8"])
                    S.op("dve", "tensor_copy", GCF[0:8, :], pt[0:8, 128:256], reads=[pn_], writes=["GCF"])
                    S.op("act", "activation", EGC, GCT, AF.Exp, reads=["sc8"], writes=["sc8"])
                    S.op("dve", "tensor_tensor", NBEG, BET, EGC, ALU.mult, reads=["sc8"], writes=["sc8"])
                    S.op("dve", "tensor_scalar", NBEG, NBEG, -1.0, None, ALU.mult, reads=["sc8"], writes=["sc8"])
                    S.op("dve", "tensor_scalar", NGC, GCT, -1.0, None, ALU.mult, reads=["sc8"], writes=["sc8"])
                    S.op("dve", "tensor_tensor", EDEC, GLT, GCT, ALU.subtract, reads=["sc8"], writes=["sc8"])
                    S.op("act", "activation", EDEC, EDEC, AF.Exp, reads=["sc8"], writes=["sc8"])
                    S.op("dve", "tensor_tensor", GM.rearrange("p (b h) -> p b h", b=16)[:, 0:nb, :],
                         GTK.unsqueeze(1).to_broadcast([128, nb, 8]), rowmask[:, 0:nb].unsqueeze(2).to_broadcast([128, nb, 8]),
                         ALU.mult, reads=["sc8", "CS"], writes=["GM"])
                    pt, pn_ = PS_()
                    S.op("pe", "matmul", pt[:, 0:nb * 8], ones, GM[:, 0:nb * 8], start=True, stop=True, reads=["GM", "CS"], writes=[pn_])
                    S.op("act", "activation", EG[:, 0:nb * 8], pt[:, 0:nb * 8], AF.Exp, reads=[pn_], writes=["EG"])
                if GSTOP <= 1:
                    continue

                S.private = {"QKV", "SG", "Kdec", "Vb", "t1", "DTi", "t2", "Dms", "DTs", "Nm", "NTm", "tB", "inT", "Rm", "y0", "y1",
                             "P0", "P0T", "P1", "P1T", "IVs", "om", "Km", "junk", "GCH", "KQM"}
                G = (4 if ti > 0 else 2) if kind == "p" else 1
                BK = [(PSM[i_], "PSM%d" % i_) for i_ in range(6)] + [(PSD[0], "PSD0"), (PSD[1], "PSD1")]
                slots = [dict(QKV=QKV, SG=SG, TT=TT, GCH=GCH, banks=BK[0:6] if G == 1 else (BK[0:3] if G == 2 else BK[0:2]))]
                if G >= 2:
                    ra, rb = [1040], [4240]

                    def allocA(k):
                        o_ = ra[0]
                        ra[0] += k
                        assert ra[0] <= 4160
                        return AR[:, o_:o_ + k]

                    def allocB(k):
                        o_ = rb[0]
                        rb[0] += k
                        assert rb[0] <= 6976
                        return AR[:, o_:o_ + k]
                    slots.append(dict(QKV=allocB(3 * n).rearrange("p (c t) -> p c t", c=3),
                                      SG=allocB(nb * 128).rearrange("p (b v) -> p b v", b=nb),
                                      GCH=allocB(128), TT=[allocA(128) for _ in range(22)],
                                      banks=BK[3:6] if G == 2 else BK[2:4]))
                if G == 4:
                    S.barrier()
                    ws_busy[0] = True
                    for k_, wsx in enumerate(WS):
                        tt_ = [wsx[:, 1792 + i_ * 128:1792 + (i_ + 1) * 128] for i_ in range(18)]
                        (Kdec, Vb, t1, t2, DTs, Nm, NTm, tB, inT, Rm, y0, y1, P0, P0T, P1, P1T, om, junk) = tt_
                        slots.append(dict(QKV=wsx[:, 0:3 * n].rearrange("p (c t) -> p c t", c=3),
                                          SG=wsx[:, 1536:1664].rearrange("p (b v) -> p b v", b=nb), GCH=wsx[:, 1664:1792],
                                          TT=[Kdec, Vb, t1, t1, t2, t2, DTs, Nm, NTm, tB, inT, Rm, y0, y1, P0, P0T, P1, P1T, t1, om, junk, junk],
                                          alias={"DTi": "t1", "Dms": "t2", "IVs": "t1", "Km": "junk"},
                                          banks=BK[4 + 2 * k_:6 + 2 * k_]))

                def p_stage(h, sl):
                    QKV, SG = sl["QKV"], sl["SG"]
                    def after_qkv(bi, w, ev, eres):
                        chn = (h, 8 + h, 16 + h)[bi]
                        cw = PC[:, l, PO["gcw"]:PO["gcw"] + 96]
                        dst = QKV[:, bi, 0:n].rearrange("p (b t) -> p b t", b=nb)
                        S.op("dve", "tensor_scalar", dst, ev[:, :, 0:L], cw[:, chn:chn + 1], None, ALU.mult,
                             reads=[eres, "PC"], writes=["QKV"])
                        for wi in (1, 2, 3):
                            S.op("dve", "scalar_tensor_tensor", dst, ev[:, :, wi:wi + L], cw[:, wi * 24 + chn:wi * 24 + chn + 1], dst,
                                 ALU.mult, ALU.add, reads=[eres, "PC", "QKV"], writes=["QKV"])
                        S.op("act", "activation", QKV[:, bi, 0:n], QKV[:, bi, 0:n], AF.Silu, reads=["QKV"], writes=["QKV"])
                    halo_proj(w_in[l], C_QKV, [(h * 128, 128), (1024 + h * 128, 128), (2048 + h * 128, 128)], 3, None, None,
                              "o_gconv", after_qkv, nbuf=2, sblk=([h, 8 + h, 16 + h], 24), wk="in%d" % l)
                    for bi, scl in ((0, 128.0 ** -0.5), (1, 1.0)):
                        pt, pn_ = PS_()
                        S.op("act", "activation", L2S[:, 0:n], QKV[:, bi, 0:n], AF.Square, reads=["QKV"], writes=["L2S"])
                        S.op("pe", "matmul", pt[:, 0:n], ones, L2S[:, 0:n], start=True, stop=True, reads=["L2S", "CS"], writes=[pn_])
                        S.op("act", "activation", L2S[:, 0:n], pt[:, 0:n], AF.Sqrt, bias=1e-6, scale=1.0, reads=[pn_], writes=["L2S"])
                        S.op("dve", "reciprocal", L2S[:, 0:n], L2S[:, 0:n], reads=["L2S"], writes=["L2S"])
                        S.op("dve", "scalar_tensor_tensor", QKV[:, bi, 0:n], QKV[:, bi, 0:n], scl, L2S[:, 0:n], ALU.mult, ALU.mult,
                             reads=["L2S", "QKV"], writes=["QKV"])
                    if first:
                        S.op("dve", "memset", SG[:, :, :], 0.0, writes=["SG"])
                    else:
                        S.dma("act", out=SG[:, :, :], in_=(o_gdn["p"][l, h] if kind == "p" else st_gdn[l, h]), reads=["o_gdn"], writes=["SG"])

                def c_stage(h, sl):
                    QKV, SG, TT, GCH = sl["QKV"], sl["SG"], sl["TT"], sl["GCH"]
                    rr_ = [0]

                    def PS_():
                        bk = sl["banks"][rr_[0] % len(sl["banks"])]
                        rr_[0] += 1
                        return bk
                    for c in range(nchunk):
                        cs_ = slice(c * 128, (c + 1) * 128)
                        qT, kT, vT = QKV[:, 0, cs_], QKV[:, 1, cs_], QKV[:, 2, cs_]
                        (Kdec, Vb, t1, DTi, t2, Dms, DTs, Nm, NTm, tB, inT, Rm, y0, y1, P0, P0T, P1, P1T, IVs, om, Km, junk) = TT
                        sc8 = SC8c[c]
                        BET, GTK, GCT, EGC, NBEG, NGC, EDEC, GLT = (sc8[:, i_ * 8:(i_ + 1) * 8] for i_ in range(8))
                        GCF = GCFc[c]
                        EGv = EGc[c].rearrange("p (b h) -> p b h", b=nb)
                        hs_ = slice(h, h + 1)
                        pt, pn_ = PS_()
                        S.op("pe", "transpose", pt[:, 0:128], kT, ident, reads=["QKV", "CS"], writes=[pn_])
                        S.op("pe", "transpose", pt[:, 128:256], vT, ident, reads=["QKV", "CS"], writes=[pn_])
                        S.op("dve", "tensor_scalar", Kdec, pt[:, 0:128], EDEC[:, hs_], None, ALU.mult, reads=[pn_, "sc8"], writes=["Kdec"])
                        S.op("dve", "tensor_scalar", Vb, pt[:, 128:256], BET[:, hs_], None, ALU.mult, reads=[pn_, "sc8"], writes=["Vb"])
                        S.op("dve", "tensor_scalar", GCH[0:8, :], GCF[0:8, :], ident[0:8, hs_], None, ALU.mult, reads=["GCF", "CS"], writes=["GCH"])
                        S.op("dve", "tensor_scalar", junk[0:8, :], BTF[0:8, cs_], ident[0:8, hs_], None, ALU.mult, reads=["BTF", "CS"], writes=["junk"])
                        pg, pgn = PS_()
                        S.op("pe", "matmul", pg[:, 0:128], kT, kT, start=True, stop=True, reads=["QKV"], writes=[pgn])
                        S.op("pe", "matmul", pg[:, 128:256], kT, qT, start=True, stop=True, reads=["QKV"], writes=[pgn])
                        S.op("pe", "matmul", pg[:, 256:384], ones[0:8, :], GCH[0:8, :], start=True, stop=True, reads=["GCH", "CS"], writes=[pgn])
                        S.op("pe", "matmul", pg[:, 384:512], ones[0:8, :], junk[0:8, :], start=True, stop=True, reads=["junk", "CS"], writes=[pgn])
                        KKp, ITp, GRp, BRp = pg[:, 0:128], pg[:, 128:256], pg[:, 256:384], pg[:, 384:512]
                        if GSTOP <= 2:
                            continue
                        S.op("dve", "tensor_tensor", t1, GRp, NEGU, ALU.add, reads=[pgn, "CK"], writes=["t1"])
                        S.op("act", "activation", DTi, t1, AF.Exp, bias=NGC[:, hs_], scale=1.0, reads=["t1", "sc8"], writes=["DTi"])
                        S.op("dve", "tensor_tensor", t2, GRp, POSL, ALU.add, reads=[pgn, "CK"], writes=["t2"])
                        S.op("act", "activation", Dms, t2, AF.Exp, bias=GCT[:, hs_], scale=-1.0, reads=["t2", "sc8"], writes=["Dms"])
                        S.op("dve", "tensor_tensor", Dms, Dms, MLs, ALU.mult, reads=["Dms", "CK"], writes=["Dms"])
                        S.op("dve", "tensor_tensor", DTs, DTi, MUs, ALU.mult, reads=["DTi", "CK"], writes=["DTs"])
                        S.op("dve", "scalar_tensor_tensor", Nm, KKp, BET[:, hs_], Dms, ALU.mult, ALU.mult, reads=[pgn, "sc8", "Dms"], writes=["Nm"])
                        S.op("dve", "tensor_tensor", tB, DTs, BRp, ALU.mult, reads=[pgn, "DTs"], writes=["tB"])
                        S.op("dve", "tensor_tensor", NTm, KKp, tB, ALU.mult, reads=[pgn, "tB"], writes=["NTm"])
                        S.op("dve", "tensor_tensor", inT, ITp, DTi, ALU.mult, reads=[pgn, "DTi"], writes=["inT"])
                        if GSTOP <= 4:
                            continue
                        def state_mm(srcT, srcname):
                            pt2, pn2 = PS_()
                            if nb == 1:
                                S.op("pe", "matmul", pt2[:, 0:128], srcT, SG[:, 0, :], start=True, stop=True, reads=[srcname, "SG"], writes=[pn2])
                            else:
                                S.op("dve", "memset", KQM_flat[:, 0:2048], 0.0, writes=["KQM"])
                                S.op("dve", "tensor_copy", KQD, srcT.rearrange("p (b t) -> p b t", b=16), reads=[srcname], writes=["KQM"])
                                for b_ in range(nb):
                                    S.op("pe", "matmul", pt2[:, 0:128], KQM[:, b_, :], SG[:, b_, :], start=(b_ == 0), stop=(b_ == nb - 1),
                                         reads=["KQM", "SG"], writes=[pn2])
                            return pt2, pn2
                        pk, pkn = state_mm(kT, "QKV")
                        S.op("dve", "scalar_tensor_tensor", Rm, pk[:, 0:128], NBEG[:, hs_], Vb, ALU.mult, ALU.add, reads=[pkn, "sc8", "Vb"], writes=["Rm"])
                        if GSTOP <= 5:
                            continue
                        pt, pn_ = PS_()
                        S.op("pe", "matmul", pt[:, 0:128], NTm, Rm, start=True, stop=True, reads=["NTm", "Rm"], writes=[pn_])
                        S.op("dve", "tensor_tensor", y0, Rm, pt[:, 0:128], ALU.subtract, reads=[pn_, "Rm"], writes=["y0"])
                        ycur, yn_, yoth, yon_ = y0, "y0", y1, "y1"
                        Pc, PcT, Pcn, PcTn = Nm, NTm, "Nm", "NTm"
                        pw = [(P0, P0T, "P0", "P0T"), (P1, P1T, "P1", "P1T")]
                        for lev in range(min(levels, int(_os.environ.get("MK_G6", "99")))):
                            Pn, PnT, Pnn, PnTn = pw[lev % 2]
                            pt, pn_ = PS_()
                            S.op("pe", "matmul", pt[:, 0:128], Pc, PcT, start=True, stop=True, reads=[Pcn, PcTn], writes=[pn_])
                            if lev < levels - 1:
                                S.op("pe", "matmul", pt[:, 128:256], PcT, Pc, start=True, stop=True, reads=[Pcn, PcTn], writes=[pn_])
                            S.op("act", "activation", PnT, pt[:, 0:128], AF.Copy, reads=[pn_], writes=[PnTn])
                            if lev < levels - 1:
                                S.op("dve", "tensor_copy", Pn, pt[:, 128:256], reads=[pn_], writes=[Pnn])
                            pt2, pn2 = PS_()
                            S.op("pe", "matmul", pt2[:, 0:128], PnT, ycur, start=True, stop=True, reads=[PnTn, yn_], writes=[pn2])
                            S.op("dve", "tensor_tensor", yoth, ycur, pt2[:, 0:128], ALU.add, reads=[pn2, yn_], writes=[yon_])
                            ycur, yn_, yoth, yon_ = yoth, yon_, ycur, yn_
                            Pc, PcT, Pcn, PcTn = Pn, PnT, Pnn, PnTn
                        vnew, vn_ = ycur, yn_
                        if GSTOP <= 6:
                            continue
                        pq, pqn = state_mm(qT, "QKV")
                        pt, pn_ = PS_()
                        S.op("pe", "matmul", pt[:, 0:128], inT, vnew, start=True, stop=True, reads=["inT", vn_], writes=[pn_])
                        S.op("act", "activation", IVs, pt[:, 0:128], AF.Copy, reads=[pn_], writes=["IVs"])
                        S.op("dve", "scalar_tensor_tensor", om, pq[:, 0:128], EGC[:, hs_], IVs, ALU.mult, ALU.add, reads=[pqn, "sc8", "IVs"], writes=["om"])
                        if GSTOP <= 7:
                            continue
                        ssq = junk[:, 0:1]
                        S.op("act", "activation", t1, om, AF.Square, accum_out=ssq, reads=["om"], writes=["t1", "junk"])
                        S.op("act", "activation", ssq, ssq, AF.Sqrt, bias=1e-6, scale=1.0 / 128, reads=["junk"], writes=["junk"])
                        S.op("dve", "reciprocal", ssq, ssq, reads=["junk"], writes=["junk"])
                        S.op("dve", "tensor_scalar", om, om, ssq, None, ALU.mult, reads=["junk", "om"], writes=["om"])
                        pt, pn_ = PS_()
                        S.op("pe", "transpose", pt[:, 0:128], om, ident, reads=["om", "CS"], writes=[pn_])
                        S.op("dve", "scalar_tensor_tensor", YBR[:, 8 + h, cs_], pt[:, 0:128], PC[:, l, PO["gng"]:PO["gng"] + 1], ZS[:, h, cs_],
                             ALU.mult, ALU.mult, reads=[pn_, "PC", "ZS"], writes=["YBR"])
                        if GSTOP <= 8:
                            continue
                        if nb == 1:
                            pt, pn_ = PS_()
                            S.op("pe", "matmul", pt[:, 0:128], Kdec, vnew, start=True, stop=True, reads=["Kdec", vn_], writes=[pn_])
                            S.op("dve", "scalar_tensor_tensor", SG[:, 0, :], SG[:, 0, :], EGv[:, 0, hs_], pt[:, 0:128], ALU.mult, ALU.add,
                                 reads=[pn_, "EG", "SG"], writes=["SG"])
                        else:
                            S.op("dve", "tensor_tensor", KQM, Kdec.unsqueeze(1).to_broadcast([128, 16, 128]),
                                 rowmask.unsqueeze(2).to_broadcast([128, 16, 128]), ALU.mult, reads=["Kdec", "CS"], writes=["KQM"])
                            for b_ in range(nb):
                                pt, pn_ = PS_()
                                S.op("pe", "matmul", pt[:, 0:128], KQM[:, b_, :], vnew, start=True, stop=True, reads=["KQM", vn_], writes=[pn_])
                                S.op("dve", "scalar_tensor_tensor", SG[:, b_, :], SG[:, b_, :], EGv[:, b_, hs_], pt[:, 0:128], ALU.mult, ALU.add,
                                     reads=[pn_, "EG", "SG"], writes=["SG"])
                    S.dma("act", out=o_gdn[kind][l, h], in_=SG[:, :, :], reads=["SG"], writes=["o_gdn"])

                for hg in range(0, 8, G):
                    for si in range(G):
                        S.sfx = "@%d" % si
                        S.alias = slots[si].get("alias", {})
                        p_stage(hg + si, slots[si])
                    ths = []
                    for si in range(G):
                        S.sfx = "@%d" % si
                        S.alias = slots[si].get("alias", {})
                        S.thread_begin()
                        c_stage(hg + si, slots[si])
                        ths.append(S.thread_end())
                    S.sfx = ""
                    S.alias = {}
                    S.replay(ths)
                if G == 4:
                    S.barrier()
                    ws_busy[0] = False
                S.dma("act", out=o_gconv[kind][l], in_=gst, reads=["STT"], writes=["o_gconv"])
                S.barrier()

                if stage < 2:
                    continue
                BR = [(0, 4), (4, 4), (8, 8)]
                for op_ in range(8):
                    def cons_gate(br):
                        def f(col, w, pap, pres):
                            sub = ((col - C_G) // 128) % 2
                            g = GT[br * 2 + sub]
                            S.op("act", "activation", g[:, 0:n], pap, AF.Sigmoid, reads=[pres], writes=["GT%d" % (br * 2 + sub)])
                        return f
                    for br in range(3):
                        c0 = C_G + br * D + op_ * 256
                        linear(w_in[l], 0, D, [(c0, 256, [(0, 128), (128, 128)])], HN, n, cons_gate(br), wk="in%d" % l)
                    for br in range(3):
                        def cons_br(col, w, pap, pres, br=br):
                            sub = (col // 128) % 2
                            g = GT[br * 2 + sub]
                            if br == 0:
                                S.op("dve", "tensor_tensor", ACC[sub][:, 0:n], g[:, 0:n], pap, ALU.mult,
                                     reads=[pres, "GT%d" % (br * 2 + sub)], writes=["ACC%d" % sub])
                            else:
                                S.op("dve", "tensor_tensor", TMP[sub][:, 0:n], g[:, 0:n], pap, ALU.mult,
                                     reads=[pres, "GT%d" % (br * 2 + sub)], writes=["TMP%d" % sub])
                                dst = ACC[sub][:, 0:n] if br == 1 else MRG[:, op_ * 2 + sub, 0:n]
                                S.op("dve", "tensor_tensor", dst, ACC[sub][:, 0:n], TMP[sub][:, 0:n], ALU.add,
                                     reads=["ACC%d" % sub, "TMP%d" % sub], writes=["ACC%d" % sub] if br == 1 else ["MRG"])
                        kc0, kn = BR[br]
                        linear(w_brs[br][l], 0, kn * 128, [(op_ * 256, 256, [(0, 128), (128, 128)])], YBR, n, cons_br, kc0=kc0, wk="br%d%d" % (br, l))

                def cons_resid(col, w, pap, pres):
                    o = col // 128
                    S.op("dve", "tensor_tensor", X[:, o, 0:n], X[:, o, 0:n], pap, ALU.add, reads=[pres, "X"], writes=["X"])
                linear(w_out[l], 0, D, [(o2 * 256, 256, [(0, 128), (128, 128)]) for o2 in range(8)], MRG, n, cons_resid, wk="out%d" % l)
                if stage < 3:
                    continue
                rmsnorm(PC[:, l, PO["g2"]:PO["g2"] + 16], n, HN, "HN")
                fin_ = o_fconv["p"][l] if kind == "p" else st_fconv[l]
                svf = STT[:, 0:88 * nb * 2].rearrange("p (c b h) -> p c b h", c=88, b=nb)
                if first:
                    S.op("dve", "memset", STT[:, 0:88 * nb * 2], 0.0, writes=["STT"])
                else:
                    S.dma("act", out=svf, in_=fin_, reads=["o_fconv"], writes=["STT"])
                for j in range(11):
                    def after_ffn(bi, w, ev, eres, j=j):
                        ch = (j * 4 + bi) if bi < 4 else (44 + j * 4 + (bi - 4))
                        cw = PC[:, l, PO["fcw"]:PO["fcw"] + 264]
                        cb = PC[:, l, PO["fcb"] + ch:PO["fcb"] + ch + 1]
                        t = GT[bi % 6] if False else (TMP[0] if bi < 4 else TMP[1])
                        tn = "TMP0" if bi < 4 else "TMP1"
                        tv = t[:, 0:n].rearrange("p (b t) -> p b t", b=nb)
                        S.op("dve", "tensor_scalar", tv, ev[:, :, 0:L], cw[:, ch:ch + 1], cb, ALU.mult, ALU.add,
                             reads=[eres, "PC"], writes=[tn])
                        for wi in (1, 2):
                            S.op("dve", "scalar_tensor_tensor", tv, ev[:, :, wi:wi + L], cw[:, wi * 88 + ch:wi * 88 + ch + 1], tv,
                                 ALU.mult, ALU.add, reads=[eres, "PC", tn], writes=[tn])
                        if bi < 4:
                            S.op("act", "activation", GT[bi][:, 0:n], t[:, 0:n], AF.Silu, reads=[tn], writes=["GT%d" % bi])
                        else:
                            S.op("dve", "tensor_tensor", AJ[:, bi - 4, 0:n], GT[bi - 4][:, 0:n], t[:, 0:n], ALU.mult,
                                 reads=[tn, "GT%d" % (bi - 4)], writes=["AJ"])
                    blocks = [(j * 512 + i * 128, 128) for i in range(4)] + [(DFF + j * 512 + i * 128, 128) for i in range(4)]
                    halo_proj(w_up[l], 0, blocks[:4], 2, None, None, "o_fconv", after_ffn, nbuf=8, sblk=(j * 4, 88), wk="up%d" % l)
                    halo_proj(w_up[l], 0, blocks[4:], 2, None, None, "o_fconv",
                              lambda bi, w, ev, eres: after_ffn(bi + 4, w, ev, eres), nbuf=8, sblk=(44 + j * 4, 88), wk="up%d" % l)
                    for half in range(2):
                        linear(w_down[l], j * 512, 512, [(half * 1024, 1024, [(i * 128, 128) for i in range(8)])], AJ, n, cons_resid, aname="AJ", wk="dn%d" % l)
                S.dma("act", out=o_fconv[kind][l], in_=svf, reads=["STT"], writes=["o_fconv"])
            final_out(kind, ti, n)
        S.final_wait("sp")
        block = es.enter_context(nc.Block())

        @block.sync
        def _(e):
            for f in S.prog["sp"]:
                f(e)

        @block.tensor
        def _(e):
            for f in S.prog["pe"]:
                f(e)

        @block.scalar
        def _(e):
            for f in S.prog["act"]:
                f(e)

        @block.vector
        def _(e):
            for f in S.prog["dve"]:
                f(e)

        @block.gpsimd
        def _(e):
            for f in S.prog["pool"]:
                f(e)
    return nc, S


PO = {}
_o = 0
for _name, _w in [("g1", 16), ("g2", 16), ("gf", 16), ("mu", 16), ("fcw", 264), ("fcb", 88), ("lre", 16), ("lim", 16), ("lst", 16), ("s5d", 4), ("gcw", 96), ("gng", 1), ("nw0", 4), ("a0", 4), ("kk", 4), ("ka", 4), ("rk", 4), ("lnw", 4), ("lnb", 4)]:
    PO[_name] = _o
    _o += _w
NPC = _o
CI = {"ident": 0, "ones": 1, "iota_p": 2, "reset_p": 6, "iota_s": 10, "reset_s": 11, "rowmask": 12, "headblk": 13}
NCST = 14


def _rw_blockcols(v):
    out = np.zeros((128, 16), np.float32)
    for i, (o, w) in enumerate(RW_BLOCKS):
        out[:w, i] = v[o:o + w]
    return out


def host_inputs(inputs, core):
    p = core // 2
    sl = slice(NSEQ_S * core, NSEQ_S * (core + 1))
    m = {}
    m["xp"] = np.ascontiguousarray(inputs["x_prompt"][p])
    m["xs"] = np.ascontiguousarray(inputs["x_sample"][sl].reshape(128, D))
    m["w_in"] = inputs["w_in"]
    G, P, GS = 32, 64, 16
    bB = np.zeros((DEPTH, 2, 512, 2048), np.float32)
    bC = np.zeros((DEPTH, 2, 2048, 512), np.float32)
    for g in range(G):
        for ri, (bk, ck) in enumerate((("s5_b_re", "s5_c_re"), ("s5_b_im", "s5_c_im"))):
            bB[:, ri, g * GS:(g + 1) * GS, g * P:(g + 1) * P] = inputs[bk][:, g].transpose(0, 2, 1)
            bC[:, ri, g * P:(g + 1) * P, g * GS:(g + 1) * GS] = inputs[ck][:, g].transpose(0, 2, 1)
    m["s5B"], m["s5C"] = bB, bC
    s5 = np.stack([inputs["state_s5_re"][:, sl], inputs["state_s5_im"][:, sl]], axis=2)
    m["st_s5"] = np.ascontiguousarray(s5.reshape(DEPTH, NSEQ_S, 2, 16, 128).transpose(0, 4, 2, 3, 1))
    m["s5_w_glu"] = inputs["s5_w_glu"]
    for k in ("w_br_s5", "w_br_rwkv", "w_br_gdn", "w_out", "ffn_w_up", "ffn_w_down"):
        m[k] = inputs[k]
    fc = inputs["state_ffn_conv"][:, sl]
    m["st_fconv"] = np.ascontiguousarray(fc.reshape(DEPTH, NSEQ_S, 2, 88, 128).transpose(0, 4, 3, 1, 2))
    pc = np.zeros((DEPTH, 128, NPC), np.float32)
    for l in range(DEPTH):
        pc[l, :, PO["g1"]:PO["g1"] + 16] = _col(inputs["norm1_g"][l])
        pc[l, :, PO["g2"]:PO["g2"] + 16] = _col(inputs["norm2_g"][l])
        pc[l, :, PO["gf"]:PO["gf"] + 16] = _col(inputs["final_norm_g"])
        pc[l, :, PO["mu"]:PO["mu"] + 16] = _rw_blockcols(inputs["rwkv_mu"][l])
        pc[l, :, PO["fcw"]:PO["fcw"] + 264] = _col(inputs["ffn_conv_w"][l]).transpose(1, 0, 2).reshape(128, 264)
        pc[l, :, PO["fcb"]:PO["fcb"] + 88] = _col(inputs["ffn_conv_b"][l])
        pc[l, :, PO["lre"]:PO["lre"] + 16] = _col(inputs["s5_lambda_re"][l].reshape(-1))
        pc[l, :, PO["lim"]:PO["lim"] + 16] = _col(inputs["s5_lambda_im"][l].reshape(-1))
        pc[l, :, PO["lst"]:PO["lst"] + 16] = _col(np.repeat(inputs["s5_log_step"][l], 64))
        pc[l, :, PO["s5d"]:PO["s5d"] + 4] = _col(inputs["s5_d"][l])
        pc[l, :, PO["gcw"]:PO["gcw"] + 96] = _col(inputs["gdn_conv_w"][l]).transpose(1, 0, 2).reshape(128, 96)
        pc[l, :, PO["gng"]] = inputs["gdn_norm_g"][l]
        pc[l, :, PO["nw0"]:PO["nw0"] + 4] = _col(inputs["rwkv_w0"][l])
        pc[l, :, PO["a0"]:PO["a0"] + 4] = _col(inputs["rwkv_a0"][l])
        pc[l, :, PO["kk"]:PO["kk"] + 4] = _col(inputs["rwkv_k_k"][l])
        pc[l, :, PO["ka"]:PO["ka"] + 4] = _col(inputs["rwkv_k_a"][l])
        pc[l, :, PO["rk"]:PO["rk"] + 4] = _col(inputs["rwkv_r_k"][l].reshape(-1))
        pc[l, :, PO["lnw"]:PO["lnw"] + 4] = _col(inputs["rwkv_ln_w"][l])
        pc[l, :, PO["lnb"]:PO["lnb"] + 4] = _col(inputs["rwkv_ln_b"][l])
    m["pcol"] = pc
    cs = np.zeros((NCST, 128, 128), np.float32)
    cs[CI["ident"]] = np.eye(128)
    cs[CI["ones"]] = 1.0
    cs[CI["iota_p"]:CI["iota_p"] + 4] = (np.arange(512, dtype=np.float32) + 1).reshape(4, 1, 128)
    rp = np.ones(512, np.float32); rp[0] = 0
    cs[CI["reset_p"]:CI["reset_p"] + 4] = rp.reshape(4, 1, 128)
    cs[CI["rowmask"]][:, :NSEQ_S] = (np.arange(128)[:, None] // LS == np.arange(NSEQ_S)[None, :])
    hb = np.arange(128) // 64
    cs[CI["headblk"]] = (hb[:, None] == hb[None, :])
    cs[CI["iota_s"]] = (np.arange(128) % LS + 1).astype(np.float32)[None, :]
    cs[CI["reset_s"]] = (np.arange(128) % LS != 0).astype(np.float32)[None, :]
    m["cst"] = cs
    ck = np.zeros((2, 6, 128, 128), np.float32)
    for ki, kd in enumerate(("p", "s")):
        mm = _masks(kd)
        for j, nm in enumerate(("MUi", "MUs", "MLs", "NEGU", "POSL")):
            ck[ki, j] = mm[nm]
        idx = np.arange(128) // LS if kd == "s" else np.zeros(128, np.int64)
        ck[ki, 5] = (idx[:, None] == idx[None, :])
    m["cstk"] = ck
    pr = np.zeros((DEPTH, 128, 16), np.float32)
    pr[:, :, 0:8] = inputs["gdn_a_log"][:, None, :]
    pr[:, :, 8:16] = inputs["gdn_dt_bias"][:, None, :]
    m["prow"] = pr
    for k_ in ("rwkv_w2", "rwkv_a2", "rwkv_g2"):
        m[k_] = inputs[k_]
    wk = inputs["state_rwkv_wkv"][:, sl]
    m["st_wkv"] = np.ascontiguousarray(wk.transpose(0, 2, 4, 1, 3).reshape(DEPTH, 4, 128, NSEQ_S, 64))
    m["st_gdn"] = np.ascontiguousarray(inputs["state_gdn"][:, sl].transpose(0, 2, 3, 1, 4))
    sh = inputs["state_rwkv_shift"][:, sl]
    t = np.zeros((DEPTH, 128, 16, NSEQ_S, 1), np.float32)
    for i, (o, w) in enumerate(RW_BLOCKS):
        t[:, :w, i, :, 0] = sh[:, :, o:o + w].transpose(0, 2, 1)
    m["st_shift"] = t
    gc = inputs["state_gdn_conv"][:, sl]
    m["st_gconv"] = np.ascontiguousarray(gc.reshape(DEPTH, NSEQ_S, 3, 24, 128).transpose(0, 4, 3, 1, 2))
    return m


_CACHE = {}


def _unblock_shift(a):
    nb = a.shape[3]
    out = np.zeros((DEPTH, nb, 1984), np.float32)
    for i, (o, w) in enumerate(RW_BLOCKS):
        out[:, :, o:o + w] = a[:, :w, i, :, 0].transpose(0, 2, 1)
    return out


def _unchunk(a):
    l, p, c, nb, h = a.shape
    return np.ascontiguousarray(a.transpose(0, 3, 4, 2, 1).reshape(l, nb, h, c * p))


def _uns5(a, ri):
    nb = a.shape[4]
    return np.ascontiguousarray(a[:, :, ri].transpose(0, 3, 2, 1).reshape(DEPTH, nb, 32, 64))


def _unwkv(a):
    nb = a.shape[3]
    return np.ascontiguousarray(a.reshape(DEPTH, 4, 2, 64, nb, 64).transpose(0, 4, 1, 2, 5, 3).reshape(DEPTH, nb, 8, 64, 64))


def _ungdn(a):
    return np.ascontiguousarray(a.transpose(0, 3, 1, 2, 4))


def kernel(**inputs):
    inputs = {k: np.asarray(v) for k, v in inputs.items()}
    if "nc" not in _CACHE:
        _CACHE["nc"] = build_program()
    nc, S = _CACHE["nc"]
    in_maps = [host_inputs(inputs, c) for c in range(8)]
    res = run_bass_kernel_spmd(nc, in_maps, core_ids=list(range(8))).results
    B = 4
    P = [res[2 * p] for p in range(B)]
    cat = lambda xs: np.concatenate(xs, axis=1)
    y_p = np.stack([r["y_p"] for r in P])
    y_s = np.concatenate([r["y_s"].reshape(NSEQ_S, LS, D) for r in res])
    outs = [y_p, y_s]
    for grp, key in ((P, "p"), (res, "s")):
        outs += [cat([_uns5(r["o_s5_" + key], 0) for r in grp]),
                 cat([_uns5(r["o_s5_" + key], 1) for r in grp]),
                 cat([_unblock_shift(r["o_shift_" + key]) for r in grp]),
                 cat([_unwkv(r["o_wkv_" + key]) for r in grp]),
                 cat([_unchunk(r["o_gconv_" + key]) for r in grp]),
                 cat([_ungdn(r["o_gdn_" + key]) for r in grp]),
                 cat([_unchunk(r["o_fconv_" + key]) for r in grp])]
    return tuple(outs)
```

```python
import numpy as np
from contextlib import ExitStack
import concourse.bass as bass
import concourse.mybir as mybir
from concourse.bass_utils import run_bass_kernel_spmd
from concourse.alu_op_type import AluOpType as ALU

AF = mybir.ActivationFunctionType
F32 = mybir.dt.float32
BF16 = mybir.dt.bfloat16
I32 = mybir.dt.int32
F32R = mybir.dt.float32r

D = 2048
DEPTH = 2
SEQ = 2048
NSEQ_S = 16
LS = 8
IN_COLS = 12752
DFF = 5632
C_S5, C_RW, C_QKV, C_Z, C_B, C_A, C_G = 0, 512, 2496, 5568, 6592, 6600, 6608
RW_BLOCKS = [(i * 128, 128) for i in range(12)] + [(1536, 96), (1632, 96), (1728, 128), (1856, 128)]
BIG = 30000.0
TWO_PI = 6.283185307179586
PI = 3.141592653589793


class Sched:
    ENGS = ("pe", "act", "dve", "pool", "sp")
    ND = 24
    EPOCH = 30000
    NEP = 6

    def __init__(self):
        self.prog = {e: [] for e in self.ENGS}
        self.cnt = {e: 0 for e in self.ENGS}
        self.waited = {e: {} for e in self.ENGS}
        self.lastw = {}
        self.readers = {}
        self.dma_cnt = [0] * self.ND
        self.dma_rr = 0
        self.semh = None
        self.n_instr = 0
        self.pending = {e: {} for e in self.ENGS}
        self.log = None
        self.sfx = ""
        self.sfx2 = ""
        self.private2 = set()
        self._stack = []
        self.f32r = False
        self.alias = {}
        self.private = set()
        self._rec = None

    def _deps(self, eng, reads, writes, px=()):
        need = {}

        def add(tok):
            sk, v = tok
            if eng == "pe" and sk[0] == "pe":
                return
            if v > need.get(sk, 0):
                need[sk] = v
        if self.pending[eng]:
            for sk, v in self.pending[eng].items():
                add((sk, v))
            self.pending[eng] = {}
        for r in tuple(reads) + tuple(writes):
            if r in self.lastw:
                add(self.lastw[r])
        for r in writes:
            for sk, v in self.readers.get(r, {}).items():
                add((sk, v))
        for r in px:
            for sk, v in self.readers.get(r, {}).items():
                if sk[0] != eng:
                    add((sk, v))
        out = []
        for sk, v in need.items():
            if v > self.waited[eng].get(sk, 0):
                self.waited[eng][sk] = v
                out.append((sk, v))
        return out

    def barrier(self):
        snap = {}
        for e in ("pe", "act", "dve", "pool"):
            if self.cnt[e]:
                sk, v = self._tok(e, self.cnt[e])
                snap[sk] = v
        for i in range(self.ND):
            if self.dma_cnt[i]:
                snap[("dma", i)] = self.dma_cnt[i]
        for e in self.ENGS:
            self.pending[e] = dict(snap)

    def _tok(self, eng, c):
        return ((eng, (c - 1) // self.EPOCH), (c - 1) % self.EPOCH + 1)

    def _commit(self, tok, reads, writes):
        for r in writes:
            self.lastw[r] = tok
            self.readers[r] = {}
        for r in reads:
            if r not in writes:
                d = self.readers.setdefault(r, {})
                if tok[1] > d.get(tok[0], 0):
                    d[tok[0]] = tok[1]

    def _nm(self, names):
        if not self.sfx and not self.sfx2:
            return list(names)
        al = self.alias
        out = []
        for r in names:
            if r in self.private:
                out.append(al.get(r, r) + self.sfx2 + self.sfx)
            elif r in self.private2:
                out.append(r + self.sfx2)
            else:
                out.append(r)
        return out

    def thread_begin(self):
        self._stack.append(self._rec)
        self._rec = []

    def thread_end(self):
        r = self._rec
        self._rec = self._stack.pop()
        return r

    def replay(self, threads):
        idx = [0] * len(threads)
        sfx, self.sfx = self.sfx, ""
        sfx2, self.sfx2 = self.sfx2, ""
        while any(idx[i] < len(t) for i, t in enumerate(threads)):
            for i, t in enumerate(threads):
                if idx[i] < len(t):
                    kind, a, kw = t[idx[i]]
                    idx[i] += 1
                    (self.op if kind == "op" else self.dma)(*a, **kw)
        self.sfx = sfx
        self.sfx2 = sfx2

    def op(self, eng, meth, *args, reads=(), writes=(), **kw):
        reads, writes = self._nm(reads), self._nm(writes)
        if meth == "matmul" and self.f32r and args[1].dtype == F32 and args[2].dtype == F32 \
                and args[0].base_partition() == 0 and args[2].shape[-1] % 2 == 0:
            args = (args[0], args[1].bitcast(F32R), args[2].bitcast(F32R)) + tuple(args[3:])
        if self._rec is not None:
            self._rec.append(("op", (eng, meth) + tuple(args), dict(reads=reads, writes=writes, **kw)))
            return None
        fn = lambda e: getattr(e, meth)(*args, **kw)
        px = [r for r in reads if r.startswith("PS")]
        waits = self._deps(eng, reads, writes, px)
        self.cnt[eng] += 1
        tok = self._tok(eng, self.cnt[eng])
        self.n_instr += 1

        def run(e, waits=waits, fn=fn, mysem=tok[0]):
            for sk, wv in waits:
                e.wait_ge(self.semh[sk], wv)
            fn(e).then_inc(self.semh[mysem], 1)
        self.prog[eng].append(run)
        self._commit(tok, reads, writes)
        if self.log is not None:
            self.log.append((eng, meth, tok, list(waits), list(reads), list(writes)))
        return tok

    def dma(self, queue, out, in_, reads=(), writes=()):
        reads, writes = self._nm(reads), self._nm(writes)
        if self._rec is not None:
            self._rec.append(("dma", (queue,), dict(out=out, in_=in_, reads=reads, writes=writes)))
            return None
        fn = lambda e: e.dma_start(out=out, in_=in_)
        waits = self._deps(queue, reads, writes)
        i = self.dma_rr
        self.dma_rr = (i + 1) % self.ND
        sk = ("dma", i)
        prev = self.dma_cnt[i]
        if prev > self.waited[queue].get(sk, 0):
            self.waited[queue][sk] = prev
            waits = waits + [(sk, prev)]
        self.dma_cnt[i] += 16
        tok = (sk, self.dma_cnt[i])
        self.n_instr += 1

        def run(e, waits=waits, fn=fn, sk=sk):
            for s2, wv in waits:
                e.wait_ge(self.semh[s2], wv)
            fn(e).then_inc(self.semh[sk], 16)
        self.prog[queue].append(run)
        self._commit(tok, reads, writes)
        if self.log is not None:
            self.log.append((queue, "dma", tok, list(waits), list(reads), list(writes)))
        return tok

    def final_wait(self, queue):
        sems = dict(self.semh)

        def run(e):
            for i in range(self.ND):
                if self.dma_cnt[i]:
                    e.wait_ge(sems[("dma", i)], self.dma_cnt[i])
            for g in ("pe", "act", "dve", "pool"):
                if self.cnt[g]:
                    sk, v = self._tok(g, self.cnt[g])
                    e.wait_ge(sems[sk], v)
        self.prog[queue].append(run)


def _col(v, width=128):
    v = np.asarray(v, np.float32)
    n = v.shape[-1] // width
    return np.ascontiguousarray(v.reshape(v.shape[:-1] + (n, width)).swapaxes(-1, -2))


def _masks(kind):
    idx = np.arange(128)
    seq = idx // 8 if kind == "s" else np.zeros(128, np.int64)
    same = (seq[:, None] == seq[None, :])
    p, f = idx[:, None], idx[None, :]
    m = {}
    m["MUi"] = (same & (p <= f)).astype(np.float32)
    m["MUs"] = (same & (p < f)).astype(np.float32)
    m["MLs"] = (same & (p > f)).astype(np.float32)
    m["NEGU"] = np.where(same & (p <= f), 0.0, -BIG).astype(np.float32)
    m["POSL"] = np.where(same & (p >= f), 0.0, BIG).astype(np.float32)
    return m


class Ctx:
    pass


def build_program(stage=99):
    nc = bass.Bass("TRN2", target_bir_lowering=False)
    S = Sched()
    import os as _os0
    if _os0.environ.get("MK_LOG"):
        S.log = []
    S.f32r = _os0.environ.get("MK_F32R", "0") == "1"
    es = ExitStack()
    K = Ctx()

    def din(name, shape):
        return nc.dram_tensor(name, list(shape), F32, kind="ExternalInput").ap()

    def dout(name, shape):
        return nc.dram_tensor(name, list(shape), F32, kind="ExternalOutput").ap()

    xp = din("xp", [SEQ, D]); xs = din("xs", [128, D])
    w_in = din("w_in", [DEPTH, D, IN_COLS])
    w_brs = [din("w_br_s5", [DEPTH, 512, D]), din("w_br_rwkv", [DEPTH, 512, D]), din("w_br_gdn", [DEPTH, 1024, D])]
    w_out = din("w_out", [DEPTH, D, D])
    w_glu = din("s5_w_glu", [DEPTH, 512, 512])
    cstk = din("cstk", [2, 6, 128, 128])
    NWC = 520
    WCT = [nc.dram_tensor("wcache%d" % i_, [130, 128, 4096], BF16, kind="Internal").ap() for i_ in range(4)]

    class _WC:
        def __getitem__(self, key):
            sl_i = key[0]
            return WCT[sl_i // 130][(sl_i % 130,) + tuple(key[1:])]
    WC = _WC()
    wc_slots = {}
    w_lora = [din("rwkv_w2", [DEPTH, 96, 512]), din("rwkv_a2", [DEPTH, 96, 512]), din("rwkv_g2", [DEPTH, 256, 512])]
    st_wkv = din("st_wkv", [DEPTH, 4, 128, NSEQ_S, 64])
    o_wkv = {"p": dout("o_wkv_p", [DEPTH, 4, 128, 1, 64]), "s": dout("o_wkv_s", [DEPTH, 4, 128, NSEQ_S, 64])}
    prow = din("prow", [DEPTH, 128, 16])
    st_gdn = din("st_gdn", [DEPTH, 8, 128, NSEQ_S, 128])
    o_gdn = {"p": dout("o_gdn_p", [DEPTH, 8, 128, 1, 128]), "s": dout("o_gdn_s", [DEPTH, 8, 128, NSEQ_S, 128])}
    s5B = din("s5B", [DEPTH, 2, 512, 2048])
    s5C = din("s5C", [DEPTH, 2, 2048, 512])
    st_s5 = din("st_s5", [DEPTH, 128, 2, 16, NSEQ_S])
    o_s5 = {"p": dout("o_s5_p", [DEPTH, 128, 2, 16, 1]), "s": dout("o_s5_s", [DEPTH, 128, 2, 16, NSEQ_S])}
    w_up = din("ffn_w_up", [DEPTH, D, 2 * DFF])
    w_down = din("ffn_w_down", [DEPTH, DFF, D])
    st_fconv = din("st_fconv", [DEPTH, 128, 88, NSEQ_S, 2])
    o_fconv = {"p": dout("o_fconv_p", [DEPTH, 128, 88, 1, 2]), "s": dout("o_fconv_s", [DEPTH, 128, 88, NSEQ_S, 2])}
    pcol = din("pcol", [DEPTH, 128, NPC])
    cst = din("cst", [NCST, 128, 128])
    st_shift = din("st_shift", [DEPTH, 128, 16, NSEQ_S, 1])
    st_gconv = din("st_gconv", [DEPTH, 128, 24, NSEQ_S, 3])
    o_y = {"p": dout("y_p", [SEQ, D]), "s": dout("y_s", [128, D])}
    o_shift = {"p": dout("o_shift_p", [DEPTH, 128, 16, 1, 1]), "s": dout("o_shift_s", [DEPTH, 128, 16, NSEQ_S, 1])}
    o_gconv = {"p": dout("o_gconv_p", [DEPTH, 128, 24, 1, 3]), "s": dout("o_gconv_s", [DEPTH, 128, 24, NSEQ_S, 3])}

    with es:
        def sb(name, shape, dt=F32):
            return es.enter_context(nc.sbuf_tensor(name, list(shape), dt))

        def ps(name):
            return es.enter_context(nc.psum_tensor(name, [128, 512], F32))

        X = sb("X", [128, 16, 512])
        HN = sb("HN", [128, 16, 512], BF16)
        WS = [sb("WS%d" % i, [128, 4096]) for i in range(2)]
        WB = [sb("WB%d" % i, [128, 4096], BF16) for i in range(2)]
        PC = sb("PC", [128, DEPTH, NPC])
        CS = sb("CS", [128, NCST, 128])
        NA = 14144
        AR = sb("AR", [128, NA])

        class View:
            def __init__(self, off, size, name):
                self.off, self.size, self.name = off, size, name

            def __getitem__(self, key):
                return AR[:, self.off:self.off + self.size][key]
        EXT = [View(i * 520, 520, "EXT%d" % i) for i in range(8)]
        STT = View(4160, 2816, "STT")
        GT = [View(6976 + i * 512, 512, "GT%d" % i) for i in range(6)]
        ACC = [View(10048 + i * 512, 512, "ACC%d" % i) for i in range(2)]
        TMP = [View(11072 + i * 512, 512, "TMP%d" % i) for i in range(2)]
        XT = View(12096, 2048, "XT")
        SQ = [View(10048 + i * 512, 512, "SQ%d" % i) for i in range(2)]
        RSTD = View(11072, 512, "RSTD")
        YBR = sb("YBR", [128, 16, 512], BF16)
        MRG = sb("MRG", [128, 16, 512], BF16)
        AJ = MRG[:, 12:16, :]
        TI = sb("TI", [128, 512], I32)
        CK = sb("CK", [128, 6, 128])
        WBA = sb("WBA", [128, 16, 16], BF16)
        PR = sb("PR", [128, DEPTH, 16])
        S5ST = sb("S5ST", [128, 2 * 16 * NSEQ_S])
        S5P = sb("S5P", [128, DEPTH, 6, 16])
        SB5 = [sb("SB5_%d" % i, [128, 2, 128]) for i in range(2)]
        PSD = [ps("PSD0"), ps("PSD1")]
        PSM = [ps("PSM%d" % i) for i in range(6)]
        semh = {}
        for e in ("pe", "act", "dve", "pool"):
            for ep in range(S.NEP if e != "pool" else 1):
                semh[(e, ep)] = es.enter_context(nc.semaphore("s_%s%d" % (e, ep)))
        for i in range(S.ND):
            semh[("dma", i)] = es.enter_context(nc.semaphore("s_dma%d" % i))
        S.semh = semh
        ident = CS[:, CI["ident"], :]
        ones = CS[:, CI["ones"], :]

        S.dma("sp", out=PC[:], in_=pcol.rearrange("l p c -> p l c"), writes=["PC"])
        S.dma("sp", out=CS[:], in_=cst.rearrange("k p f -> p k f"), writes=["CS"])
        S.dma("sp", out=PR[:], in_=prow.rearrange("l p c -> p l c"), writes=["PR"])


        def wrap_sin(dst, src, n_, tA, tB, eng="dve", srcn=(), dstn=()):
            S.op(eng, "tensor_scalar", tA, src, 1.0 / TWO_PI, None, ALU.mult, reads=list(srcn) + ["TA", "TB"], writes=["wrapA", "TA"])
            S.op(eng, "tensor_copy", TI[:, 0:n_], tA, reads=["wrapA"], writes=["TI"])
            S.op(eng, "tensor_copy", tA, TI[:, 0:n_], reads=["TI"], writes=["wrapA"])
            S.op(eng, "scalar_tensor_tensor", tB, tA, -TWO_PI, src, ALU.mult, ALU.add, reads=["wrapA"] + list(srcn), writes=["wrapB", "TB"])
            S.op(eng, "tensor_scalar", tB, tB, -PI, PI, ALU.max, ALU.min, reads=["wrapB"], writes=["wrapB"])
            S.op("act", "activation", dst, tB, AF.Sin, reads=["wrapB"], writes=["wrapD"] + list(dstn))

        S.barrier()
        for l in range(DEPTH):
            lre = PC[:, l, PO["lre"]:PO["lre"] + 16]
            lim = PC[:, l, PO["lim"]:PO["lim"] + 16]
            lst = PC[:, l, PO["lst"]:PO["lst"] + 16]
            t = [AR[:, i * 16:(i + 1) * 16] for i in range(12)]
            rho, th, cr, ci, ncr = (S5P[:, l, i, :] for i in range(5))
            S.op("act", "activation", t[0], lst, AF.Exp, reads=["PC"], writes=["p5"])
            S.op("dve", "tensor_tensor", t[1], lre, t[0], ALU.mult, reads=["p5", "PC"], writes=["p5"])
            S.op("act", "activation", rho, t[1], AF.Exp, reads=["p5"], writes=["S5P"])
            S.op("dve", "tensor_tensor", th, lim, t[0], ALU.mult, reads=["p5", "PC"], writes=["S5P"])
            S.barrier()
            wrap_sin(t[2], th, 16, t[8], t[9])
            S.op("dve", "tensor_scalar", t[3], th, PI / 2, None, ALU.add, reads=["S5P"], writes=["p5"])
            S.barrier()
            wrap_sin(t[4], t[3], 16, t[8], t[9])
            S.barrier()
            S.op("dve", "tensor_tensor", t[5], rho, t[4], ALU.mult, writes=["p5"])
            S.op("dve", "tensor_scalar", t[5], t[5], -1.0, None, ALU.add, writes=["p5"])
            S.op("dve", "tensor_tensor", t[6], rho, t[2], ALU.mult, writes=["p5"])
            S.op("dve", "tensor_tensor", t[7], lre, lre, ALU.mult, writes=["p5"])
            S.op("dve", "tensor_tensor", t[10], lim, lim, ALU.mult, writes=["p5"])
            S.op("dve", "tensor_tensor", t[7], t[7], t[10], ALU.add, writes=["p5"])
            S.op("dve", "reciprocal", t[7], t[7], writes=["p5"])
            S.op("dve", "tensor_tensor", t[10], t[5], lre, ALU.mult, writes=["p5"])
            S.op("dve", "tensor_tensor", t[11], t[6], lim, ALU.mult, writes=["p5"])
            S.op("dve", "tensor_tensor", t[10], t[10], t[11], ALU.add, writes=["p5"])
            S.op("dve", "tensor_tensor", cr, t[10], t[7], ALU.mult, writes=["p5", "S5P"])
            S.op("dve", "tensor_tensor", t[10], t[6], lre, ALU.mult, writes=["p5"])
            S.op("dve", "tensor_tensor", t[11], t[5], lim, ALU.mult, writes=["p5"])
            S.op("dve", "tensor_tensor", t[10], t[10], t[11], ALU.subtract, writes=["p5"])
            S.op("dve", "tensor_tensor", ci, t[10], t[7], ALU.mult, writes=["p5", "S5P"])
            S.op("dve", "tensor_scalar", ncr, cr, -1.0, None, ALU.mult, writes=["p5", "S5P"])
            S.barrier()

        dense_rr = [0]
        w_rr = [0]
        ws_busy = [False]
        deep_pf = [False]
        wx_rr = [0]
        WBX = [(WB[0], "WB0"), (WB[1], "WB1")]
        for i_ in range(2):
            v_ = WS[i_][:, :].bitcast(BF16)
            WBX += [(v_[:, 0:4096], "WS%da" % i_), (v_[:, 4096:8192], "WS%db" % i_)]

        def linear(wd, row0, nrows, groups, act_t, n, consume, kc0=0, aname=None, wk=None):
            KC = max(1, nrows // 128)
            kp = min(nrows, 128)
            for (col0, ncols, subs) in groups:
                i = w_rr[0] % 2
                w_rr[0] += 1
                ws, wb = WS[i], WB[i]
                m_ = KC * ncols
                ckey = (wk, row0, col0, ncols) if wk is not None else None
                wbn = "WB%d" % i
                if ckey is not None and ckey in wc_slots:
                    sl_i = wc_slots[ckey]
                    if deep_pf[0] and not ws_busy[0]:
                        wb, wbn = WBX[wx_rr[0] % 6]
                        wx_rr[0] += 1
                    S.dma("sp", out=wb[0:kp, 0:m_], in_=WC[sl_i, 0:kp, 0:m_], reads=["WC%d" % sl_i], writes=[wbn])
                else:
                    assert not ws_busy[0], "uncached weight block while staging buffers host mixer slots"
                    src = wd[row0:row0 + nrows, col0:col0 + ncols].rearrange("(kc p) n -> p kc n", p=kp)
                    dst = ws[0:kp, 0:m_].rearrange("p (kc n) -> p kc n", kc=KC)
                    S.dma("sp", out=dst, in_=src, writes=["WS%d" % i])
                    assert not deep_pf[0]
                    S.op("act", "activation", wb[0:kp, 0:m_], ws[0:kp, 0:m_], AF.Copy,
                         reads=["WS%d" % i], writes=[wbn])
                    if ckey is not None and len(wc_slots) < NWC:
                        sl_i = len(wc_slots)
                        wc_slots[ckey] = sl_i
                        S.dma("act", out=WC[sl_i, 0:kp, 0:m_], in_=wb[0:kp, 0:m_], reads=[wbn], writes=["WC%d" % sl_i])
                wv = wb[:, 0:KC * ncols].rearrange("p (kc n) -> p kc n", kc=KC)
                for (off, w) in subs:
                    j = dense_rr[0] % 2
                    dense_rr[0] += 1
                    pt = PSD[j]
                    for kc in range(KC):
                        S.op("pe", "matmul",
                            pt[0:w, 0:n], wv[0:kp, kc, off:off + w], act_t[0:kp, kc0 + kc, 0:n], start=(kc == 0), stop=(kc == KC - 1),
                            reads=[wbn, aname or act_t.name], writes=["PSD%d" % j])
                    consume(col0 + off, w, pt[0:w, 0:n], "PSD%d" % j)

        def rmsnorm(gcol, n, dst, dstname):
            pn = PSM[0]
            S.barrier()
            for c in range(16):
                q = SQ[c % 2]
                S.op("act", "activation", q[:, 0:n], X[:, c, 0:n], AF.Square,
                     reads=["X"], writes=["SQ%d" % (c % 2)])
                S.op("pe", "matmul", pn[:, 0:n], ones, q[:, 0:n], start=(c == 0), stop=(c == 15),
                     reads=["SQ%d" % (c % 2), "CS"], writes=["PSM0"])
            S.op("act", "activation", SQ[0][:, 0:n], pn[:, 0:n], AF.Sqrt, bias=1e-6, scale=1.0 / D,
                 reads=["PSM0"], writes=["SQ0"])
            S.op("dve", "reciprocal", RSTD[:, 0:n], SQ[0][:, 0:n], reads=["SQ0"], writes=["RSTD"])
            for c in range(16):
                S.op("dve", "scalar_tensor_tensor", dst[:, c, 0:n], X[:, c, 0:n], gcol[:, c:c + 1],
                                                                   RSTD[:, 0:n], ALU.mult, ALU.mult,
                     reads=["X", "RSTD", "PC"], writes=[dstname])
            S.barrier()

        def final_out(kind, ti, n):
            pn = PSM[0]
            S.barrier()
            for c in range(16):
                q = SQ[c % 2]
                S.op("act", "activation", q[:, 0:n], X[:, c, 0:n], AF.Square, reads=["X"], writes=["SQ%d" % (c % 2)])
                S.op("pe", "matmul", pn[:, 0:n], ones, q[:, 0:n], start=(c == 0), stop=(c == 15),
                     reads=["SQ%d" % (c % 2), "CS"], writes=["PSM0"])
            S.op("act", "activation", SQ[0][:, 0:n], pn[:, 0:n], AF.Sqrt, bias=1e-6, scale=1.0 / D, reads=["PSM0"], writes=["SQ0"])
            S.op("dve", "reciprocal", RSTD[:, 0:n], SQ[0][:, 0:n], reads=["SQ0"], writes=["RSTD"])
            gf = PC[:, 0, PO["gf"]:PO["gf"] + 16]
            for c in range(16):
                S.op("dve", "scalar_tensor_tensor", X[:, c, 0:n], X[:, c, 0:n], gf[:, c:c + 1], RSTD[:, 0:n], ALU.mult, ALU.mult,
                     reads=["X", "RSTD", "PC"], writes=["X"])
            S.barrier()
            ydst = o_y[kind]
            for s_ in range(n // 128):
                for cg in range(4):
                    pt = PSM[1 + cg % 2]
                    for c4 in range(4):
                        c = cg * 4 + c4
                        S.op("pe", "transpose", pt[:, c4 * 128:(c4 + 1) * 128], X[:, c, s_ * 128:(s_ + 1) * 128], ident,
                             reads=["X", "CS"], writes=["PSM%d" % (1 + cg % 2)])
                    S.op("act", "activation", XT[:, cg * 512:(cg + 1) * 512], pt[:, :], AF.Copy,
                         reads=["PSM%d" % (1 + cg % 2)], writes=["XT"])
                r0 = ti * 512 + s_ * 128
                S.dma("act", out=ydst[r0:r0 + 128, :], in_=XT[:], reads=["XT"], writes=["o_y"])

        tiles = [("p", t) for t in range(4)] + [("s", 0)]
        import os
        if os.environ.get("MK_TILES"):
            tiles = [tiles[int(i)] for i in os.environ["MK_TILES"].split(",")]
        for (kind, ti) in tiles:
            n = 512 if kind == "p" else 128
            nb = 1 if kind == "p" else NSEQ_S
            L = n // nb
            xsrc = xp[ti * 512:(ti + 1) * 512, :] if kind == "p" else xs
            if kind == "s" or ti > 0:
                S.barrier()
                deep_pf[0] = True
            S.barrier()
            S.dma("sp", out=CK[:], in_=cstk[0 if kind == "p" else 1].rearrange("k p f -> p k f"), writes=["CK"])
            for s_ in range(n // 128):
                S.dma("sp", out=XT[:], in_=xsrc[s_ * 128:(s_ + 1) * 128, :], writes=["XT"])
                for cg in range(4):
                    pt = PSM[1 + cg % 2]
                    for c4 in range(4):
                        c = cg * 4 + c4
                        S.op("pe", "transpose", pt[:, c4 * 128:(c4 + 1) * 128],
                                                                           XT[:, c * 128:(c + 1) * 128], ident,
                             reads=["XT", "CS"], writes=["PSM%d" % (1 + cg % 2)])
                    S.op("dve", "tensor_copy",
                        X[:, cg * 4:cg * 4 + 4, s_ * 128:(s_ + 1) * 128], pt[:, :].rearrange("p (c t) -> p c t", c=4),
                        reads=["PSM%d" % (1 + cg % 2)], writes=["X"])
            for l in range(DEPTH):
                rmsnorm(PC[:, l, PO["g1"]:PO["g1"] + 16], n, HN, "HN")
                first = (kind == "p" and ti == 0)

                def halo_proj(wd, col_base, blocks, H, st_in, st_out, oname, after, nbuf=2, sblk=None, wk=None):
                    nblk = len(blocks) if sblk is None else sblk[1]
                    sidx = list(range(len(blocks))) if sblk is None else (
                        sblk[0] if isinstance(sblk[0], (list, tuple)) else [sblk[0] + i_ for i_ in range(len(blocks))])
                    sv = STT[:, 0:nblk * nb * H].rearrange("p (c b h) -> p c b h", c=nblk, b=nb)
                    if st_in is None:
                        pass
                    elif first:
                        S.op("dve", "memset", STT[:, 0:nblk * nb * H], 0.0, writes=["STT"])
                    else:
                        S.dma("act", out=sv, in_=st_in, reads=[oname], writes=["STT"])
                    blk_idx = {col_base + o: i for i, (o, w) in enumerate(blocks)}
                    nbl = len(blocks)

                    def consume(col, w, pap, pres):
                        bi = blk_idx[col]
                        ex = EXT[bi % nbuf]
                        ev = ex[:, 0:nb * (H + L)].rearrange("p (b t) -> p b t", b=nb)
                        S.op("dve", "tensor_copy", ev[0:w, :, 0:H], sv[0:w, sidx[bi], :, :],
                             reads=["STT"], writes=["EXT%d" % (bi % nbuf)])
                        S.op("act", "activation", ev[0:w, :, H:H + L], pap.rearrange("p (b t) -> p b t", b=nb), AF.Copy,
                             reads=[pres], writes=["EXT%d" % (bi % nbuf)])
                        S.op("dve", "tensor_copy", sv[0:w, sidx[bi], :, :], ev[0:w, :, L:L + H],
                             reads=["EXT%d" % (bi % nbuf)], writes=["STT"])
                        after(bi, w, ev, "EXT%d" % (bi % nbuf))
                    groups = []
                    i = 0
                    while i < nbl:
                        o0, w0 = blocks[i]
                        if i + 1 < nbl and blocks[i + 1][0] == o0 + w0:
                            o1, w1 = blocks[i + 1]
                            groups.append((col_base + o0, w0 + w1, [(0, w0), (w0, w1)]))
                            i += 2
                        else:
                            groups.append((col_base + o0, w0, [(0, w0)]))
                            i += 1
                    linear(wd, 0, D, groups, HN, n, consume, wk=wk)
                    if st_out is not None:
                        S.dma("act", out=st_out, in_=sv, reads=["STT"], writes=[oname])


                S.barrier()
                U = AR[:, 0:2048].rearrange("p (c t) -> p c t", c=4)
                T5 = [AR[:, 2048 + i * 512:2048 + (i + 1) * 512] for i in range(12)]
                Y1F = AR[:, 8192:10240].rearrange("p (c t) -> p c t", c=4)
                iota = CS[:, CI["iota_p"]:CI["iota_p"] + 4, :] if kind == "p" else CS[:, CI["iota_s"]:CI["iota_s"] + 1, :]
                rset = CS[:, CI["reset_p"]:CI["reset_p"] + 4, :] if kind == "p" else CS[:, CI["reset_s"]:CI["reset_s"] + 1, :]
                iota = iota.rearrange("p c t -> p (c t)")
                rset = rset.rearrange("p c t -> p (c t)")

                def cons_u(col, w, pap, pres):
                    S.op("act", "activation", U[:, col // 128, 0:n], pap, AF.Copy, reads=[pres], writes=["U"])
                linear(w_in[l], 0, D, [(0, 256, [(0, 128), (128, 128)]), (256, 256, [(0, 128), (128, 128)])], HN, n, cons_u, wk="in%d" % l)
                s5v = S5ST[:, 0:2 * 16 * nb].rearrange("p (r c b) -> p r c b", r=2, c=16)
                if first:
                    S.op("dve", "memset", S5ST[:, 0:2 * 16 * nb], 0.0, writes=["S5ST"])
                else:
                    S.dma("act", out=s5v, in_=(o_s5["p"][l] if kind == "p" else st_s5[l]), reads=["o_s5"], writes=["S5ST"])
                rho, th, cr, ci, ncr = (S5P[:, l, i, :] for i in range(5))
                psy = [PSM[4], PSM[5]]
                v3 = lambda ap: ap.rearrange("p (b t) -> p b t", b=nb)
                for sc in range(16):
                    kc, oc = sc // 4, sc // 4
                    sbt = SB5[sc % 2]
                    S.dma("sp", out=sbt[:], in_=s5B[l, :, kc * 128:(kc + 1) * 128, sc * 128:(sc + 1) * 128].rearrange("r p m -> p r m"),
                          writes=["SB5_%d" % (sc % 2)])
                    pb = [PSM[2], PSM[3]]
                    for r in range(2):
                        S.op("pe", "matmul", pb[r][:, 0:n], sbt[:, r, :], U[:, kc, 0:n], start=True, stop=True,
                             reads=["SB5_%d" % (sc % 2), "U"], writes=["PSM%d" % (2 + r)])
                    sl_ = slice(sc, sc + 1)
                    ANG, A2, SIN, COS, IR, II, RR, RI, ZR, ZI, TA, TB = (x[:, 0:n] for x in T5)
                    S.op("dve", "tensor_scalar", ANG, iota[:, 0:n], th[:, sl_], None, ALU.mult, reads=["CS", "S5P"], writes=["ANG"])
                    S.op("dve", "tensor_scalar", A2, ANG, PI / 2, None, ALU.add, reads=["ANG"], writes=["A2"])
                    wrap_sin(SIN, ANG, n, TA, TB, srcn=["ANG"], dstn=["SIN"])
                    S.barrier()
                    wrap_sin(COS, A2, n, TA, TB, srcn=["A2"], dstn=["COS"])
                    S.barrier()
                    S.op("dve", "tensor_scalar", IR, COS, cr[:, sl_], None, ALU.mult, writes=["IR"])
                    S.op("dve", "scalar_tensor_tensor", IR, SIN, ci[:, sl_], IR, ALU.mult, ALU.add, writes=["IR"])
                    S.op("dve", "tensor_scalar", II, COS, ci[:, sl_], None, ALU.mult, writes=["II"])
                    S.op("dve", "scalar_tensor_tensor", II, SIN, ncr[:, sl_], II, ALU.mult, ALU.add, reads=["II"], writes=["II"])
                    S.op("dve", "tensor_tensor", RR, IR, pb[0][:, 0:n], ALU.mult, reads=["IR", "PSM2"], writes=["RR"])
                    S.op("dve", "tensor_tensor", TA, II, pb[1][:, 0:n], ALU.mult, reads=["II", "PSM3"], writes=["TA"])
                    S.op("dve", "tensor_tensor", RR, RR, TA, ALU.subtract, reads=["TA", "RR"], writes=["RR"])
                    S.op("dve", "tensor_tensor", RI, IR, pb[1][:, 0:n], ALU.mult, reads=["IR", "PSM3"], writes=["RI"])
                    S.op("dve", "tensor_tensor", TB, II, pb[0][:, 0:n], ALU.mult, reads=["II", "PSM2"], writes=["TB"])
                    S.op("dve", "tensor_tensor", RI, RI, TB, ALU.add, reads=["TB", "RI"], writes=["RI"])
                    S.op("dve", "tensor_scalar", TA, rset[:, 0:n], rho[:, sl_], None, ALU.mult, reads=["TA", "CS", "S5P"], writes=["TA"])
                    for r, RX in ((0, RR), (1, RI)):
                        S.op("dve", "tensor_scalar", TB[:, 0:nb], s5v[:, r, sc, :], rho[:, sl_], None, ALU.mult,
                             reads=["S5ST", "TB"], writes=["TB"])
                        S.op("dve", "tensor_tensor", v3(RX)[:, :, 0], v3(RX)[:, :, 0], TB[:, 0:nb], ALU.add,
                             reads=["TB", "RR", "RI"], writes=["RR", "RI"])
                    S.op("dve", "tensor_tensor_scan", ZR, TA, RR, 0.0, ALU.mult, ALU.add, reads=["TA", "RR"], writes=["ZR"])
                    S.op("dve", "tensor_tensor_scan", ZI, TA, RI, 0.0, ALU.mult, ALU.add, reads=["TA", "RI"], writes=["ZI"])
                    S.barrier()
                    S.op("dve", "tensor_tensor", RR, COS, ZR, ALU.mult, writes=["RR"])
                    S.op("dve", "tensor_tensor", TA, SIN, ZI, ALU.mult, writes=["TA"])
                    S.op("dve", "tensor_tensor", RR, RR, TA, ALU.subtract, reads=["TA"], writes=["RR"])
                    S.op("dve", "tensor_tensor", RI, COS, ZI, ALU.mult, writes=["RI"])
                    S.op("dve", "tensor_tensor", TB, SIN, ZR, ALU.mult, writes=["TB"])
                    S.op("dve", "tensor_tensor", RI, RI, TB, ALU.add, reads=["TB"], writes=["RI"])
                    S.op("dve", "tensor_copy", s5v[:, 0, sc, :], v3(RR)[:, :, L - 1], reads=["RR"], writes=["S5ST"])
                    S.op("dve", "tensor_copy", s5v[:, 1, sc, :], v3(RI)[:, :, L - 1], reads=["RI"], writes=["S5ST"])
                    sct = SB5[sc % 2]
                    S.dma("sp", out=sct[:], in_=s5C[l, :, sc * 128:(sc + 1) * 128, oc * 128:(oc + 1) * 128].rearrange("r p m -> p r m"),
                          reads=[], writes=["SB5_%d" % (sc % 2)])
                    for r, RX, nm in ((0, RR, "RR"), (1, RI, "RI")):
                        S.op("pe", "matmul", psy[r][:, 0:n], sct[:, r, :], RX, start=(sc % 4 == 0), stop=(sc % 4 == 3),
                             reads=["SB5_%d" % (sc % 2), nm], writes=["PSM%d" % (4 + r)])
                    if sc % 4 == 3:
                        dcol = PC[:, l, PO["s5d"] + oc:PO["s5d"] + oc + 1]
                        Y = ZR
                        S.op("act", "activation", TA, psy[1][:, 0:n], AF.Copy, reads=["PSM5"], writes=["TA"])
                        S.op("dve", "tensor_tensor", Y, psy[0][:, 0:n], TA, ALU.subtract, reads=["PSM4", "TA"], writes=["ZR"])
                        S.op("dve", "scalar_tensor_tensor", Y, U[:, oc, 0:n], dcol, Y, ALU.mult, ALU.add, reads=["U", "PC"], writes=["ZR"])
                        S.op("dve", "tensor_tensor", TB, Y, Y, ALU.mult, reads=["ZR"], writes=["TB"])
                        S.op("dve", "tensor_scalar", TB, TB, 0.044715, 1.0, ALU.mult, ALU.add, writes=["TB"])
                        S.op("dve", "tensor_tensor", TB, TB, Y, ALU.mult, writes=["TB"])
                        S.op("act", "activation", TB, TB, AF.Sigmoid, scale=1.5957691216057308, reads=["TB"], writes=["TB"])
                        S.op("dve", "tensor_tensor", Y1F[:, oc, 0:n], Y, TB, ALU.mult, reads=["TB", "ZR"], writes=["Y1F"])
                        S.op("dve", "tensor_copy", AJ[:, oc, 0:n], Y1F[:, oc, 0:n], reads=["Y1F"], writes=["AJ"])
                        S.barrier()

                def cons_glu(col, w, pap, pres):
                    oc = col // 128
                    S.op("act", "activation", T5[0][:, 0:n], pap, AF.Sigmoid, reads=[pres], writes=["ANG"])
                    S.op("dve", "tensor_tensor", YBR[:, oc, 0:n], Y1F[:, oc, 0:n], T5[0][:, 0:n], ALU.mult,
                         reads=["ANG", "Y1F"], writes=["YBR"])
                linear(w_glu[l], 0, 512, [(0, 256, [(0, 128), (128, 128)]), (256, 256, [(0, 128), (128, 128)])], AJ, n, cons_glu, aname="AJ", wk="glu%d" % l)
                S.dma("act", out=o_s5[kind][l], in_=s5v, reads=["S5ST"], writes=["o_s5"])
                S.barrier()


                S.barrier()
                MUi, MUs, MLs, NEGU, POSL, SAME = (CK[:, i_, :] for i_ in range(6))
                rowmask = CS[:, CI["rowmask"], 0:NSEQ_S] if kind == "s" else CS[:, CI["ones"], 0:NSEQ_S]
                headblk = CS[:, CI["headblk"], :]
                nchunk = n // 128
                levels = 6 if kind == "p" else 2
                bump = [6976]

                def alloc(k):
                    o_ = bump[0]
                    bump[0] += k
                    assert bump[0] <= NA, bump[0]
                    return AR[:, o_:o_ + k]
                RKV = alloc(3 * n).rearrange("p (c t) -> p c t", c=3)
                AAb, BBb, LWb, Gb, BON = (alloc(n) for _ in range(5))
                PW = alloc(nb * 64).rearrange("p (b v) -> p b v", b=nb)
                TT = [alloc(128) for _ in range(20)]
                UU = alloc(128)
                KQM_flat = AR[:, 1040:1040 + 2176]
                KQM = KQM_flat[:, 0:2048].rearrange("p (b c) -> p b c", b=16)
                KQD = KQM_flat.rearrange("p (b x) -> p b x", x=136)[:, :, 0:8]
                if nb > 1:
                    KM1 = alloc(128)
                    KM2 = alloc(128)
                S.private = {"LTm", "Lm", "AKT", "RBT", "RKT", "y0", "y1", "P1T"}
                XT1 = [AR[:, 1040 + i_ * 128:1040 + (i_ + 1) * 128] for i_ in range(11)] if nb == 1 else None
                LORA = MRG[:, 8:12, :]
                rst = STT[:, 0:16 * nb * 1].rearrange("p (c b h) -> p c b h", c=16, b=nb)
                if first:
                    S.op("dve", "memset", STT[:, 0:16 * nb], 0.0, writes=["STT"])
                else:
                    S.dma("act", out=rst, in_=(o_shift["p"][l] if kind == "p" else st_shift[l]), reads=["o_shift"], writes=["STT"])
                mu = PC[:, l, PO["mu"]:PO["mu"] + 16]

                def shifted(bi_g, w, ev, eres, dst, dname):
                    hsv = dst.rearrange("p (b t) -> p b t", b=nb)
                    S.op("dve", "tensor_tensor", hsv[0:w], ev[0:w, :, 0:L], ev[0:w, :, 1:1 + L], ALU.subtract, reads=[eres], writes=[dname])
                    S.op("dve", "scalar_tensor_tensor", hsv[0:w], hsv[0:w], mu[0:w, bi_g:bi_g + 1], ev[0:w, :, 1:1 + L], ALU.mult, ALU.add,
                         reads=[eres, "PC", dname], writes=[dname])

                HS = TT[0:4]

                def after_lora(bi, w, ev, eres):
                    hs_t = AR[:, 6976 + 3 * n:6976 + 4 * n]
                    shifted(12 + bi, w, ev, eres, hs_t[:, 0:n], "AAb")
                    fn = (AF.Tanh, AF.Copy, AF.Sigmoid, AF.Sigmoid)[bi]
                    S.op("act", "activation", LORA[0:w, bi, 0:n], hs_t[0:w, 0:n], fn, reads=["AAb"], writes=["LORA"])
                halo_proj(w_in[l], C_RW, RW_BLOCKS[12:16], 1, None, None, "o_shift", after_lora, nbuf=2, sblk=([12, 13, 14, 15], 16), wk="in%d" % l)

                ps_rr = [0]

                def PS_():
                    i_ = ps_rr[0] % 6
                    ps_rr[0] += 1
                    return PSM[i_], "PSM%d" % i_

                S.private2 = {"RKV", "AAb", "BBb", "LWb", "Gb", "BON", "PW", "GL", "GAM", "GINV", "GPREV", "RT", "KT", "BT_", "AT",
                              "Vt", "KTt", "BTt", "YN", "UU", "P0x", "P0Tx", "P1x", "KM1", "KM2", "KQM"}
                NP = 2 if (kind == "p" and ti > 0) else 1
                BK = [(PSM[i_], "PSM%d" % i_) for i_ in range(6)] + [(PSD[0], "PSD0"), (PSD[1], "PSD1")]
                pslots = [dict(RKV=RKV, AR5=(AAb, BBb, LWb, Gb, BON), PW=PW, TT=TT, UU=UU, XT1=XT1, banks=BK[0:6] if NP == 1 else BK[0:4])]
                if NP == 2:
                    S.barrier()
                    ws_busy[0] = True
                    w0_, w1_ = WS[0], WS[1]
                    pslots.append(dict(RKV=w0_[:, 0:3 * n].rearrange("p (c t) -> p c t", c=3),
                                       AR5=tuple(w0_[:, 1536 + i_ * 512:1536 + (i_ + 1) * 512] for i_ in range(5)),
                                       PW=w1_[:, 0:64].rearrange("p (b v) -> p b v", b=1),
                                       TT=[w1_[:, 64 + i_ * 128:64 + (i_ + 1) * 128] for i_ in range(20)],
                                       UU=w1_[:, 2624:2752],
                                       XT1=[AR[:, 2448 + i_ * 128:2448 + (i_ + 1) * 128] for i_ in range(11)],
                                       banks=BK[4:8]))

                def pair_p(pr, sl):
                    RKV, (AAb, BBb, LWb, Gb, BON), PW = sl["RKV"], sl["AR5"], sl["PW"]

                    def after_rkv(bi, w, ev, eres):
                        shifted((pr, 4 + pr, 8 + pr)[bi], w, ev, eres, RKV[:, bi, 0:n], "RKV")
                    halo_proj(w_in[l], C_RW, [(pr * 128, 128), (512 + pr * 128, 128), (1024 + pr * 128, 128)], 1, None, None,
                              "o_shift", after_rkv, nbuf=2, sblk=([pr, 4 + pr, 8 + pr], 16), wk="in%d" % l)
                    rT_, kT_, vT_ = RKV[:, 0, 0:n], RKV[:, 1, 0:n], RKV[:, 2, 0:n]
                    pcol_ = lambda nm: PC[:, l, PO[nm] + pr:PO[nm] + pr + 1]
                    T1, T2 = BON[:, 0:n], Gb[:, 0:n]

                    def cons_w(col, w, pap, pres):
                        S.op("act", "activation", LWb[:, 0:n], pap, AF.Identity, bias=pcol_("nw0"), scale=1.0, reads=[pres, "PC"], writes=["LWb"])
                        S.op("act", "activation", LWb[:, 0:n], LWb[:, 0:n], AF.Exp, scale=-1.0, reads=["LWb"], writes=["LWb"])
                        S.op("act", "activation", LWb[:, 0:n], LWb[:, 0:n], AF.Ln, bias=1.0, scale=1.0, reads=["LWb"], writes=["LWb"])
                        S.op("act", "activation", LWb[:, 0:n], LWb[:, 0:n], AF.Exp, bias=-0.5, scale=-1.0, reads=["LWb"], writes=["LWb"])
                        S.op("dve", "tensor_scalar", LWb[:, 0:n], LWb[:, 0:n], -1.0, None, ALU.mult, reads=["LWb"], writes=["LWb"])
                    linear(w_lora[0][l], 0, 96, [(pr * 128, 128, [(0, 128)])], LORA, n, cons_w, kc0=0, aname="LORA", wk="lo0%d" % l)

                    def cons_a(col, w, pap, pres):
                        S.op("act", "activation", AAb[:, 0:n], pap, AF.Sigmoid, bias=pcol_("a0"), scale=1.0, reads=[pres, "PC"], writes=["AAb"])
                    linear(w_lora[1][l], 0, 96, [(pr * 128, 128, [(0, 128)])], LORA, n, cons_a, kc0=1, aname="LORA", wk="lo1%d" % l)
                    S.op("dve", "tensor_scalar", BBb[:, 0:n], kT_, pcol_("kk"), None, ALU.mult, reads=["RKV", "PC"], writes=["BBb"])
                    S.op("act", "activation", T1, BBb[:, 0:n], AF.Square, reads=["BBb"], writes=["BON"])
                    pt, pn_ = PS_()
                    S.op("pe", "matmul", pt[:, 0:n], headblk, T1, start=True, stop=True, reads=["BON", "CS"], writes=[pn_])
                    S.op("act", "activation", T1, pt[:, 0:n], AF.Sqrt, bias=1e-6, scale=1.0, reads=[pn_], writes=["BON"])
                    S.op("dve", "reciprocal", T1, T1, reads=["BON"], writes=["BON"])
                    S.op("dve", "tensor_tensor", BBb[:, 0:n], BBb[:, 0:n], T1, ALU.mult, reads=["BON", "BBb"], writes=["BBb"])
                    S.op("dve", "tensor_scalar", T1, AAb[:, 0:n], -1.0, pcol_("ka"), ALU.add, ALU.mult, reads=["AAb", "PC"], writes=["BON"])
                    S.op("dve", "tensor_scalar", T1, T1, 1.0, None, ALU.add, reads=["BON"], writes=["BON"])
                    S.op("dve", "tensor_tensor", kT_, kT_, T1, ALU.mult, reads=["BON", "RKV"], writes=["RKV"])
                    S.op("dve", "tensor_tensor", T1, BBb[:, 0:n], AAb[:, 0:n], ALU.mult, reads=["BBb", "AAb"], writes=["BON"])
                    S.op("dve", "tensor_scalar", AAb[:, 0:n], BBb[:, 0:n], -1.0, None, ALU.mult, reads=["BBb", "AAb"], writes=["AAb"])
                    S.op("dve", "tensor_copy", BBb[:, 0:n], T1, reads=["BON"], writes=["BBb"])
                    S.op("dve", "scalar_tensor_tensor", T1, rT_, pcol_("rk"), kT_, ALU.mult, ALU.mult, reads=["RKV", "PC", "BON"], writes=["BON"])
                    pt, pn_ = PS_()
                    S.op("pe", "matmul", pt[:, 0:n], headblk, T1, start=True, stop=True, reads=["BON", "CS"], writes=[pn_])
                    S.op("dve", "tensor_tensor", BON[:, 0:n], pt[:, 0:n], vT_, ALU.mult, reads=[pn_, "RKV", "BON"], writes=["BON"])

                    def cons_g(col, w, pap, pres):
                        S.op("act", "activation", Gb[:, 0:n], pap, AF.Copy, reads=[pres], writes=["Gb"])
                    linear(w_lora[2][l], 0, 256, [(pr * 128, 128, [(0, 128)])], LORA, n, cons_g, kc0=2, aname="LORA", wk="lo2%d" % l)
                    if first:
                        S.op("dve", "memset", PW[:, :, :], 0.0, writes=["PW"])
                    else:
                        S.dma("act", out=PW[:, :, :], in_=(o_wkv["p"][l, pr] if kind == "p" else st_wkv[l, pr]), reads=["o_wkv"], writes=["PW"])

                def pair_c(pr, sl):
                    RKV, (AAb, BBb, LWb, Gb, BON), PW = sl["RKV"], sl["AR5"], sl["PW"]
                    TT, UU, XT1 = sl["TT"], sl["UU"], sl["XT1"]
                    pcol_ = lambda nm: PC[:, l, PO[nm] + pr:PO[nm] + pr + 1]
                    prr_ = [0]

                    def PS_():
                        bk = sl["banks"][prr_[0] % len(sl["banks"])]
                        prr_[0] += 1
                        return bk
                    for c in range(nchunk):
                        if _os0.environ.get("MK_RSTOP"):
                            continue
                        cs_ = slice(c * 128, (c + 1) * 128)
                        (GL, GAM, GINV, GPREV, RT, KT, BT_, AT, Vt, KTt, BTt, LTm, Lm, AKT, RBT, RKT, y0, y1, YN, P1T) = TT
                        P0, P0T, P1 = GINV, GPREV, GL
                        rs_ = (CS[:, CI["reset_p"], :] if kind == "p" else CS[:, CI["reset_s"], :])
                        S.op("dve", "tensor_tensor_scan", GL, rs_, LWb[:, cs_], 0.0, ALU.mult, ALU.add, reads=["CS", "LWb"], writes=["GL"])
                        S.op("act", "activation", GAM, GL, AF.Exp, reads=["GL"], writes=["GAM"])
                        S.op("act", "activation", GINV, GL, AF.Exp, scale=-1.0, reads=["GL"], writes=["GINV"])
                        S.op("dve", "tensor_tensor", GPREV, GL, LWb[:, cs_], ALU.subtract, reads=["GL", "LWb"], writes=["GPREV"])
                        S.op("act", "activation", GPREV, GPREV, AF.Exp, reads=["GPREV"], writes=["GPREV"])
                        S.op("dve", "tensor_tensor", RT, RKV[:, 0, cs_], GAM, ALU.mult, reads=["RKV", "GAM"], writes=["RT"])
                        S.op("dve", "tensor_tensor", KT, RKV[:, 1, cs_], GINV, ALU.mult, reads=["RKV", "GINV"], writes=["KT"])
                        S.op("dve", "tensor_tensor", BT_, BBb[:, cs_], GINV, ALU.mult, reads=["BBb", "GINV"], writes=["BT_"])
                        S.op("dve", "tensor_tensor", AT, AAb[:, cs_], GPREV, ALU.mult, reads=["AAb", "GPREV"], writes=["AT"])
                        pt, pn_ = PS_()
                        S.op("pe", "transpose", pt[:, 0:128], RKV[:, 2, cs_], ident, reads=["RKV", "CS"], writes=[pn_])
                        S.op("pe", "transpose", pt[:, 128:256], KT, ident, reads=["KT", "CS"], writes=[pn_])
                        S.op("pe", "transpose", pt[:, 256:384], BT_, ident, reads=["BT_", "CS"], writes=[pn_])
                        S.op("act", "activation", Vt, pt[:, 0:128], AF.Copy, reads=[pn_], writes=["Vt"])
                        S.op("dve", "tensor_copy", KTt, pt[:, 128:256], reads=[pn_], writes=["KTt"])
                        S.op("act", "activation", BTt, pt[:, 256:384], AF.Copy, reads=[pn_], writes=["BTt"])
                        HB = [(LTm, Lm, AKT, RBT, RKT, y0, y1, P1T, P0, P0T, P1, "GINV", "GPREV", "GL")]
                        if nb == 1:
                            HB.append(tuple(XT1) + ("P0x", "P0Tx", "P1x"))

                        def head_chain(hh, threaded):
                            (LTm, Lm, AKT, RBT, RKT, y0, y1, P1T, P0, P0T, P1, nP0, nP0T, nP1) = HB[hh if threaded else 0]
                            nbk = len(sl["banks"]) // 2
                            banks = sl["banks"][hh * nbk:(hh + 1) * nbk] if threaded else sl["banks"]
                            rr_ = [0]

                            def PS_():
                                bk = banks[rr_[0] % len(banks)]
                                rr_[0] += 1
                                return bk
                            hp = slice(hh * 64, (hh + 1) * 64)
                            hv = slice(hh * 64, (hh + 1) * 64)
                            pa, pan = PS_()
                            S.op("pe", "matmul", pa[:, 0:128], BT_[hp, :], AT[hp, :], start=True, stop=True, reads=["BT_", "AT"], writes=[pan])
                            S.op("pe", "matmul", pa[:, 128:256], AT[hp, :], BT_[hp, :], start=True, stop=True, reads=["BT_", "AT"], writes=[pan])
                            S.op("pe", "matmul", pa[:, 256:384], KT[hp, :], AT[hp, :], start=True, stop=True, reads=["KT", "AT"], writes=[pan])
                            pb2, pbn = PS_()
                            S.op("pe", "matmul", pb2[:, 0:128], BT_[hp, :], RT[hp, :], start=True, stop=True, reads=["BT_", "RT"], writes=[pbn])
                            S.op("pe", "matmul", pb2[:, 128:256], KT[hp, :], RT[hp, :], start=True, stop=True, reads=["KT", "RT"], writes=[pbn])
                            S.op("dve", "tensor_tensor", LTm, pa[:, 0:128], MUs, ALU.mult, reads=[pan, "CK"], writes=["LTm"])
                            S.op("dve", "tensor_tensor", Lm, pa[:, 128:256], MLs, ALU.mult, reads=[pan, "CK"], writes=["Lm"])
                            S.op("dve", "tensor_tensor", AKT, pa[:, 256:384], MUs, ALU.mult, reads=[pan, "CK"], writes=["AKT"])
                            S.op("dve", "tensor_tensor", RBT, pb2[:, 0:128], MUi, ALU.mult, reads=[pbn, "CK"], writes=["RBT"])
                            S.op("dve", "tensor_tensor", RKT, pb2[:, 128:256], MUi, ALU.mult, reads=[pbn, "CK"], writes=["RKT"])

                            def state_acc(pt2, pn2, srcT, srcname, first_start):
                                if nb == 1:
                                    S.op("pe", "matmul", pt2[:, 0:64], srcT[hp, :], PW[hp, 0, :], start=first_start, stop=True,
                                         reads=[srcname, "PW"], writes=[pn2])
                                else:
                                    S.op("dve", "memset", KQM_flat[:, 0:2048], 0.0, writes=["KQM"])
                                    S.op("dve", "tensor_copy", KQD[hp], srcT[hp, :].rearrange("p (b t) -> p b t", b=16), reads=[srcname], writes=["KQM"])
                                    for b_ in range(nb):
                                        S.op("pe", "matmul", pt2[:, 0:64], KQM[hp, b_, :], PW[hp, b_, :], start=(first_start and b_ == 0),
                                             stop=(b_ == nb - 1), reads=["KQM", "PW"], writes=[pn2])
                            pr_, prn = PS_()
                            S.op("pe", "matmul", pr_[:, 0:64], AKT, Vt[:, hv], start=True, stop=False, reads=["AKT", "Vt"], writes=[prn])
                            state_acc(pr_, prn, AT, "AT", False)
                            S.op("act", "activation", y0[:, 0:64], pr_[:, 0:64], AF.Copy, reads=[prn], writes=["y0"])
                            pt, pn_ = PS_()
                            S.op("pe", "matmul", pt[:, 0:64], LTm, y0[:, 0:64], start=True, stop=True, reads=["LTm", "y0"], writes=[pn_])
                            S.op("dve", "tensor_tensor", y1[:, 0:64], y0[:, 0:64], pt[:, 0:64], ALU.add, reads=[pn_, "y0"], writes=["y1"])
                            ycur, yn_, yoth, yon_ = y1, "y1", y0, "y0"
                            Pc, PcT, Pcn, PcTn = Lm, LTm, "Lm", "LTm"
                            pw = [(P0, P0T, nP0, nP0T), (P1, P1T, nP1, "P1T")]
                            for lev in range(levels):
                                Pn, PnT, Pnn, PnTn = pw[lev % 2]
                                pt, pn_ = PS_()
                                S.op("pe", "matmul", pt[:, 0:128], Pc, PcT, start=True, stop=True, reads=[Pcn, PcTn], writes=[pn_])
                                if lev < levels - 1:
                                    S.op("pe", "matmul", pt[:, 128:256], PcT, Pc, start=True, stop=True, reads=[Pcn, PcTn], writes=[pn_])
                                S.op("act", "activation", PnT, pt[:, 0:128], AF.Copy, reads=[pn_], writes=[PnTn])
                                if lev < levels - 1:
                                    S.op("act", "activation", Pn, pt[:, 128:256], AF.Copy, reads=[pn_], writes=[Pnn])
                                pt2, pn2 = PS_()
                                S.op("pe", "matmul", pt2[:, 0:64], PnT, ycur[:, 0:64], start=True, stop=True, reads=[PnTn, yn_], writes=[pn2])
                                S.op("dve", "tensor_tensor", yoth[:, 0:64], ycur[:, 0:64], pt2[:, 0:64], ALU.add, reads=[pn2, yn_], writes=[yon_])
                                ycur, yn_, yoth, yon_ = yoth, yon_, ycur, yn_
                                Pc, PcT, Pcn, PcTn = Pn, PnT, Pnn, PnTn
                            Um, Un_ = ycur, yn_
                            py, pyn = PS_()
                            S.op("pe", "matmul", py[:, 0:64], RBT, Um[:, 0:64], start=True, stop=False, reads=["RBT", Un_], writes=[pyn])
                            S.op("pe", "matmul", py[:, 0:64], RKT, Vt[:, hv], start=False, stop=False, reads=["RKT", "Vt"], writes=[pyn])
                            state_acc(py, pyn, RT, "RT", False)
                            st6 = AKT[:, 0:6]
                            mv = AKT[:, 8:10]
                            S.op("act", "activation", RBT[:, 0:64], py[:, 0:64], AF.Copy, reads=[pyn], writes=["RBT"])
                            S.op("dve", "bn_stats", st6, RBT[:, 0:64], reads=["RBT"], writes=["AKT"])
                            S.op("dve", "bn_aggr", mv, st6, reads=["AKT"], writes=["AKT"])
                            S.op("act", "activation", mv[:, 1:2], mv[:, 1:2], AF.Sqrt, bias=64e-5, scale=1.0, reads=["AKT"], writes=["AKT"])
                            S.op("dve", "reciprocal", mv[:, 1:2], mv[:, 1:2], reads=["AKT"], writes=["AKT"])
                            S.op("dve", "tensor_scalar", YN[:, hv], RBT[:, 0:64], mv[:, 0:1], mv[:, 1:2], ALU.subtract, ALU.mult,
                                 reads=["RBT", "AKT"], writes=["YN"])
                            S.op("dve", "tensor_copy", UU[:, hv], Um[:, 0:64], reads=[Un_], writes=["UU"])
                        if nb == 1:
                            ths = []
                            for hh in range(2):
                                S.sfx = "@%d" % hh
                                S.thread_begin()
                                head_chain(hh, True)
                                ths.append(S.thread_end())
                            S.sfx = ""
                            S.replay(ths)
                        else:
                            for hh in range(2):
                                head_chain(hh, False)
                        for b_ in range(nb):
                            if nb > 1:
                                S.op("dve", "tensor_scalar", KM1, BTt, rowmask[:, b_:b_ + 1], None, ALU.mult, reads=["BTt", "CS"], writes=["KM1"])
                                S.op("dve", "tensor_scalar", KM2, KTt, rowmask[:, b_:b_ + 1], None, ALU.mult, reads=["KTt", "CS"], writes=["KM2"])
                                lb, lbn = (KM1, KM2), ("KM1", "KM2")
                            else:
                                lb, lbn = (BTt, KTt), ("BTt", "KTt")
                            lastc = (b_ + 1) * L - 1 if nb > 1 else 127
                            for hh in range(2):
                                hp = slice(hh * 64, (hh + 1) * 64)
                                hv = hp
                                pt, pn_ = PS_()
                                S.op("pe", "matmul", pt[hp, 0:64], lb[0][:, hv], UU[:, hv], start=True, stop=False, reads=[lbn[0], "UU"], writes=[pn_])
                                S.op("pe", "matmul", pt[hp, 0:64], lb[1][:, hv], Vt[:, hv], start=False, stop=True, reads=[lbn[1], "Vt"], writes=[pn_])
                                S.op("dve", "tensor_tensor", PW[hp, b_, :], PW[hp, b_, :], pt[hp, 0:64], ALU.add, reads=[pn_, "PW"], writes=["PW"])
                                S.op("dve", "tensor_scalar", PW[hp, b_, :], PW[hp, b_, :], GAM[hp, lastc:lastc + 1], None, ALU.mult,
                                     reads=["GAM", "PW"], writes=["PW"])
                        pt, pn_ = PS_()
                        S.op("pe", "transpose", pt[:, 0:128], YN, ident, reads=["YN", "CS"], writes=[pn_])
                        S.op("dve", "tensor_scalar", RT, pt[:, 0:128], pcol_("lnw"), pcol_("lnb"), ALU.mult, ALU.add, reads=[pn_, "PC"], writes=["RT"])
                        S.op("dve", "tensor_tensor", RT, RT, BON[:, cs_], ALU.add, reads=["RT", "BON"], writes=["RT"])
                        S.op("dve", "tensor_tensor", YBR[:, 4 + pr, cs_], RT, Gb[:, cs_], ALU.mult, reads=["RT", "Gb"], writes=["YBR"])
                    S.dma("act", out=o_wkv[kind][l, pr], in_=PW[:, :, :], reads=["PW"], writes=["o_wkv"])

                for pg in range(0, 4, NP):
                    for k_ in range(NP):
                        S.sfx2 = "#%d" % k_
                        pair_p(pg + k_, pslots[k_])
                    pths = []
                    for k_ in range(NP):
                        S.sfx2 = "#%d" % k_
                        S.thread_begin()
                        pair_c(pg + k_, pslots[k_])
                        pths.append(S.thread_end())
                    S.sfx2 = ""
                    S.replay(pths)
                if NP == 2:
                    S.barrier()
                    ws_busy[0] = False
                S.private2 = set()
                S.dma("act", out=o_shift[kind][l], in_=rst, reads=["STT"], writes=["o_shift"])
                S.barrier()

                S.barrier()
                MUi, MUs, MLs, NEGU, POSL, SAME = (CK[:, i_, :] for i_ in range(6))
                rowmask = CS[:, CI["rowmask"], 0:NSEQ_S] if kind == "s" else CS[:, CI["ones"], 0:NSEQ_S]
                nchunk = n // 128
                levels = 6 if kind == "p" else 2
                bump = [6976]

                def alloc(k):
                    o_ = bump[0]
                    bump[0] += k
                    assert bump[0] <= NA, bump[0]
                    return AR[:, o_:o_ + k]
                QKV = alloc(3 * n).rearrange("p (c t) -> p c t", c=3)
                SG = alloc(nb * 128).rearrange("p (b v) -> p b v", b=nb)
                TT = [alloc(128) for _ in range(22)]
                BTF = alloc(n)
                GCH = alloc(128)
                L2S = alloc(n)
                KQM_flat = AR[:, 1040:1040 + 2176]
                KQM = KQM_flat[:, 0:2048].rearrange("p (b c) -> p b c", b=16)
                KQD = KQM_flat.rearrange("p (b x) -> p b x", x=136)[:, :, 0:8]
                SC8c = [alloc(64) for _ in range(nchunk)]
                GCFc = [alloc(128) for _ in range(nchunk)]
                EGc = [alloc(nb * 8) for _ in range(nchunk)]
                GM = SB5[1][:, 1, :]
                ZS = MRG
                def cons_z(col, w, pap, pres):
                    S.op("act", "activation", ZS[:, (col - C_Z) // 128, 0:n], pap, AF.Silu, reads=[pres], writes=["ZS"])
                linear(w_in[l], 0, D, [(C_Z + i_ * 256, 256, [(0, 128), (128, 128)]) for i_ in range(4)], HN, n, cons_z, wk="in%d" % l)
                ws_ = SB5[0][:, :, :].rearrange("p a b -> p (a b)")
                S.dma("sp", out=ws_[:, 0:256].rearrange("p (kc n) -> p kc n", kc=16),
                      in_=w_in[l][:, C_B:C_B + 16].rearrange("(kc p) n -> p kc n", p=128), writes=["SB5_0"])
                S.op("dve", "tensor_copy", WBA[:, :, :], ws_[:, 0:256].rearrange("p (kc n) -> p kc n", kc=16),
                     reads=["SB5_0"], writes=["WBA"])
                wv_ = WBA
                pb_, pa_ = PSM[0], PSM[1]
                for (pp, c0, pn_) in ((pb_, 0, "PSM0"), (pa_, 8, "PSM1")):
                    for kc in range(16):
                        S.op("pe", "matmul", pp[0:8, 0:n], wv_[:, kc, c0:c0 + 8], HN[:, kc, 0:n], start=(kc == 0), stop=(kc == 15),
                             reads=["WBA", "HN"], writes=[pn_])
                S.op("act", "activation", BTF[0:8, 0:n], pb_[0:8, 0:n], AF.Sigmoid, reads=["PSM0"], writes=["BTF"])
                alog = PR[:, l, 0:8]
                dtb = PR[:, l, 8:16]
                NEA = alloc(8)
                S.op("act", "activation", NEA, alog, AF.Exp, reads=["PR"], writes=["NEA"])
                S.op("dve", "tensor_scalar", NEA, NEA, -1.0, None, ALU.mult, reads=["NEA"], writes=["NEA"])

                gst = STT[:, 0:24 * nb * 3].rearrange("p (c b h) -> p c b h", c=24, b=nb)
                if first:
                    S.op("dve", "memset", STT[:, 0:24 * nb * 3], 0.0, writes=["STT"])
                else:
                    S.dma("act", out=gst, in_=(o_gconv["p"][l] if kind == "p" else st_gconv[l]), reads=["o_gconv"], writes=["STT"])

                ps_rr = [0]

                def PS_():
                    i_ = ps_rr[0] % 6
                    ps_rr[0] += 1
                    return PSM[i_], "PSM%d" % i_

                import os as _os
                GSTOP = int(_os.environ.get("MK_GSTOP", "99"))
                for c in range(nchunk):
                    cs_ = slice(c * 128, (c + 1) * 128)
                    sc8 = SC8c[c]
                    BET, GTK, GCT, EGC, NBEG, NGC, EDEC, GLT = (sc8[:, i_ * 8:(i_ + 1) * 8] for i_ in range(8))
                    GCF = GCFc[c]
                    EG = EGc[c]
                    pt, pn_ = PS_()
                    for kc in range(16):
                        S.op("pe", "matmul", pt[:, 0:16], HN[:, kc, cs_], wv_[:, kc, 0:16], start=(kc == 0), stop=(kc == 15),
                             reads=["WBA", "HN"], writes=[pn_])
                    S.op("act", "activation", BET, pt[:, 0:8], AF.Sigmoid, reads=[pn_], writes=["sc8"])
                    S.op("dve", "tensor_tensor", GTK, pt[:, 8:16], dtb, ALU.add, reads=[pn_, "PR"], writes=["sc8"])
                    S.op("act", "activation", GTK, GTK, AF.Exp, reads=["sc8"], writes=["sc8"])
                    S.op("act", "activation", GTK, GTK, AF.Ln, bias=1.0, scale=1.0, reads=["sc8"], writes=["sc8"])
                    S.op("dve", "tensor_tensor", GTK, GTK, NEA, ALU.mult, reads=["sc8", "NEA"], writes=["sc8"])
                    pt, pn_ = PS_()
                    S.op("pe", "matmul", pt[:, 0:8], MUi, GTK, start=True, stop=True, reads=["CK", "sc8"], writes=[pn_])
                    S.op("pe", "matmul", pt[:, 8:16], SAME, GTK, start=True, stop=True, reads=["CK", "sc8"], writes=[pn_])
                    S.op("pe", "matmul", pt[0:8, 128:256], GTK, MUi, start=True, stop=True, reads=["CK", "sc8"], writes=[pn_])
                    S.op("dve", "tensor_copy", GCT, pt[:, 0:8], reads=[pn_], writes=["sc8"])
                    S.op("dve", "tensor_copy", GLT, pt[:, 8:16], reads=[pn_], writes=["sc8"])
                    S.op("dve", "tensor_copy", GCF[0:8, :], pt[0:8, 128:256], reads=[pn_], writes=["GCF"])
                    S.op("act", "activation", EGC, GCT, AF.Exp, reads=["sc8"], writes=["sc8"])
                    S.op("dve", "tensor_tensor", NBEG, BET, EGC, ALU.mult, reads=["sc8"], writes=["sc8"])
                    S.op("dve", "tensor_scalar", NBEG, NBEG, -1.0, None, ALU.mult, reads=["sc8"], writes=["sc8"])
                    S.op("dve", "tensor_scalar", NGC, GCT, -1.0, None, ALU.mult, reads=["sc8"], writes=["sc8"])
                    S.op("dve", "tensor_tensor", EDEC, GLT, GCT, ALU.subtract, reads=["sc8"], writes=["sc8"])
                    S.op("act", "activation", EDEC, EDEC, AF.Exp, reads=["sc8"], writes=["sc8"])
                    S.op("dve", "tensor_tensor", GM.rearrange("p (b h) -> p b h", b=16)[:, 0:nb, :],
                         GTK.unsqueeze(1).to_broadcast([128, nb, 8]), rowmask[:, 0:nb].unsqueeze(2).to_broadcast([128, nb, 8]),
                         ALU.mult, reads=["sc8", "CS"], writes=["GM"])
                    pt, pn_ = PS_()
                    S.op("pe", "matmul", pt[:, 0:nb * 8], ones, GM[:, 0:nb * 8], start=True, stop=True, reads=["GM", "CS"], writes=[pn_])
                    S.op("act", "activation", EG[:, 0:nb * 8], pt[:, 0:nb * 8], AF.Exp, reads=[pn_], writes=["EG"])
                if GSTOP <= 1:
                    continue

                S.private = {"QKV", "SG", "Kdec", "Vb", "t1", "DTi", "t2", "Dms", "DTs", "Nm", "NTm", "tB", "inT", "Rm", "y0", "y1",
                             "P0", "P0T", "P1", "P1T", "IVs", "om", "Km", "junk", "GCH", "KQM"}
                G = (4 if ti > 0 else 2) if kind == "p" else 1
                BK = [(PSM[i_], "PSM%d" % i_) for i_ in range(6)] + [(PSD[0], "PSD0"), (PSD[1], "PSD1")]
                slots = [dict(QKV=QKV, SG=SG, TT=TT, GCH=GCH, banks=BK[0:6] if G == 1 else (BK[0:3] if G == 2 else BK[0:2]))]
                if G >= 2:
                    ra, rb = [1040], [4240]

                    def allocA(k):
                        o_ = ra[0]
                        ra[0] += k
                        assert ra[0] <= 4160
                        return AR[:, o_:o_ + k]

                    def allocB(k):
                        o_ = rb[0]
                        rb[0] += k
                        assert rb[0] <= 6976
                        return AR[:, o_:o_ + k]
                    slots.append(dict(QKV=allocB(3 * n).rearrange("p (c t) -> p c t", c=3),
                                      SG=allocB(nb * 128).rearrange("p (b v) -> p b v", b=nb),
                                      GCH=allocB(128), TT=[allocA(128) for _ in range(22)],
                                      banks=BK[3:6] if G == 2 else BK[2:4]))
                if G == 4:
                    S.barrier()
                    ws_busy[0] = True
                    for k_, wsx in enumerate(WS):
                        tt_ = [wsx[:, 1792 + i_ * 128:1792 + (i_ + 1) * 128] for i_ in range(18)]
                        (Kdec, Vb, t1, t2, DTs, Nm, NTm, tB, inT, Rm, y0, y1, P0, P0T, P1, P1T, om, junk) = tt_
                        slots.append(dict(QKV=wsx[:, 0:3 * n].rearrange("p (c t) -> p c t", c=3),
                                          SG=wsx[:, 1536:1664].rearrange("p (b v) -> p b v", b=nb), GCH=wsx[:, 1664:1792],
                                          TT=[Kdec, Vb, t1, t1, t2, t2, DTs, Nm, NTm, tB, inT, Rm, y0, y1, P0, P0T, P1, P1T, t1, om, junk, junk],
                                          alias={"DTi": "t1", "Dms": "t2", "IVs": "t1", "Km": "junk"},
                                          banks=BK[4 + 2 * k_:6 + 2 * k_]))

                def p_stage(h, sl):
                    QKV, SG = sl["QKV"], sl["SG"]
                    def after_qkv(bi, w, ev, eres):
                        chn = (h, 8 + h, 16 + h)[bi]
                        cw = PC[:, l, PO["gcw"]:PO["gcw"] + 96]
                        dst = QKV[:, bi, 0:n].rearrange("p (b t) -> p b t", b=nb)
                        S.op("dve", "tensor_scalar", dst, ev[:, :, 0:L], cw[:, chn:chn + 1], None, ALU.mult,
                             reads=[eres, "PC"], writes=["QKV"])
                        for wi in (1, 2, 3):
                            S.op("dve", "scalar_tensor_tensor", dst, ev[:, :, wi:wi + L], cw[:, wi * 24 + chn:wi * 24 + chn + 1], dst,
                                 ALU.mult, ALU.add, reads=[eres, "PC", "QKV"], writes=["QKV"])
                        S.op("act", "activation", QKV[:, bi, 0:n], QKV[:, bi, 0:n], AF.Silu, reads=["QKV"], writes=["QKV"])
                    halo_proj(w_in[l], C_QKV, [(h * 128, 128), (1024 + h * 128, 128), (2048 + h * 128, 128)], 3, None, None,
                              "o_gconv", after_qkv, nbuf=2, sblk=([h, 8 + h, 16 + h], 24), wk="in%d" % l)
                    for bi, scl in ((0, 128.0 ** -0.5), (1, 1.0)):
                        pt, pn_ = PS_()
                        S.op("act", "activation", L2S[:, 0:n], QKV[:, bi, 0:n], AF.Square, reads=["QKV"], writes=["L2S"])
                        S.op("pe", "matmul", pt[:, 0:n], ones, L2S[:, 0:n], start=True, stop=True, reads=["L2S", "CS"], writes=[pn_])
                        S.op("act", "activation", L2S[:, 0:n], pt[:, 0:n], AF.Sqrt, bias=1e-6, scale=1.0, reads=[pn_], writes=["L2S"])
                        S.op("dve", "reciprocal", L2S[:, 0:n], L2S[:, 0:n], reads=["L2S"], writes=["L2S"])
                        S.op("dve", "scalar_tensor_tensor", QKV[:, bi, 0:n], QKV[:, bi, 0:n], scl, L2S[:, 0:n], ALU.mult, ALU.mult,
                             reads=["L2S", "QKV"], writes=["QKV"])
                    if first:
                        S.op("dve", "memset", SG[:, :, :], 0.0, writes=["SG"])
                    else:
                        S.dma("act", out=SG[:, :, :], in_=(o_gdn["p"][l, h] if kind == "p" else st_gdn[l, h]), reads=["o_gdn"], writes=["SG"])

                def c_stage(h, sl):
                    QKV, SG, TT, GCH = sl["QKV"], sl["SG"], sl["TT"], sl["GCH"]
                    rr_ = [0]

                    def PS_():
                        bk = sl["banks"][rr_[0] % len(sl["banks"])]
                        rr_[0] += 1
                        return bk
                    for c in range(nchunk):
                        cs_ = slice(c * 128, (c + 1) * 128)
                        qT, kT, vT = QKV[:, 0, cs_], QKV[:, 1, cs_], QKV[:, 2, cs_]
                        (Kdec, Vb, t1, DTi, t2, Dms, DTs, Nm, NTm, tB, inT, Rm, y0, y1, P0, P0T, P1, P1T, IVs, om, Km, junk) = TT
                        sc8 = SC8c[c]
                        BET, GTK, GCT, EGC, NBEG, NGC, EDEC, GLT = (sc8[:, i_ * 8:(i_ + 1) * 8] for i_ in range(8))
                        GCF = GCFc[c]
                        EGv = EGc[c].rearrange("p (b h) -> p b h", b=nb)
                        hs_ = slice(h, h + 1)
                        pt, pn_ = PS_()
                        S.op("pe", "transpose", pt[:, 0:128], kT, ident, reads=["QKV", "CS"], writes=[pn_])
                        S.op("pe", "transpose", pt[:, 128:256], vT, ident, reads=["QKV", "CS"], writes=[pn_])
                        S.op("dve", "tensor_scalar", Kdec, pt[:, 0:128], EDEC[:, hs_], None, ALU.mult, reads=[pn_, "sc8"], writes=["Kdec"])
                        S.op("dve", "tensor_scalar", Vb, pt[:, 128:256], BET[:, hs_], None, ALU.mult, reads=[pn_, "sc8"], writes=["Vb"])
                        S.op("dve", "tensor_scalar", GCH[0:8, :], GCF[0:8, :], ident[0:8, hs_], None, ALU.mult, reads=["GCF", "CS"], writes=["GCH"])
                        S.op("dve", "tensor_scalar", junk[0:8, :], BTF[0:8, cs_], ident[0:8, hs_], None, ALU.mult, reads=["BTF", "CS"], writes=["junk"])
                        pg, pgn = PS_()
                        S.op("pe", "matmul", pg[:, 0:128], kT, kT, start=True, stop=True, reads=["QKV"], writes=[pgn])
                        S.op("pe", "matmul", pg[:, 128:256], kT, qT, start=True, stop=True, reads=["QKV"], writes=[pgn])
                        S.op("pe", "matmul", pg[:, 256:384], ones[0:8, :], GCH[0:8, :], start=True, stop=True, reads=["GCH", "CS"], writes=[pgn])
                        S.op("pe", "matmul", pg[:, 384:512], ones[0:8, :], junk[0:8, :], start=True, stop=True, reads=["junk", "CS"], writes=[pgn])
                        KKp, ITp, GRp, BRp = pg[:, 0:128], pg[:, 128:256], pg[:, 256:384], pg[:, 384:512]
                        if GSTOP <= 2:
                            continue
                        S.op("dve", "tensor_tensor", t1, GRp, NEGU, ALU.add, reads=[pgn, "CK"], writes=["t1"])
                        S.op("act", "activation", DTi, t1, AF.Exp, bias=NGC[:, hs_], scale=1.0, reads=["t1", "sc8"], writes=["DTi"])
                        S.op("dve", "tensor_tensor", t2, GRp, POSL, ALU.add, reads=[pgn, "CK"], writes=["t2"])
                        S.op("act", "activation", Dms, t2, AF.Exp, bias=GCT[:, hs_], scale=-1.0, reads=["t2", "sc8"], writes=["Dms"])
                        S.op("dve", "tensor_tensor", Dms, Dms, MLs, ALU.mult, reads=["Dms", "CK"], writes=["Dms"])
                        S.op("dve", "tensor_tensor", DTs, DTi, MUs, ALU.mult, reads=["DTi", "CK"], writes=["DTs"])
                        S.op("dve", "scalar_tensor_tensor", Nm, KKp, BET[:, hs_], Dms, ALU.mult, ALU.mult, reads=[pgn, "sc8", "Dms"], writes=["Nm"])
                        S.op("dve", "tensor_tensor", tB, DTs, BRp, ALU.mult, reads=[pgn, "DTs"], writes=["tB"])
                        S.op("dve", "tensor_tensor", NTm, KKp, tB, ALU.mult, reads=[pgn, "tB"], writes=["NTm"])
                        S.op("dve", "tensor_tensor", inT, ITp, DTi, ALU.mult, reads=[pgn, "DTi"], writes=["inT"])
                        if GSTOP <= 4:
                            continue
                        def state_mm(srcT, srcname):
                            pt2, pn2 = PS_()
                            if nb == 1:
                                S.op("pe", "matmul", pt2[:, 0:128], srcT, SG[:, 0, :], start=True, stop=True, reads=[srcname, "SG"], writes=[pn2])
                            else:
                                S.op("dve", "memset", KQM_flat[:, 0:2048], 0.0, writes=["KQM"])
                                S.op("dve", "tensor_copy", KQD, srcT.rearrange("p (b t) -> p b t", b=16), reads=[srcname], writes=["KQM"])
                                for b_ in range(nb):
                                    S.op("pe", "matmul", pt2[:, 0:128], KQM[:, b_, :], SG[:, b_, :], start=(b_ == 0), stop=(b_ == nb - 1),
                                         reads=["KQM", "SG"], writes=[pn2])
                            return pt2, pn2
                        pk, pkn = state_mm(kT, "QKV")
                        S.op("dve", "scalar_tensor_tensor", Rm, pk[:, 0:128], NBEG[:, hs_], Vb, ALU.mult, ALU.add, reads=[pkn, "sc8", "Vb"], writes=["Rm"])
                        if GSTOP <= 5:
                            continue
                        pt, pn_ = PS_()
                        S.op("pe", "matmul", pt[:, 0:128], NTm, Rm, start=True, stop=True, reads=["NTm", "Rm"], writes=[pn_])
                        S.op("dve", "tensor_tensor", y0, Rm, pt[:, 0:128], ALU.subtract, reads=[pn_, "Rm"], writes=["y0"])
                        ycur, yn_, yoth, yon_ = y0, "y0", y1, "y1"
                        Pc, PcT, Pcn, PcTn = Nm, NTm, "Nm", "NTm"
                        pw = [(P0, P0T, "P0", "P0T"), (P1, P1T, "P1", "P1T")]
                        for lev in range(min(levels, int(_os.environ.get("MK_G6", "99")))):
                            Pn, PnT, Pnn, PnTn = pw[lev % 2]
                            pt, pn_ = PS_()
                            S.op("pe", "matmul", pt[:, 0:128], Pc, PcT, start=True, stop=True, reads=[Pcn, PcTn], writes=[pn_])
                            if lev < levels - 1:
                                S.op("pe", "matmul", pt[:, 128:256], PcT, Pc, start=True, stop=True, reads=[Pcn, PcTn], writes=[pn_])
                            S.op("act", "activation", PnT, pt[:, 0:128], AF.Copy, reads=[pn_], writes=[PnTn])
                            if lev < levels - 1:
                                S.op("act", "activation", Pn, pt[:, 128:256], AF.Copy, reads=[pn_], writes=[Pnn])
                            pt2, pn2 = PS_()
                            S.op("pe", "matmul", pt2[:, 0:128], PnT, ycur, start=True, stop=True, reads=[PnTn, yn_], writes=[pn2])
                            S.op("dve", "tensor_tensor", yoth, ycur, pt2[:, 0:128], ALU.add, reads=[pn2, yn_], writes=[yon_])
                            ycur, yn_, yoth, yon_ = yoth, yon_, ycur, yn_
                            Pc, PcT, Pcn, PcTn = Pn, PnT, Pnn, PnTn
                        vnew, vn_ = ycur, yn_
                        if GSTOP <= 6:
                            continue
                        pq, pqn = state_mm(qT, "QKV")
                        pt, pn_ = PS_()
                        S.op("pe", "matmul", pt[:, 0:128], inT, vnew, start=True, stop=True, reads=["inT", vn_], writes=[pn_])
                        S.op("act", "activation", IVs, pt[:, 0:128], AF.Copy, reads=[pn_], writes=["IVs"])
                        S.op("dve", "scalar_tensor_tensor", om, pq[:, 0:128], EGC[:, hs_], IVs, ALU.mult, ALU.add, reads=[pqn, "sc8", "IVs"], writes=["om"])
                        if GSTOP <= 7:
                            continue
                        ssq = junk[:, 0:1]
                        S.op("act", "activation", t1, om, AF.Square, accum_out=ssq, reads=["om"], writes=["t1", "junk"])
                        S.op("act", "activation", ssq, ssq, AF.Sqrt, bias=1e-6, scale=1.0 / 128, reads=["junk"], writes=["junk"])
                        S.op("dve", "reciprocal", ssq, ssq, reads=["junk"], writes=["junk"])
                        S.op("dve", "tensor_scalar", om, om, ssq, None, ALU.mult, reads=["junk", "om"], writes=["om"])
                        pt, pn_ = PS_()
                        S.op("pe", "transpose", pt[:, 0:128], om, ident, reads=["om", "CS"], writes=[pn_])
                        S.op("dve", "scalar_tensor_tensor", YBR[:, 8 + h, cs_], pt[:, 0:128], PC[:, l, PO["gng"]:PO["gng"] + 1], ZS[:, h, cs_],
                             ALU.mult, ALU.mult, reads=[pn_, "PC", "ZS"], writes=["YBR"])
                        if GSTOP <= 8:
                            continue
                        if nb == 1:
                            pt, pn_ = PS_()
                            S.op("pe", "matmul", pt[:, 0:128], Kdec, vnew, start=True, stop=True, reads=["Kdec", vn_], writes=[pn_])
                            S.op("dve", "scalar_tensor_tensor", SG[:, 0, :], SG[:, 0, :], EGv[:, 0, hs_], pt[:, 0:128], ALU.mult, ALU.add,
                                 reads=[pn_, "EG", "SG"], writes=["SG"])
                        else:
                            S.op("dve", "tensor_tensor", KQM, Kdec.unsqueeze(1).to_broadcast([128, 16, 128]),
                                 rowmask.unsqueeze(2).to_broadcast([128, 16, 128]), ALU.mult, reads=["Kdec", "CS"], writes=["KQM"])
                            for b_ in range(nb):
                                pt, pn_ = PS_()
                                S.op("pe", "matmul", pt[:, 0:128], KQM[:, b_, :], vnew, start=True, stop=True, reads=["KQM", vn_], writes=[pn_])
                                S.op("dve", "scalar_tensor_tensor", SG[:, b_, :], SG[:, b_, :], EGv[:, b_, hs_], pt[:, 0:128], ALU.mult, ALU.add,
                                     reads=[pn_, "EG", "SG"], writes=["SG"])
                    S.dma("act", out=o_gdn[kind][l, h], in_=SG[:, :, :], reads=["SG"], writes=["o_gdn"])

                for hg in range(0, 8, G):
                    for si in range(G):
                        S.sfx = "@%d" % si
                        S.alias = slots[si].get("alias", {})
                        p_stage(hg + si, slots[si])
                    ths = []
                    for si in range(G):
                        S.sfx = "@%d" % si
                        S.alias = slots[si].get("alias", {})
                        S.thread_begin()
                        c_stage(hg + si, slots[si])
                        ths.append(S.thread_end())
                    S.sfx = ""
                    S.alias = {}
                    S.replay(ths)
                if G == 4:
                    S.barrier()
                    ws_busy[0] = False
                S.dma("act", out=o_gconv[kind][l], in_=gst, reads=["STT"], writes=["o_gconv"])
                S.barrier()

                if stage < 2:
                    continue
                BR = [(0, 4), (4, 4), (8, 8)]
                for op_ in range(8):
                    def cons_gate(br):
                        def f(col, w, pap, pres):
                            sub = ((col - C_G) // 128) % 2
                            g = GT[br * 2 + sub]
                            S.op("act", "activation", g[:, 0:n], pap, AF.Sigmoid, reads=[pres], writes=["GT%d" % (br * 2 + sub)])
                        return f
                    for br in range(3):
                        c0 = C_G + br * D + op_ * 256
                        linear(w_in[l], 0, D, [(c0, 256, [(0, 128), (128, 128)])], HN, n, cons_gate(br), wk="in%d" % l)
                    for br in range(3):
                        def cons_br(col, w, pap, pres, br=br):
                            sub = (col // 128) % 2
                            g = GT[br * 2 + sub]
                            if br == 0:
                                S.op("dve", "tensor_tensor", ACC[sub][:, 0:n], g[:, 0:n], pap, ALU.mult,
                                     reads=[pres, "GT%d" % (br * 2 + sub)], writes=["ACC%d" % sub])
                            else:
                                S.op("dve", "tensor_tensor", TMP[sub][:, 0:n], g[:, 0:n], pap, ALU.mult,
                                     reads=[pres, "GT%d" % (br * 2 + sub)], writes=["TMP%d" % sub])
                                dst = ACC[sub][:, 0:n] if br == 1 else MRG[:, op_ * 2 + sub, 0:n]
                                S.op("dve", "tensor_tensor", dst, ACC[sub][:, 0:n], TMP[sub][:, 0:n], ALU.add,
                                     reads=["ACC%d" % sub, "TMP%d" % sub], writes=["ACC%d" % sub] if br == 1 else ["MRG"])
                        kc0, kn = BR[br]
                        linear(w_brs[br][l], 0, kn * 128, [(op_ * 256, 256, [(0, 128), (128, 128)])], YBR, n, cons_br, kc0=kc0, wk="br%d%d" % (br, l))

                def cons_resid(col, w, pap, pres):
                    o = col // 128
                    S.op("dve", "tensor_tensor", X[:, o, 0:n], X[:, o, 0:n], pap, ALU.add, reads=[pres, "X"], writes=["X"])
                linear(w_out[l], 0, D, [(o2 * 256, 256, [(0, 128), (128, 128)]) for o2 in range(8)], MRG, n, cons_resid, wk="out%d" % l)
                if stage < 3:
                    continue
                rmsnorm(PC[:, l, PO["g2"]:PO["g2"] + 16], n, HN, "HN")
                fin_ = o_fconv["p"][l] if kind == "p" else st_fconv[l]
                svf = STT[:, 0:88 * nb * 2].rearrange("p (c b h) -> p c b h", c=88, b=nb)
                if first:
                    S.op("dve", "memset", STT[:, 0:88 * nb * 2], 0.0, writes=["STT"])
                else:
                    S.dma("act", out=svf, in_=fin_, reads=["o_fconv"], writes=["STT"])
                for j in range(11):
                    def after_ffn(bi, w, ev, eres, j=j):
                        ch = (j * 4 + bi) if bi < 4 else (44 + j * 4 + (bi - 4))
                        cw = PC[:, l, PO["fcw"]:PO["fcw"] + 264]
                        cb = PC[:, l, PO["fcb"] + ch:PO["fcb"] + ch + 1]
                        t = GT[bi % 6] if False else (TMP[0] if bi < 4 else TMP[1])
                        tn = "TMP0" if bi < 4 else "TMP1"
                        tv = t[:, 0:n].rearrange("p (b t) -> p b t", b=nb)
                        S.op("dve", "tensor_scalar", tv, ev[:, :, 0:L], cw[:, ch:ch + 1], cb, ALU.mult, ALU.add,
                             reads=[eres, "PC"], writes=[tn])
                        for wi in (1, 2):
                            S.op("dve", "scalar_tensor_tensor", tv, ev[:, :, wi:wi + L], cw[:, wi * 88 + ch:wi * 88 + ch + 1], tv,
                                 ALU.mult, ALU.add, reads=[eres, "PC", tn], writes=[tn])
                        if bi < 4:
                            S.op("act", "activation", GT[bi][:, 0:n], t[:, 0:n], AF.Silu, reads=[tn], writes=["GT%d" % bi])
                        else:
                            S.op("dve", "tensor_tensor", AJ[:, bi - 4, 0:n], GT[bi - 4][:, 0:n], t[:, 0:n], ALU.mult,
                                 reads=[tn, "GT%d" % (bi - 4)], writes=["AJ"])
                    blocks = [(j * 512 + i * 128, 128) for i in range(4)] + [(DFF + j * 512 + i * 128, 128) for i in range(4)]
                    halo_proj(w_up[l], 0, blocks[:4], 2, None, None, "o_fconv", after_ffn, nbuf=8, sblk=(j * 4, 88), wk="up%d" % l)
                    halo_proj(w_up[l], 0, blocks[4:], 2, None, None, "o_fconv",
                              lambda bi, w, ev, eres: after_ffn(bi + 4, w, ev, eres), nbuf=8, sblk=(44 + j * 4, 88), wk="up%d" % l)
                    for half in range(2):
                        linear(w_down[l], j * 512, 512, [(half * 1024, 1024, [(i * 128, 128) for i in range(8)])], AJ, n, cons_resid, aname="AJ", wk="dn%d" % l)
                S.dma("act", out=o_fconv[kind][l], in_=svf, reads=["STT"], writes=["o_fconv"])
            final_out(kind, ti, n)
        S.final_wait("sp")
        block = es.enter_context(nc.Block())

        @block.sync
        def _(e):
            for f in S.prog["sp"]:
                f(e)

        @block.tensor
        def _(e):
            for f in S.prog["pe"]:
                f(e)

        @block.scalar
        def _(e):
            for f in S.prog["act"]:
                f(e)

        @block.vector
        def _(e):
            for f in S.prog["dve"]:
                f(e)

        @block.gpsimd
        def _(e):
            for f in S.prog["pool"]:
                f(e)
    return nc, S


PO = {}
_o = 0
for _name, _w in [("g1", 16), ("g2", 16), ("gf", 16), ("mu", 16), ("fcw", 264), ("fcb", 88), ("lre", 16), ("lim", 16), ("lst", 16), ("s5d", 4), ("gcw", 96), ("gng", 1), ("nw0", 4), ("a0", 4), ("kk", 4), ("ka", 4), ("rk", 4), ("lnw", 4), ("lnb", 4)]:
    PO[_name] = _o
    _o += _w
NPC = _o
CI = {"ident": 0, "ones": 1, "iota_p": 2, "reset_p": 6, "iota_s": 10, "reset_s": 11, "rowmask": 12, "headblk": 13}
NCST = 14


def _rw_blockcols(v):
    out = np.zeros((128, 16), np.float32)
    for i, (o, w) in enumerate(RW_BLOCKS):
        out[:w, i] = v[o:o + w]
    return out


def host_inputs(inputs, core):
    p = core // 2
    sl = slice(NSEQ_S * core, NSEQ_S * (core + 1))
    m = {}
    m["xp"] = np.ascontiguousarray(inputs["x_prompt"][p])
    m["xs"] = np.ascontiguousarray(inputs["x_sample"][sl].reshape(128, D))
    m["w_in"] = inputs["w_in"]
    G, P, GS = 32, 64, 16
    bB = np.zeros((DEPTH, 2, 512, 2048), np.float32)
    bC = np.zeros((DEPTH, 2, 2048, 512), np.float32)
    for g in range(G):
        for ri, (bk, ck) in enumerate((("s5_b_re", "s5_c_re"), ("s5_b_im", "s5_c_im"))):
            bB[:, ri, g * GS:(g + 1) * GS, g * P:(g + 1) * P] = inputs[bk][:, g].transpose(0, 2, 1)
            bC[:, ri, g * P:(g + 1) * P, g * GS:(g + 1) * GS] = inputs[ck][:, g].transpose(0, 2, 1)
    m["s5B"], m["s5C"] = bB, bC
    s5 = np.stack([inputs["state_s5_re"][:, sl], inputs["state_s5_im"][:, sl]], axis=2)
    m["st_s5"] = np.ascontiguousarray(s5.reshape(DEPTH, NSEQ_S, 2, 16, 128).transpose(0, 4, 2, 3, 1))
    m["s5_w_glu"] = inputs["s5_w_glu"]
    for k in ("w_br_s5", "w_br_rwkv", "w_br_gdn", "w_out", "ffn_w_up", "ffn_w_down"):
        m[k] = inputs[k]
    fc = inputs["state_ffn_conv"][:, sl]
    m["st_fconv"] = np.ascontiguousarray(fc.reshape(DEPTH, NSEQ_S, 2, 88, 128).transpose(0, 4, 3, 1, 2))
    pc = np.zeros((DEPTH, 128, NPC), np.float32)
    for l in range(DEPTH):
        pc[l, :, PO["g1"]:PO["g1"] + 16] = _col(inputs["norm1_g"][l])
        pc[l, :, PO["g2"]:PO["g2"] + 16] = _col(inputs["norm2_g"][l])
        pc[l, :, PO["gf"]:PO["gf"] + 16] = _col(inputs["final_norm_g"])
        pc[l, :, PO["mu"]:PO["mu"] + 16] = _rw_blockcols(inputs["rwkv_mu"][l])
        pc[l, :, PO["fcw"]:PO["fcw"] + 264] = _col(inputs["ffn_conv_w"][l]).transpose(1, 0, 2).reshape(128, 264)
        pc[l, :, PO["fcb"]:PO["fcb"] + 88] = _col(inputs["ffn_conv_b"][l])
        pc[l, :, PO["lre"]:PO["lre"] + 16] = _col(inputs["s5_lambda_re"][l].reshape(-1))
        pc[l, :, PO["lim"]:PO["lim"] + 16] = _col(inputs["s5_lambda_im"][l].reshape(-1))
        pc[l, :, PO["lst"]:PO["lst"] + 16] = _col(np.repeat(inputs["s5_log_step"][l], 64))
        pc[l, :, PO["s5d"]:PO["s5d"] + 4] = _col(inputs["s5_d"][l])
        pc[l, :, PO["gcw"]:PO["gcw"] + 96] = _col(inputs["gdn_conv_w"][l]).transpose(1, 0, 2).reshape(128, 96)
        pc[l, :, PO["gng"]] = inputs["gdn_norm_g"][l]
        pc[l, :, PO["nw0"]:PO["nw0"] + 4] = _col(inputs["rwkv_w0"][l])
        pc[l, :, PO["a0"]:PO["a0"] + 4] = _col(inputs["rwkv_a0"][l])
        pc[l, :, PO["kk"]:PO["kk"] + 4] = _col(inputs["rwkv_k_k"][l])
        pc[l, :, PO["ka"]:PO["ka"] + 4] = _col(inputs["rwkv_k_a"][l])
        pc[l, :, PO["rk"]:PO["rk"] + 4] = _col(inputs["rwkv_r_k"][l].reshape(-1))
        pc[l, :, PO["lnw"]:PO["lnw"] + 4] = _col(inputs["rwkv_ln_w"][l])
        pc[l, :, PO["lnb"]:PO["lnb"] + 4] = _col(inputs["rwkv_ln_b"][l])
    m["pcol"] = pc
    cs = np.zeros((NCST, 128, 128), np.float32)
    cs[CI["ident"]] = np.eye(128)
    cs[CI["ones"]] = 1.0
    cs[CI["iota_p"]:CI["iota_p"] + 4] = (np.arange(512, dtype=np.float32) + 1).reshape(4, 1, 128)
    rp = np.ones(512, np.float32); rp[0] = 0
    cs[CI["reset_p"]:CI["reset_p"] + 4] = rp.reshape(4, 1, 128)
    cs[CI["rowmask"]][:, :NSEQ_S] = (np.arange(128)[:, None] // LS == np.arange(NSEQ_S)[None, :])
    hb = np.arange(128) // 64
    cs[CI["headblk"]] = (hb[:, None] == hb[None, :])
    cs[CI["iota_s"]] = (np.arange(128) % LS + 1).astype(np.float32)[None, :]
    cs[CI["reset_s"]] = (np.arange(128) % LS != 0).astype(np.float32)[None, :]
    m["cst"] = cs
    ck = np.zeros((2, 6, 128, 128), np.float32)
    for ki, kd in enumerate(("p", "s")):
        mm = _masks(kd)
        for j, nm in enumerate(("MUi", "MUs", "MLs", "NEGU", "POSL")):
            ck[ki, j] = mm[nm]
        idx = np.arange(128) // LS if kd == "s" else np.zeros(128, np.int64)
        ck[ki, 5] = (idx[:, None] == idx[None, :])
    m["cstk"] = ck
    pr = np.zeros((DEPTH, 128, 16), np.float32)
    pr[:, :, 0:8] = inputs["gdn_a_log"][:, None, :]
    pr[:, :, 8:16] = inputs["gdn_dt_bias"][:, None, :]
    m["prow"] = pr
    for k_ in ("rwkv_w2", "rwkv_a2", "rwkv_g2"):
        m[k_] = inputs[k_]
    wk = inputs["state_rwkv_wkv"][:, sl]
    m["st_wkv"] = np.ascontiguousarray(wk.transpose(0, 2, 4, 1, 3).reshape(DEPTH, 4, 128, NSEQ_S, 64))
    m["st_gdn"] = np.ascontiguousarray(inputs["state_gdn"][:, sl].transpose(0, 2, 3, 1, 4))
    sh = inputs["state_rwkv_shift"][:, sl]
    t = np.zeros((DEPTH, 128, 16, NSEQ_S, 1), np.float32)
    for i, (o, w) in enumerate(RW_BLOCKS):
        t[:, :w, i, :, 0] = sh[:, :, o:o + w].transpose(0, 2, 1)
    m["st_shift"] = t
    gc = inputs["state_gdn_conv"][:, sl]
    m["st_gconv"] = np.ascontiguousarray(gc.reshape(DEPTH, NSEQ_S, 3, 24, 128).transpose(0, 4, 3, 1, 2))
    return m


_CACHE = {}


def _unblock_shift(a):
    nb = a.shape[3]
    out = np.zeros((DEPTH, nb, 1984), np.float32)
    for i, (o, w) in enumerate(RW_BLOCKS):
        out[:, :, o:o + w] = a[:, :w, i, :, 0].transpose(0, 2, 1)
    return out


def _unchunk(a):
    l, p, c, nb, h = a.shape
    return np.ascontiguousarray(a.transpose(0, 3, 4, 2, 1).reshape(l, nb, h, c * p))


def _uns5(a, ri):
    nb = a.shape[4]
    return np.ascontiguousarray(a[:, :, ri].transpose(0, 3, 2, 1).reshape(DEPTH, nb, 32, 64))


def _unwkv(a):
    nb = a.shape[3]
    return np.ascontiguousarray(a.reshape(DEPTH, 4, 2, 64, nb, 64).transpose(0, 4, 1, 2, 5, 3).reshape(DEPTH, nb, 8, 64, 64))


def _ungdn(a):
    return np.ascontiguousarray(a.transpose(0, 3, 1, 2, 4))


def kernel(**inputs):
    inputs = {k: np.asarray(v) for k, v in inputs.items()}
    if "nc" not in _CACHE:
        _CACHE["nc"] = build_program()
    nc, S = _CACHE["nc"]
    in_maps = [host_inputs(inputs, c) for c in range(8)]
    res = run_bass_kernel_spmd(nc, in_maps, core_ids=list(range(8))).results
    B = 4
    P = [res[2 * p] for p in range(B)]
    cat = lambda xs: np.concatenate(xs, axis=1)
    y_p = np.stack([r["y_p"] for r in P])
    y_s = np.concatenate([r["y_s"].reshape(NSEQ_S, LS, D) for r in res])
    outs = [y_p, y_s]
    for grp, key in ((P, "p"), (res, "s")):
        outs += [cat([_uns5(r["o_s5_" + key], 0) for r in grp]),
                 cat([_uns5(r["o_s5_" + key], 1) for r in grp]),
                 cat([_unblock_shift(r["o_shift_" + key]) for r in grp]),
                 cat([_unwkv(r["o_wkv_" + key]) for r in grp]),
                 cat([_unchunk(r["o_gconv_" + key]) for r in grp]),
                 cat([_ungdn(r["o_gdn_" + key]) for r in grp]),
                 cat([_unchunk(r["o_fconv_" + key]) for r in grp])]
    return tuple(outs)
```

```python
import numpy as np
from contextlib import ExitStack
import concourse.bass as bass
import concourse.mybir as mybir
from concourse.bass_utils import run_bass_kernel_spmd
from concourse.alu_op_type import AluOpType as ALU

AF = mybir.ActivationFunctionType
F32 = mybir.dt.float32
BF16 = mybir.dt.bfloat16
I32 = mybir.dt.int32
F32R = mybir.dt.float32r

D = 2048
DEPTH = 2
SEQ = 2048
NSEQ_S = 16
LS = 8
IN_COLS = 12752
DFF = 5632
C_S5, C_RW, C_QKV, C_Z, C_B, C_A, C_G = 0, 512, 2496, 5568, 6592, 6600, 6608
RW_BLOCKS = [(i * 128, 128) for i in range(12)] + [(1536, 96), (1632, 96), (1728, 128), (1856, 128)]
BIG = 30000.0
TWO_PI = 6.283185307179586
PI = 3.141592653589793


class Sched:
    ENGS = ("pe", "act", "dve", "pool", "sp")
    ND = 24
    EPOCH = 30000
    NEP = 6

    def __init__(self):
        self.prog = {e: [] for e in self.ENGS}
        self.cnt = {e: 0 for e in self.ENGS}
        self.waited = {e: {} for e in self.ENGS}
        self.lastw = {}
        self.readers = {}
        self.dma_cnt = [0] * self.ND
        self.dma_rr = 0
        self.semh = None
        self.n_instr = 0
        self.pending = {e: {} for e in self.ENGS}
        self.log = None
        self.sfx = ""
        self.sfx2 = ""
        self.private2 = set()
        self._stack = []
        self.f32r = False
        self.alias = {}
        self.private = set()
        self._rec = None

    def _deps(self, eng, reads, writes, px=()):
        need = {}

        def add(tok):
            sk, v = tok
            if eng == "pe" and sk[0] == "pe":
                return
            if v > need.get(sk, 0):
                need[sk] = v
        if self.pending[eng]:
            for sk, v in self.pending[eng].items():
                add((sk, v))
            self.pending[eng] = {}
        for r in tuple(reads) + tuple(writes):
            if r in self.lastw:
                add(self.lastw[r])
        for r in writes:
            for sk, v in self.readers.get(r, {}).items():
                add((sk, v))
        for r in px:
            for sk, v in self.readers.get(r, {}).items():
                if sk[0] != eng:
                    add((sk, v))
        out = []
        for sk, v in need.items():
            if v > self.waited[eng].get(sk, 0):
                self.waited[eng][sk] = v
                out.append((sk, v))
        return out

    def barrier(self):
        snap = {}
        for e in ("pe", "act", "dve", "pool"):
            if self.cnt[e]:
                sk, v = self._tok(e, self.cnt[e])
                snap[sk] = v
        for i in range(self.ND):
            if self.dma_cnt[i]:
                snap[("dma", i)] = self.dma_cnt[i]
        for e in self.ENGS:
            self.pending[e] = dict(snap)

    def _tok(self, eng, c):
        return ((eng, (c - 1) // self.EPOCH), (c - 1) % self.EPOCH + 1)

    def _commit(self, tok, reads, writes):
        for r in writes:
            self.lastw[r] = tok
            self.readers[r] = {}
        for r in reads:
            if r not in writes:
                d = self.readers.setdefault(r, {})
                if tok[1] > d.get(tok[0], 0):
                    d[tok[0]] = tok[1]

    def _nm(self, names):
        if not self.sfx and not self.sfx2:
            return list(names)
        al = self.alias
        out = []
        for r in names:
            if r in self.private:
                out.append(al.get(r, r) + self.sfx2 + self.sfx)
            elif r in self.private2:
                out.append(r + self.sfx2)
            else:
                out.append(r)
        return out

    def thread_begin(self):
        self._stack.append(self._rec)
        self._rec = []

    def thread_end(self):
        r = self._rec
        self._rec = self._stack.pop()
        return r

    def replay(self, threads):
        idx = [0] * len(threads)
        sfx, self.sfx = self.sfx, ""
        sfx2, self.sfx2 = self.sfx2, ""
        while any(idx[i] < len(t) for i, t in enumerate(threads)):
            for i, t in enumerate(threads):
                if idx[i] < len(t):
                    kind, a, kw = t[idx[i]]
                    idx[i] += 1
                    (self.op if kind == "op" else self.dma)(*a, **kw)
        self.sfx = sfx
        self.sfx2 = sfx2

    def op(self, eng, meth, *args, reads=(), writes=(), inc=True, **kw):
        reads, writes = self._nm(reads), self._nm(writes)
        if not inc:
            assert eng == "pe"
            if self._rec is not None:
                self._rec.append(("op", (eng, meth) + tuple(args), dict(reads=reads, writes=writes, inc=False, **kw)))
                return None
            fn0 = lambda e: getattr(e, meth)(*args, **kw)
            px0 = [r for r in reads if r.startswith("PS")]
            waits0 = self._deps(eng, reads, writes, px0)
            tok0 = self._tok(eng, self.cnt[eng] + 1)
            self.n_instr += 1

            def run0(e, waits=waits0, fn=fn0):
                for sk, wv in waits:
                    e.wait_ge(self.semh[sk], wv)
                fn(e)
            self.prog[eng].append(run0)
            self._commit(tok0, reads, writes)
            return tok0
        if meth == "matmul" and self.f32r and args[1].dtype == F32 and args[2].dtype == F32 \
                and args[0].base_partition() == 0 and args[2].shape[-1] % 2 == 0:
            args = (args[0], args[1].bitcast(F32R), args[2].bitcast(F32R)) + tuple(args[3:])
        if self._rec is not None:
            self._rec.append(("op", (eng, meth) + tuple(args), dict(reads=reads, writes=writes, **kw)))
            return None
        fn = lambda e: getattr(e, meth)(*args, **kw)
        px = [r for r in reads if r.startswith("PS")]
        waits = self._deps(eng, reads, writes, px)
        self.cnt[eng] += 1
        tok = self._tok(eng, self.cnt[eng])
        self.n_instr += 1

        def run(e, waits=waits, fn=fn, mysem=tok[0]):
            for sk, wv in waits:
                e.wait_ge(self.semh[sk], wv)
            fn(e).then_inc(self.semh[mysem], 1)
        self.prog[eng].append(run)
        self._commit(tok, reads, writes)
        if self.log is not None:
            self.log.append((eng, meth, tok, list(waits), list(reads), list(writes)))
        return tok

    def dma(self, queue, out, in_, reads=(), writes=()):
        reads, writes = self._nm(reads), self._nm(writes)
        if self._rec is not None:
            self._rec.append(("dma", (queue,), dict(out=out, in_=in_, reads=reads, writes=writes)))
            return None
        fn = lambda e: e.dma_start(out=out, in_=in_)
        waits = self._deps(queue, reads, writes)
        i = self.dma_rr
        self.dma_rr = (i + 1) % self.ND
        sk = ("dma", i)
        prev = self.dma_cnt[i]
        if prev > self.waited[queue].get(sk, 0):
            self.waited[queue][sk] = prev
            waits = waits + [(sk, prev)]
        self.dma_cnt[i] += 16
        tok = (sk, self.dma_cnt[i])
        self.n_instr += 1

        def run(e, waits=waits, fn=fn, sk=sk):
            for s2, wv in waits:
                e.wait_ge(self.semh[s2], wv)
            fn(e).then_inc(self.semh[sk], 16)
        self.prog[queue].append(run)
        self._commit(tok, reads, writes)
        if self.log is not None:
            self.log.append((queue, "dma", tok, list(waits), list(reads), list(writes)))
        return tok

    def final_wait(self, queue):
        sems = dict(self.semh)

        def run(e):
            for i in range(self.ND):
                if self.dma_cnt[i]:
                    e.wait_ge(sems[("dma", i)], self.dma_cnt[i])
            for g in ("pe", "act", "dve", "pool"):
                if self.cnt[g]:
                    sk, v = self._tok(g, self.cnt[g])
                    e.wait_ge(sems[sk], v)
        self.prog[queue].append(run)


def _col(v, width=128):
    v = np.asarray(v, np.float32)
    n = v.shape[-1] // width
    return np.ascontiguousarray(v.reshape(v.shape[:-1] + (n, width)).swapaxes(-1, -2))


def _masks(kind):
    idx = np.arange(128)
    seq = idx // 8 if kind == "s" else np.zeros(128, np.int64)
    same = (seq[:, None] == seq[None, :])
    p, f = idx[:, None], idx[None, :]
    m = {}
    m["MUi"] = (same & (p <= f)).astype(np.float32)
    m["MUs"] = (same & (p < f)).astype(np.float32)
    m["MLs"] = (same & (p > f)).astype(np.float32)
    m["NEGU"] = np.where(same & (p <= f), 0.0, -BIG).astype(np.float32)
    m["POSL"] = np.where(same & (p >= f), 0.0, BIG).astype(np.float32)
    return m


class Ctx:
    pass


def build_program(stage=99):
    nc = bass.Bass("TRN2", target_bir_lowering=False)
    S = Sched()
    import os as _os0
    if _os0.environ.get("MK_LOG"):
        S.log = []
    S.f32r = _os0.environ.get("MK_F32R", "0") == "1"
    es = ExitStack()
    K = Ctx()

    def din(name, shape):
        return nc.dram_tensor(name, list(shape), F32, kind="ExternalInput").ap()

    def dout(name, shape):
        return nc.dram_tensor(name, list(shape), F32, kind="ExternalOutput").ap()

    xp = din("xp", [SEQ, D]); xs = din("xs", [128, D])
    w_in = din("w_in", [DEPTH, D, IN_COLS])
    w_brs = [din("w_br_s5", [DEPTH, 512, D]), din("w_br_rwkv", [DEPTH, 512, D]), din("w_br_gdn", [DEPTH, 1024, D])]
    w_out = din("w_out", [DEPTH, D, D])
    w_glu = din("s5_w_glu", [DEPTH, 512, 512])
    cstk = din("cstk", [2, 6, 128, 128])
    NWC = 520
    WCT = [nc.dram_tensor("wcache%d" % i_, [130, 128, 4096], BF16, kind="Internal").ap() for i_ in range(4)]

    class _WC:
        def __getitem__(self, key):
            sl_i = key[0]
            return WCT[sl_i // 130][(sl_i % 130,) + tuple(key[1:])]
    WC = _WC()
    wc_slots = {}
    w_lora = [din("rwkv_w2", [DEPTH, 96, 512]), din("rwkv_a2", [DEPTH, 96, 512]), din("rwkv_g2", [DEPTH, 256, 512])]
    st_wkv = din("st_wkv", [DEPTH, 4, 128, NSEQ_S, 64])
    o_wkv = {"p": dout("o_wkv_p", [DEPTH, 4, 128, 1, 64]), "s": dout("o_wkv_s", [DEPTH, 4, 128, NSEQ_S, 64])}
    prow = din("prow", [DEPTH, 128, 16])
    st_gdn = din("st_gdn", [DEPTH, 8, 128, NSEQ_S, 128])
    o_gdn = {"p": dout("o_gdn_p", [DEPTH, 8, 128, 1, 128]), "s": dout("o_gdn_s", [DEPTH, 8, 128, NSEQ_S, 128])}
    s5B = din("s5B", [DEPTH, 2, 512, 2048])
    s5C = din("s5C", [DEPTH, 2, 2048, 512])
    st_s5 = din("st_s5", [DEPTH, 128, 2, 16, NSEQ_S])
    o_s5 = {"p": dout("o_s5_p", [DEPTH, 128, 2, 16, 1]), "s": dout("o_s5_s", [DEPTH, 128, 2, 16, NSEQ_S])}
    w_up = din("ffn_w_up", [DEPTH, D, 2 * DFF])
    w_down = din("ffn_w_down", [DEPTH, DFF, D])
    st_fconv = din("st_fconv", [DEPTH, 128, 88, NSEQ_S, 2])
    o_fconv = {"p": dout("o_fconv_p", [DEPTH, 128, 88, 1, 2]), "s": dout("o_fconv_s", [DEPTH, 128, 88, NSEQ_S, 2])}
    pcol = din("pcol", [DEPTH, 128, NPC])
    cst = din("cst", [NCST, 128, 128])
    st_shift = din("st_shift", [DEPTH, 128, 16, NSEQ_S, 1])
    st_gconv = din("st_gconv", [DEPTH, 128, 24, NSEQ_S, 3])
    o_y = {"p": dout("y_p", [SEQ, D]), "s": dout("y_s", [128, D])}
    o_shift = {"p": dout("o_shift_p", [DEPTH, 128, 16, 1, 1]), "s": dout("o_shift_s", [DEPTH, 128, 16, NSEQ_S, 1])}
    o_gconv = {"p": dout("o_gconv_p", [DEPTH, 128, 24, 1, 3]), "s": dout("o_gconv_s", [DEPTH, 128, 24, NSEQ_S, 3])}

    with es:
        def sb(name, shape, dt=F32):
            return es.enter_context(nc.sbuf_tensor(name, list(shape), dt))

        def ps(name):
            return es.enter_context(nc.psum_tensor(name, [128, 512], F32))

        X = sb("X", [128, 16, 512])
        HN = sb("HN", [128, 16, 512], BF16)
        WS = [sb("WS%d" % i, [128, 4096]) for i in range(2)]
        WB = [sb("WB%d" % i, [128, 4096], BF16) for i in range(2)]
        PC = sb("PC", [128, DEPTH, NPC])
        CS = sb("CS", [128, NCST, 128])
        NA = 14144
        AR = sb("AR", [128, NA])

        class View:
            def __init__(self, off, size, name):
                self.off, self.size, self.name = off, size, name

            def __getitem__(self, key):
                return AR[:, self.off:self.off + self.size][key]
        EXT = [View(i * 520, 520, "EXT%d" % i) for i in range(8)]
        STT = View(4160, 2816, "STT")
        GT = [View(6976 + i * 512, 512, "GT%d" % i) for i in range(6)]
        ACC = [View(10048 + i * 512, 512, "ACC%d" % i) for i in range(2)]
        TMP = [View(11072 + i * 512, 512, "TMP%d" % i) for i in range(2)]
        XT = View(12096, 2048, "XT")
        SQ = [View(10048 + i * 512, 512, "SQ%d" % i) for i in range(2)]
        RSTD = View(11072, 512, "RSTD")
        YBR = sb("YBR", [128, 16, 512], BF16)
        MRG = sb("MRG", [128, 16, 512], BF16)
        AJ = MRG[:, 12:16, :]
        TI = sb("TI", [128, 512], I32)
        CK = sb("CK", [128, 6, 128])
        WBA = sb("WBA", [128, 16, 16], BF16)
        PR = sb("PR", [128, DEPTH, 16])
        S5ST = sb("S5ST", [128, 2 * 16 * NSEQ_S])
        S5P = sb("S5P", [128, DEPTH, 6, 16])
        SB5 = [sb("SB5_%d" % i, [128, 2, 128]) for i in range(2)]
        PSD = [ps("PSD0"), ps("PSD1")]
        PSM = [ps("PSM%d" % i) for i in range(6)]
        semh = {}
        for e in ("pe", "act", "dve", "pool"):
            for ep in range(S.NEP if e != "pool" else 1):
                semh[(e, ep)] = es.enter_context(nc.semaphore("s_%s%d" % (e, ep)))
        for i in range(S.ND):
            semh[("dma", i)] = es.enter_context(nc.semaphore("s_dma%d" % i))
        S.semh = semh
        ident = CS[:, CI["ident"], :]
        ones = CS[:, CI["ones"], :]

        S.dma("sp", out=PC[:], in_=pcol.rearrange("l p c -> p l c"), writes=["PC"])
        S.dma("sp", out=CS[:], in_=cst.rearrange("k p f -> p k f"), writes=["CS"])
        S.dma("sp", out=PR[:], in_=prow.rearrange("l p c -> p l c"), writes=["PR"])


        def wrap_sin(dst, src, n_, tA, tB, eng="dve", srcn=(), dstn=()):
            S.op(eng, "tensor_scalar", tA, src, 1.0 / TWO_PI, None, ALU.mult, reads=list(srcn) + ["TA", "TB"], writes=["wrapA", "TA"])
            S.op(eng, "tensor_copy", TI[:, 0:n_], tA, reads=["wrapA"], writes=["TI"])
            S.op(eng, "tensor_copy", tA, TI[:, 0:n_], reads=["TI"], writes=["wrapA"])
            S.op(eng, "scalar_tensor_tensor", tB, tA, -TWO_PI, src, ALU.mult, ALU.add, reads=["wrapA"] + list(srcn), writes=["wrapB", "TB"])
            S.op(eng, "tensor_scalar", tB, tB, -PI, PI, ALU.max, ALU.min, reads=["wrapB"], writes=["wrapB"])
            S.op("act", "activation", dst, tB, AF.Sin, reads=["wrapB"], writes=["wrapD"] + list(dstn))

        S.barrier()
        for l in range(DEPTH):
            lre = PC[:, l, PO["lre"]:PO["lre"] + 16]
            lim = PC[:, l, PO["lim"]:PO["lim"] + 16]
            lst = PC[:, l, PO["lst"]:PO["lst"] + 16]
            t = [AR[:, i * 16:(i + 1) * 16] for i in range(12)]
            rho, th, cr, ci, ncr = (S5P[:, l, i, :] for i in range(5))
            S.op("act", "activation", t[0], lst, AF.Exp, reads=["PC"], writes=["p5"])
            S.op("dve", "tensor_tensor", t[1], lre, t[0], ALU.mult, reads=["p5", "PC"], writes=["p5"])
            S.op("act", "activation", rho, t[1], AF.Exp, reads=["p5"], writes=["S5P"])
            S.op("dve", "tensor_tensor", th, lim, t[0], ALU.mult, reads=["p5", "PC"], writes=["S5P"])
            S.barrier()
            wrap_sin(t[2], th, 16, t[8], t[9])
            S.op("dve", "tensor_scalar", t[3], th, PI / 2, None, ALU.add, reads=["S5P"], writes=["p5"])
            S.barrier()
            wrap_sin(t[4], t[3], 16, t[8], t[9])
            S.barrier()
            S.op("dve", "tensor_tensor", t[5], rho, t[4], ALU.mult, writes=["p5"])
            S.op("dve", "tensor_scalar", t[5], t[5], -1.0, None, ALU.add, writes=["p5"])
            S.op("dve", "tensor_tensor", t[6], rho, t[2], ALU.mult, writes=["p5"])
            S.op("dve", "tensor_tensor", t[7], lre, lre, ALU.mult, writes=["p5"])
            S.op("dve", "tensor_tensor", t[10], lim, lim, ALU.mult, writes=["p5"])
            S.op("dve", "tensor_tensor", t[7], t[7], t[10], ALU.add, writes=["p5"])
            S.op("dve", "reciprocal", t[7], t[7], writes=["p5"])
            S.op("dve", "tensor_tensor", t[10], t[5], lre, ALU.mult, writes=["p5"])
            S.op("dve", "tensor_tensor", t[11], t[6], lim, ALU.mult, writes=["p5"])
            S.op("dve", "tensor_tensor", t[10], t[10], t[11], ALU.add, writes=["p5"])
            S.op("dve", "tensor_tensor", cr, t[10], t[7], ALU.mult, writes=["p5", "S5P"])
            S.op("dve", "tensor_tensor", t[10], t[6], lre, ALU.mult, writes=["p5"])
            S.op("dve", "tensor_tensor", t[11], t[5], lim, ALU.mult, writes=["p5"])
            S.op("dve", "tensor_tensor", t[10], t[10], t[11], ALU.subtract, writes=["p5"])
            S.op("dve", "tensor_tensor", ci, t[10], t[7], ALU.mult, writes=["p5", "S5P"])
            S.op("dve", "tensor_scalar", ncr, cr, -1.0, None, ALU.mult, writes=["p5", "S5P"])
            S.barrier()

        dense_rr = [0]
        w_rr = [0]
        ws_busy = [False]
        deep_pf = [False]
        wx_rr = [0]
        WBX = [(WB[0], "WB0"), (WB[1], "WB1")]
        for i_ in range(2):
            v_ = WS[i_][:, :].bitcast(BF16)
            WBX += [(v_[:, 0:4096], "WS%da" % i_), (v_[:, 4096:8192], "WS%db" % i_)]

        def linear(wd, row0, nrows, groups, act_t, n, consume, kc0=0, aname=None, wk=None):
            KC = max(1, nrows // 128)
            kp = min(nrows, 128)
            for (col0, ncols, subs) in groups:
                i = w_rr[0] % 2
                w_rr[0] += 1
                ws, wb = WS[i], WB[i]
                m_ = KC * ncols
                ckey = (wk, row0, col0, ncols) if wk is not None else None
                wbn = "WB%d" % i
                if ckey is not None and ckey in wc_slots:
                    sl_i = wc_slots[ckey]
                    if deep_pf[0] and not ws_busy[0]:
                        wb, wbn = WBX[wx_rr[0] % 6]
                        wx_rr[0] += 1
                    S.dma("sp", out=wb[0:kp, 0:m_], in_=WC[sl_i, 0:kp, 0:m_], reads=["WC%d" % sl_i], writes=[wbn])
                else:
                    assert not ws_busy[0], "uncached weight block while staging buffers host mixer slots"
                    src = wd[row0:row0 + nrows, col0:col0 + ncols].rearrange("(kc p) n -> p kc n", p=kp)
                    dst = ws[0:kp, 0:m_].rearrange("p (kc n) -> p kc n", kc=KC)
                    S.dma("sp", out=dst, in_=src, writes=["WS%d" % i])
                    assert not deep_pf[0]
                    S.op("act", "activation", wb[0:kp, 0:m_], ws[0:kp, 0:m_], AF.Copy,
                         reads=["WS%d" % i], writes=[wbn])
                    if ckey is not None and len(wc_slots) < NWC:
                        sl_i = len(wc_slots)
                        wc_slots[ckey] = sl_i
                        S.dma("act", out=WC[sl_i, 0:kp, 0:m_], in_=wb[0:kp, 0:m_], reads=[wbn], writes=["WC%d" % sl_i])
                wv = wb[:, 0:KC * ncols].rearrange("p (kc n) -> p kc n", kc=KC)
                for (off, w) in subs:
                    j = dense_rr[0] % 2
                    dense_rr[0] += 1
                    pt = PSD[j]
                    for kc in range(KC):
                        S.op("pe", "matmul",
                            pt[0:w, 0:n], wv[0:kp, kc, off:off + w], act_t[0:kp, kc0 + kc, 0:n], start=(kc == 0), stop=(kc == KC - 1),
                            inc=(kc == KC - 1),
                            reads=[wbn, aname or act_t.name], writes=["PSD%d" % j])
                    consume(col0 + off, w, pt[0:w, 0:n], "PSD%d" % j)

        def rmsnorm(gcol, n, dst, dstname):
            pn = PSM[0]
            S.barrier()
            for c in range(16):
                q = SQ[c % 2]
                S.op("act", "activation", q[:, 0:n], X[:, c, 0:n], AF.Square,
                     reads=["X"], writes=["SQ%d" % (c % 2)])
                S.op("pe", "matmul", pn[:, 0:n], ones, q[:, 0:n], start=(c == 0), stop=(c == 15),
                     reads=["SQ%d" % (c % 2), "CS"], writes=["PSM0"])
            S.op("act", "activation", SQ[0][:, 0:n], pn[:, 0:n], AF.Sqrt, bias=1e-6, scale=1.0 / D,
                 reads=["PSM0"], writes=["SQ0"])
            S.op("dve", "reciprocal", RSTD[:, 0:n], SQ[0][:, 0:n], reads=["SQ0"], writes=["RSTD"])
            for c in range(16):
                S.op("dve", "scalar_tensor_tensor", dst[:, c, 0:n], X[:, c, 0:n], gcol[:, c:c + 1],
                                                                   RSTD[:, 0:n], ALU.mult, ALU.mult,
                     reads=["X", "RSTD", "PC"], writes=[dstname])
            S.barrier()

        def final_out(kind, ti, n):
            pn = PSM[0]
            S.barrier()
            for c in range(16):
                q = SQ[c % 2]
                S.op("act", "activation", q[:, 0:n], X[:, c, 0:n], AF.Square, reads=["X"], writes=["SQ%d" % (c % 2)])
                S.op("pe", "matmul", pn[:, 0:n], ones, q[:, 0:n], start=(c == 0), stop=(c == 15),
                     reads=["SQ%d" % (c % 2), "CS"], writes=["PSM0"])
            S.op("act", "activation", SQ[0][:, 0:n], pn[:, 0:n], AF.Sqrt, bias=1e-6, scale=1.0 / D, reads=["PSM0"], writes=["SQ0"])
            S.op("dve", "reciprocal", RSTD[:, 0:n], SQ[0][:, 0:n], reads=["SQ0"], writes=["RSTD"])
            gf = PC[:, 0, PO["gf"]:PO["gf"] + 16]
            for c in range(16):
                S.op("dve", "scalar_tensor_tensor", X[:, c, 0:n], X[:, c, 0:n], gf[:, c:c + 1], RSTD[:, 0:n], ALU.mult, ALU.mult,
                     reads=["X", "RSTD", "PC"], writes=["X"])
            S.barrier()
            ydst = o_y[kind]
            for s_ in range(n // 128):
                for cg in range(4):
                    pt = PSM[1 + cg % 2]
                    for c4 in range(4):
                        c = cg * 4 + c4
                        S.op("pe", "transpose", pt[:, c4 * 128:(c4 + 1) * 128], X[:, c, s_ * 128:(s_ + 1) * 128], ident,
                             reads=["X", "CS"], writes=["PSM%d" % (1 + cg % 2)])
                    S.op("act", "activation", XT[:, cg * 512:(cg + 1) * 512], pt[:, :], AF.Copy,
                         reads=["PSM%d" % (1 + cg % 2)], writes=["XT"])
                r0 = ti * 512 + s_ * 128
                S.dma("act", out=ydst[r0:r0 + 128, :], in_=XT[:], reads=["XT"], writes=["o_y"])

        tiles = [("p", t) for t in range(4)] + [("s", 0)]
        import os
        if os.environ.get("MK_TILES"):
            tiles = [tiles[int(i)] for i in os.environ["MK_TILES"].split(",")]
        for (kind, ti) in tiles:
            n = 512 if kind == "p" else 128
            nb = 1 if kind == "p" else NSEQ_S
            L = n // nb
            xsrc = xp[ti * 512:(ti + 1) * 512, :] if kind == "p" else xs
            if kind == "s" or ti > 0:
                S.barrier()
                deep_pf[0] = True
            S.barrier()
            S.dma("sp", out=CK[:], in_=cstk[0 if kind == "p" else 1].rearrange("k p f -> p k f"), writes=["CK"])
            for s_ in range(n // 128):
                S.dma("sp", out=XT[:], in_=xsrc[s_ * 128:(s_ + 1) * 128, :], writes=["XT"])
                for cg in range(4):
                    pt = PSM[1 + cg % 2]
                    for c4 in range(4):
                        c = cg * 4 + c4
                        S.op("pe", "transpose", pt[:, c4 * 128:(c4 + 1) * 128],
                                                                           XT[:, c * 128:(c + 1) * 128], ident,
                             reads=["XT", "CS"], writes=["PSM%d" % (1 + cg % 2)])
                    S.op("dve", "tensor_copy",
                        X[:, cg * 4:cg * 4 + 4, s_ * 128:(s_ + 1) * 128], pt[:, :].rearrange("p (c t) -> p c t", c=4),
                        reads=["PSM%d" % (1 + cg % 2)], writes=["X"])
            for l in range(DEPTH):
                rmsnorm(PC[:, l, PO["g1"]:PO["g1"] + 16], n, HN, "HN")
                first = (kind == "p" and ti == 0)

                def halo_proj(wd, col_base, blocks, H, st_in, st_out, oname, after, nbuf=2, sblk=None, wk=None):
                    nblk = len(blocks) if sblk is None else sblk[1]
                    sidx = list(range(len(blocks))) if sblk is None else (
                        sblk[0] if isinstance(sblk[0], (list, tuple)) else [sblk[0] + i_ for i_ in range(len(blocks))])
                    sv = STT[:, 0:nblk * nb * H].rearrange("p (c b h) -> p c b h", c=nblk, b=nb)
                    if st_in is None:
                        pass
                    elif first:
                        S.op("dve", "memset", STT[:, 0:nblk * nb * H], 0.0, writes=["STT"])
                    else:
                        S.dma("act", out=sv, in_=st_in, reads=[oname], writes=["STT"])
                    blk_idx = {col_base + o: i for i, (o, w) in enumerate(blocks)}
                    nbl = len(blocks)

                    def consume(col, w, pap, pres):
                        bi = blk_idx[col]
                        ex = EXT[bi % nbuf]
                        ev = ex[:, 0:nb * (H + L)].rearrange("p (b t) -> p b t", b=nb)
                        S.op("dve", "tensor_copy", ev[0:w, :, 0:H], sv[0:w, sidx[bi], :, :],
                             reads=["STT"], writes=["EXT%d" % (bi % nbuf)])
                        S.op("act", "activation", ev[0:w, :, H:H + L], pap.rearrange("p (b t) -> p b t", b=nb), AF.Copy,
                             reads=[pres], writes=["EXT%d" % (bi % nbuf)])
                        S.op("dve", "tensor_copy", sv[0:w, sidx[bi], :, :], ev[0:w, :, L:L + H],
                             reads=["EXT%d" % (bi % nbuf)], writes=["STT"])
                        after(bi, w, ev, "EXT%d" % (bi % nbuf))
                    groups = []
                    i = 0
                    while i < nbl:
                        o0, w0 = blocks[i]
                        if i + 1 < nbl and blocks[i + 1][0] == o0 + w0:
                            o1, w1 = blocks[i + 1]
                            groups.append((col_base + o0, w0 + w1, [(0, w0), (w0, w1)]))
                            i += 2
                        else:
                            groups.append((col_base + o0, w0, [(0, w0)]))
                            i += 1
                    linear(wd, 0, D, groups, HN, n, consume, wk=wk)
                    if st_out is not None:
                        S.dma("act", out=st_out, in_=sv, reads=["STT"], writes=[oname])


                S.barrier()
                U = AR[:, 0:2048].rearrange("p (c t) -> p c t", c=4)
                T5 = [AR[:, 2048 + i * 512:2048 + (i + 1) * 512] for i in range(12)]
                Y1F = AR[:, 8192:10240].rearrange("p (c t) -> p c t", c=4)
                iota = CS[:, CI["iota_p"]:CI["iota_p"] + 4, :] if kind == "p" else CS[:, CI["iota_s"]:CI["iota_s"] + 1, :]
                rset = CS[:, CI["reset_p"]:CI["reset_p"] + 4, :] if kind == "p" else CS[:, CI["reset_s"]:CI["reset_s"] + 1, :]
                iota = iota.rearrange("p c t -> p (c t)")
                rset = rset.rearrange("p c t -> p (c t)")

                def cons_u(col, w, pap, pres):
                    S.op("act", "activation", U[:, col // 128, 0:n], pap, AF.Copy, reads=[pres], writes=["U"])
                linear(w_in[l], 0, D, [(0, 256, [(0, 128), (128, 128)]), (256, 256, [(0, 128), (128, 128)])], HN, n, cons_u, wk="in%d" % l)
                s5v = S5ST[:, 0:2 * 16 * nb].rearrange("p (r c b) -> p r c b", r=2, c=16)
                if first:
                    S.op("dve", "memset", S5ST[:, 0:2 * 16 * nb], 0.0, writes=["S5ST"])
                else:
                    S.dma("act", out=s5v, in_=(o_s5["p"][l] if kind == "p" else st_s5[l]), reads=["o_s5"], writes=["S5ST"])
                rho, th, cr, ci, ncr = (S5P[:, l, i, :] for i in range(5))
                psy = [PSM[4], PSM[5]]
                v3 = lambda ap: ap.rearrange("p (b t) -> p b t", b=nb)
                for sc in range(16):
                    kc, oc = sc // 4, sc // 4
                    sbt = SB5[sc % 2]
                    S.dma("sp", out=sbt[:], in_=s5B[l, :, kc * 128:(kc + 1) * 128, sc * 128:(sc + 1) * 128].rearrange("r p m -> p r m"),
                          writes=["SB5_%d" % (sc % 2)])
                    pb = [PSM[2], PSM[3]]
                    for r in range(2):
                        S.op("pe", "matmul", pb[r][:, 0:n], sbt[:, r, :], U[:, kc, 0:n], start=True, stop=True,
                             reads=["SB5_%d" % (sc % 2), "U"], writes=["PSM%d" % (2 + r)])
                    sl_ = slice(sc, sc + 1)
                    ANG, A2, SIN, COS, IR, II, RR, RI, ZR, ZI, TA, TB = (x[:, 0:n] for x in T5)
                    S.op("dve", "tensor_scalar", ANG, iota[:, 0:n], th[:, sl_], None, ALU.mult, reads=["CS", "S5P"], writes=["ANG"])
                    S.op("dve", "tensor_scalar", A2, ANG, PI / 2, None, ALU.add, reads=["ANG"], writes=["A2"])
                    wrap_sin(SIN, ANG, n, TA, TB, srcn=["ANG"], dstn=["SIN"])
                    S.barrier()
                    wrap_sin(COS, A2, n, TA, TB, srcn=["A2"], dstn=["COS"])
                    S.barrier()
                    S.op("dve", "tensor_scalar", IR, COS, cr[:, sl_], None, ALU.mult, writes=["IR"])
                    S.op("dve", "scalar_tensor_tensor", IR, SIN, ci[:, sl_], IR, ALU.mult, ALU.add, writes=["IR"])
                    S.op("dve", "tensor_scalar", II, COS, ci[:, sl_], None, ALU.mult, writes=["II"])
                    S.op("dve", "scalar_tensor_tensor", II, SIN, ncr[:, sl_], II, ALU.mult, ALU.add, reads=["II"], writes=["II"])
                    S.op("dve", "tensor_tensor", RR, IR, pb[0][:, 0:n], ALU.mult, reads=["IR", "PSM2"], writes=["RR"])
                    S.op("dve", "tensor_tensor", TA, II, pb[1][:, 0:n], ALU.mult, reads=["II", "PSM3"], writes=["TA"])
                    S.op("dve", "tensor_tensor", RR, RR, TA, ALU.subtract, reads=["TA", "RR"], writes=["RR"])
                    S.op("dve", "tensor_tensor", RI, IR, pb[1][:, 0:n], ALU.mult, reads=["IR", "PSM3"], writes=["RI"])
                    S.op("dve", "tensor_tensor", TB, II, pb[0][:, 0:n], ALU.mult, reads=["II", "PSM2"], writes=["TB"])
                    S.op("dve", "tensor_tensor", RI, RI, TB, ALU.add, reads=["TB", "RI"], writes=["RI"])
                    S.op("dve", "tensor_scalar", TA, rset[:, 0:n], rho[:, sl_], None, ALU.mult, reads=["TA", "CS", "S5P"], writes=["TA"])
                    for r, RX in ((0, RR), (1, RI)):
                        S.op("dve", "tensor_scalar", TB[:, 0:nb], s5v[:, r, sc, :], rho[:, sl_], None, ALU.mult,
                             reads=["S5ST", "TB"], writes=["TB"])
                        S.op("dve", "tensor_tensor", v3(RX)[:, :, 0], v3(RX)[:, :, 0], TB[:, 0:nb], ALU.add,
                             reads=["TB", "RR", "RI"], writes=["RR", "RI"])
                    S.op("dve", "tensor_tensor_scan", ZR, TA, RR, 0.0, ALU.mult, ALU.add, reads=["TA", "RR"], writes=["ZR"])
                    S.op("dve", "tensor_tensor_scan", ZI, TA, RI, 0.0, ALU.mult, ALU.add, reads=["TA", "RI"], writes=["ZI"])
                    S.barrier()
                    S.op("dve", "tensor_tensor", RR, COS, ZR, ALU.mult, writes=["RR"])
                    S.op("dve", "tensor_tensor", TA, SIN, ZI, ALU.mult, writes=["TA"])
                    S.op("dve", "tensor_tensor", RR, RR, TA, ALU.subtract, reads=["TA"], writes=["RR"])
                    S.op("dve", "tensor_tensor", RI, COS, ZI, ALU.mult, writes=["RI"])
                    S.op("dve", "tensor_tensor", TB, SIN, ZR, ALU.mult, writes=["TB"])
                    S.op("dve", "tensor_tensor", RI, RI, TB, ALU.add, reads=["TB"], writes=["RI"])
                    S.op("dve", "tensor_copy", s5v[:, 0, sc, :], v3(RR)[:, :, L - 1], reads=["RR"], writes=["S5ST"])
                    S.op("dve", "tensor_copy", s5v[:, 1, sc, :], v3(RI)[:, :, L - 1], reads=["RI"], writes=["S5ST"])
                    sct = SB5[sc % 2]
                    S.dma("sp", out=sct[:], in_=s5C[l, :, sc * 128:(sc + 1) * 128, oc * 128:(oc + 1) * 128].rearrange("r p m -> p r m"),
                          reads=[], writes=["SB5_%d" % (sc % 2)])
                    for r, RX, nm in ((0, RR, "RR"), (1, RI, "RI")):
                        S.op("pe", "matmul", psy[r][:, 0:n], sct[:, r, :], RX, start=(sc % 4 == 0), stop=(sc % 4 == 3),
                             reads=["SB5_%d" % (sc % 2), nm], writes=["PSM%d" % (4 + r)])
                    if sc % 4 == 3:
                        dcol = PC[:, l, PO["s5d"] + oc:PO["s5d"] + oc + 1]
                        Y = ZR
                        S.op("act", "activation", TA, psy[1][:, 0:n], AF.Copy, reads=["PSM5"], writes=["TA"])
                        S.op("dve", "tensor_tensor", Y, psy[0][:, 0:n], TA, ALU.subtract, reads=["PSM4", "TA"], writes=["ZR"])
                        S.op("dve", "scalar_tensor_tensor", Y, U[:, oc, 0:n], dcol, Y, ALU.mult, ALU.add, reads=["U", "PC"], writes=["ZR"])
                        S.op("dve", "tensor_tensor", TB, Y, Y, ALU.mult, reads=["ZR"], writes=["TB"])
                        S.op("dve", "tensor_scalar", TB, TB, 0.044715, 1.0, ALU.mult, ALU.add, writes=["TB"])
                        S.op("dve", "tensor_tensor", TB, TB, Y, ALU.mult, writes=["TB"])
                        S.op("act", "activation", TB, TB, AF.Sigmoid, scale=1.5957691216057308, reads=["TB"], writes=["TB"])
                        S.op("dve", "tensor_tensor", Y1F[:, oc, 0:n], Y, TB, ALU.mult, reads=["TB", "ZR"], writes=["Y1F"])
                        S.op("dve", "tensor_copy", AJ[:, oc, 0:n], Y1F[:, oc, 0:n], reads=["Y1F"], writes=["AJ"])
                        S.barrier()

                def cons_glu(col, w, pap, pres):
                    oc = col // 128
                    S.op("act", "activation", T5[0][:, 0:n], pap, AF.Sigmoid, reads=[pres], writes=["ANG"])
                    S.op("dve", "tensor_tensor", YBR[:, oc, 0:n], Y1F[:, oc, 0:n], T5[0][:, 0:n], ALU.mult,
                         reads=["ANG", "Y1F"], writes=["YBR"])
                linear(w_glu[l], 0, 512, [(0, 256, [(0, 128), (128, 128)]), (256, 256, [(0, 128), (128, 128)])], AJ, n, cons_glu, aname="AJ", wk="glu%d" % l)
                S.dma("act", out=o_s5[kind][l], in_=s5v, reads=["S5ST"], writes=["o_s5"])
                S.barrier()


                S.barrier()
                MUi, MUs, MLs, NEGU, POSL, SAME = (CK[:, i_, :] for i_ in range(6))
                rowmask = CS[:, CI["rowmask"], 0:NSEQ_S] if kind == "s" else CS[:, CI["ones"], 0:NSEQ_S]
                headblk = CS[:, CI["headblk"], :]
                nchunk = n // 128
                levels = 6 if kind == "p" else 2
                bump = [6976]

                def alloc(k):
                    o_ = bump[0]
                    bump[0] += k
                    assert bump[0] <= NA, bump[0]
                    return AR[:, o_:o_ + k]
                RKV = alloc(3 * n).rearrange("p (c t) -> p c t", c=3)
                AAb, BBb, LWb, Gb, BON = (alloc(n) for _ in range(5))
                PW = alloc(nb * 64).rearrange("p (b v) -> p b v", b=nb)
                TT = [alloc(128) for _ in range(20)]
                UU = alloc(128)
                KQM_flat = AR[:, 1040:1040 + 2176]
                KQM = KQM_flat[:, 0:2048].rearrange("p (b c) -> p b c", b=16)
                KQD = KQM_flat.rearrange("p (b x) -> p b x", x=136)[:, :, 0:8]
                if nb > 1:
                    KM1 = alloc(128)
                    KM2 = alloc(128)
                S.private = {"LTm", "Lm", "AKT", "RBT", "RKT", "y0", "y1", "P1T"}
                XT1 = [AR[:, 1040 + i_ * 128:1040 + (i_ + 1) * 128] for i_ in range(11)] if nb == 1 else None
                LORA = MRG[:, 8:12, :]
                rst = STT[:, 0:16 * nb * 1].rearrange("p (c b h) -> p c b h", c=16, b=nb)
                if first:
                    S.op("dve", "memset", STT[:, 0:16 * nb], 0.0, writes=["STT"])
                else:
                    S.dma("act", out=rst, in_=(o_shift["p"][l] if kind == "p" else st_shift[l]), reads=["o_shift"], writes=["STT"])
                mu = PC[:, l, PO["mu"]:PO["mu"] + 16]

                def shifted(bi_g, w, ev, eres, dst, dname):
                    hsv = dst.rearrange("p (b t) -> p b t", b=nb)
                    S.op("dve", "tensor_tensor", hsv[0:w], ev[0:w, :, 0:L], ev[0:w, :, 1:1 + L], ALU.subtract, reads=[eres], writes=[dname])
                    S.op("dve", "scalar_tensor_tensor", hsv[0:w], hsv[0:w], mu[0:w, bi_g:bi_g + 1], ev[0:w, :, 1:1 + L], ALU.mult, ALU.add,
                         reads=[eres, "PC", dname], writes=[dname])

                HS = TT[0:4]

                def after_lora(bi, w, ev, eres):
                    hs_t = AR[:, 6976 + 3 * n:6976 + 4 * n]
                    shifted(12 + bi, w, ev, eres, hs_t[:, 0:n], "AAb")
                    fn = (AF.Tanh, AF.Copy, AF.Sigmoid, AF.Sigmoid)[bi]
                    S.op("act", "activation", LORA[0:w, bi, 0:n], hs_t[0:w, 0:n], fn, reads=["AAb"], writes=["LORA"])
                halo_proj(w_in[l], C_RW, RW_BLOCKS[12:16], 1, None, None, "o_shift", after_lora, nbuf=2, sblk=([12, 13, 14, 15], 16), wk="in%d" % l)

                ps_rr = [0]

                def PS_():
                    i_ = ps_rr[0] % 6
                    ps_rr[0] += 1
                    return PSM[i_], "PSM%d" % i_

                S.private2 = {"RKV", "AAb", "BBb", "LWb", "Gb", "BON", "PW", "GL", "GAM", "GINV", "GPREV", "RT", "KT", "BT_", "AT",
                              "Vt", "KTt", "BTt", "YN", "UU", "P0x", "P0Tx", "P1x", "KM1", "KM2", "KQM"}
                NP = 2 if (kind == "p" and ti > 0) else 1
                BK = [(PSM[i_], "PSM%d" % i_) for i_ in range(6)] + [(PSD[0], "PSD0"), (PSD[1], "PSD1")]
                pslots = [dict(RKV=RKV, AR5=(AAb, BBb, LWb, Gb, BON), PW=PW, TT=TT, UU=UU, XT1=XT1, banks=BK[0:6] if NP == 1 else BK[0:4])]
                if NP == 2:
                    S.barrier()
                    ws_busy[0] = True
                    w0_, w1_ = WS[0], WS[1]
                    pslots.append(dict(RKV=w0_[:, 0:3 * n].rearrange("p (c t) -> p c t", c=3),
                                       AR5=tuple(w0_[:, 1536 + i_ * 512:1536 + (i_ + 1) * 512] for i_ in range(5)),
                                       PW=w1_[:, 0:64].rearrange("p (b v) -> p b v", b=1),
                                       TT=[w1_[:, 64 + i_ * 128:64 + (i_ + 1) * 128] for i_ in range(20)],
                                       UU=w1_[:, 2624:2752],
                                       XT1=[AR[:, 2448 + i_ * 128:2448 + (i_ + 1) * 128] for i_ in range(11)],
                                       banks=BK[4:8]))

                def pair_p(pr, sl):
                    RKV, (AAb, BBb, LWb, Gb, BON), PW = sl["RKV"], sl["AR5"], sl["PW"]

                    def after_rkv(bi, w, ev, eres):
                        shifted((pr, 4 + pr, 8 + pr)[bi], w, ev, eres, RKV[:, bi, 0:n], "RKV")
                    halo_proj(w_in[l], C_RW, [(pr * 128, 128), (512 + pr * 128, 128), (1024 + pr * 128, 128)], 1, None, None,
                              "o_shift", after_rkv, nbuf=2, sblk=([pr, 4 + pr, 8 + pr], 16), wk="in%d" % l)
                    rT_, kT_, vT_ = RKV[:, 0, 0:n], RKV[:, 1, 0:n], RKV[:, 2, 0:n]
                    pcol_ = lambda nm: PC[:, l, PO[nm] + pr:PO[nm] + pr + 1]
                    T1, T2 = BON[:, 0:n], Gb[:, 0:n]

                    def cons_w(col, w, pap, pres):
                        S.op("act", "activation", LWb[:, 0:n], pap, AF.Identity, bias=pcol_("nw0"), scale=1.0, reads=[pres, "PC"], writes=["LWb"])
                        S.op("act", "activation", LWb[:, 0:n], LWb[:, 0:n], AF.Exp, scale=-1.0, reads=["LWb"], writes=["LWb"])
                        S.op("act", "activation", LWb[:, 0:n], LWb[:, 0:n], AF.Ln, bias=1.0, scale=1.0, reads=["LWb"], writes=["LWb"])
                        S.op("act", "activation", LWb[:, 0:n], LWb[:, 0:n], AF.Exp, bias=-0.5, scale=-1.0, reads=["LWb"], writes=["LWb"])
                        S.op("dve", "tensor_scalar", LWb[:, 0:n], LWb[:, 0:n], -1.0, None, ALU.mult, reads=["LWb"], writes=["LWb"])
                    linear(w_lora[0][l], 0, 96, [(pr * 128, 128, [(0, 128)])], LORA, n, cons_w, kc0=0, aname="LORA", wk="lo0%d" % l)

                    def cons_a(col, w, pap, pres):
                        S.op("act", "activation", AAb[:, 0:n], pap, AF.Sigmoid, bias=pcol_("a0"), scale=1.0, reads=[pres, "PC"], writes=["AAb"])
                    linear(w_lora[1][l], 0, 96, [(pr * 128, 128, [(0, 128)])], LORA, n, cons_a, kc0=1, aname="LORA", wk="lo1%d" % l)
                    S.op("dve", "tensor_scalar", BBb[:, 0:n], kT_, pcol_("kk"), None, ALU.mult, reads=["RKV", "PC"], writes=["BBb"])
                    S.op("act", "activation", T1, BBb[:, 0:n], AF.Square, reads=["BBb"], writes=["BON"])
                    pt, pn_ = PS_()
                    S.op("pe", "matmul", pt[:, 0:n], headblk, T1, start=True, stop=True, reads=["BON", "CS"], writes=[pn_])
                    S.op("act", "activation", T1, pt[:, 0:n], AF.Sqrt, bias=1e-6, scale=1.0, reads=[pn_], writes=["BON"])
                    S.op("dve", "reciprocal", T1, T1, reads=["BON"], writes=["BON"])
                    S.op("dve", "tensor_tensor", BBb[:, 0:n], BBb[:, 0:n], T1, ALU.mult, reads=["BON", "BBb"], writes=["BBb"])
                    S.op("dve", "tensor_scalar", T1, AAb[:, 0:n], -1.0, pcol_("ka"), ALU.add, ALU.mult, reads=["AAb", "PC"], writes=["BON"])
                    S.op("dve", "tensor_scalar", T1, T1, 1.0, None, ALU.add, reads=["BON"], writes=["BON"])
                    S.op("dve", "tensor_tensor", kT_, kT_, T1, ALU.mult, reads=["BON", "RKV"], writes=["RKV"])
                    S.op("dve", "tensor_tensor", T1, BBb[:, 0:n], AAb[:, 0:n], ALU.mult, reads=["BBb", "AAb"], writes=["BON"])
                    S.op("dve", "tensor_scalar", AAb[:, 0:n], BBb[:, 0:n], -1.0, None, ALU.mult, reads=["BBb", "AAb"], writes=["AAb"])
                    S.op("dve", "tensor_copy", BBb[:, 0:n], T1, reads=["BON"], writes=["BBb"])
                    S.op("dve", "scalar_tensor_tensor", T1, rT_, pcol_("rk"), kT_, ALU.mult, ALU.mult, reads=["RKV", "PC", "BON"], writes=["BON"])
                    pt, pn_ = PS_()
                    S.op("pe", "matmul", pt[:, 0:n], headblk, T1, start=True, stop=True, reads=["BON", "CS"], writes=[pn_])
                    S.op("dve", "tensor_tensor", BON[:, 0:n], pt[:, 0:n], vT_, ALU.mult, reads=[pn_, "RKV", "BON"], writes=["BON"])

                    def cons_g(col, w, pap, pres):
                        S.op("act", "activation", Gb[:, 0:n], pap, AF.Copy, reads=[pres], writes=["Gb"])
                    linear(w_lora[2][l], 0, 256, [(pr * 128, 128, [(0, 128)])], LORA, n, cons_g, kc0=2, aname="LORA", wk="lo2%d" % l)
                    if first:
                        S.op("dve", "memset", PW[:, :, :], 0.0, writes=["PW"])
                    else:
                        S.dma("act", out=PW[:, :, :], in_=(o_wkv["p"][l, pr] if kind == "p" else st_wkv[l, pr]), reads=["o_wkv"], writes=["PW"])

                def pair_c(pr, sl):
                    RKV, (AAb, BBb, LWb, Gb, BON), PW = sl["RKV"], sl["AR5"], sl["PW"]
                    TT, UU, XT1 = sl["TT"], sl["UU"], sl["XT1"]
                    pcol_ = lambda nm: PC[:, l, PO[nm] + pr:PO[nm] + pr + 1]
                    prr_ = [0]

                    def PS_():
                        bk = sl["banks"][prr_[0] % len(sl["banks"])]
                        prr_[0] += 1
                        return bk
                    for c in range(nchunk):
                        if _os0.environ.get("MK_RSTOP"):
                            continue
                        cs_ = slice(c * 128, (c + 1) * 128)
                        (GL, GAM, GINV, GPREV, RT, KT, BT_, AT, Vt, KTt, BTt, LTm, Lm, AKT, RBT, RKT, y0, y1, YN, P1T) = TT
                        P0, P0T, P1 = GINV, GPREV, GL
                        rs_ = (CS[:, CI["reset_p"], :] if kind == "p" else CS[:, CI["reset_s"], :])
                        S.op("dve", "tensor_tensor_scan", GL, rs_, LWb[:, cs_], 0.0, ALU.mult, ALU.add, reads=["CS", "LWb"], writes=["GL"])
                        S.op("act", "activation", GAM, GL, AF.Exp, reads=["GL"], writes=["GAM"])
                        S.op("act", "activation", GINV, GL, AF.Exp, scale=-1.0, reads=["GL"], writes=["GINV"])
                        S.op("dve", "tensor_tensor", GPREV, GL, LWb[:, cs_], ALU.subtract, reads=["GL", "LWb"], writes=["GPREV"])
                        S.op("act", "activation", GPREV, GPREV, AF.Exp, reads=["GPREV"], writes=["GPREV"])
                        S.op("dve", "tensor_tensor", RT, RKV[:, 0, cs_], GAM, ALU.mult, reads=["RKV", "GAM"], writes=["RT"])
                        S.op("dve", "tensor_tensor", KT, RKV[:, 1, cs_], GINV, ALU.mult, reads=["RKV", "GINV"], writes=["KT"])
                        S.op("dve", "tensor_tensor", BT_, BBb[:, cs_], GINV, ALU.mult, reads=["BBb", "GINV"], writes=["BT_"])
                        S.op("dve", "tensor_tensor", AT, AAb[:, cs_], GPREV, ALU.mult, reads=["AAb", "GPREV"], writes=["AT"])
                        pt, pn_ = PS_()
                        S.op("pe", "transpose", pt[:, 0:128], RKV[:, 2, cs_], ident, reads=["RKV", "CS"], writes=[pn_])
                        S.op("pe", "transpose", pt[:, 128:256], KT, ident, reads=["KT", "CS"], writes=[pn_])
                        S.op("pe", "transpose", pt[:, 256:384], BT_, ident, reads=["BT_", "CS"], writes=[pn_])
                        S.op("act", "activation", Vt, pt[:, 0:128], AF.Copy, reads=[pn_], writes=["Vt"])
                        S.op("dve", "tensor_copy", KTt, pt[:, 128:256], reads=[pn_], writes=["KTt"])
                        S.op("act", "activation", BTt, pt[:, 256:384], AF.Copy, reads=[pn_], writes=["BTt"])
                        HB = [(LTm, Lm, AKT, RBT, RKT, y0, y1, P1T, P0, P0T, P1, "GINV", "GPREV", "GL")]
                        if nb == 1:
                            HB.append(tuple(XT1) + ("P0x", "P0Tx", "P1x"))

                        def head_chain(hh, threaded):
                            (LTm, Lm, AKT, RBT, RKT, y0, y1, P1T, P0, P0T, P1, nP0, nP0T, nP1) = HB[hh if threaded else 0]
                            nbk = len(sl["banks"]) // 2
                            banks = sl["banks"][hh * nbk:(hh + 1) * nbk] if threaded else sl["banks"]
                            rr_ = [0]

                            def PS_():
                                bk = banks[rr_[0] % len(banks)]
                                rr_[0] += 1
                                return bk
                            hp = slice(hh * 64, (hh + 1) * 64)
                            hv = slice(hh * 64, (hh + 1) * 64)
                            pa, pan = PS_()
                            S.op("pe", "matmul", pa[:, 0:128], BT_[hp, :], AT[hp, :], start=True, stop=True, reads=["BT_", "AT"], writes=[pan])
                            S.op("pe", "matmul", pa[:, 128:256], AT[hp, :], BT_[hp, :], start=True, stop=True, reads=["BT_", "AT"], writes=[pan])
                            S.op("pe", "matmul", pa[:, 256:384], KT[hp, :], AT[hp, :], start=True, stop=True, reads=["KT", "AT"], writes=[pan])
                            pb2, pbn = PS_()
                            S.op("pe", "matmul", pb2[:, 0:128], BT_[hp, :], RT[hp, :], start=True, stop=True, reads=["BT_", "RT"], writes=[pbn])
                            S.op("pe", "matmul", pb2[:, 128:256], KT[hp, :], RT[hp, :], start=True, stop=True, reads=["KT", "RT"], writes=[pbn])
                            S.op("dve", "tensor_tensor", LTm, pa[:, 0:128], MUs, ALU.mult, reads=[pan, "CK"], writes=["LTm"])
                            S.op("dve", "tensor_tensor", Lm, pa[:, 128:256], MLs, ALU.mult, reads=[pan, "CK"], writes=["Lm"])
                            S.op("dve", "tensor_tensor", AKT, pa[:, 256:384], MUs, ALU.mult, reads=[pan, "CK"], writes=["AKT"])
                            S.op("dve", "tensor_tensor", RBT, pb2[:, 0:128], MUi, ALU.mult, reads=[pbn, "CK"], writes=["RBT"])
                            S.op("dve", "tensor_tensor", RKT, pb2[:, 128:256], MUi, ALU.mult, reads=[pbn, "CK"], writes=["RKT"])

                            def state_acc(pt2, pn2, srcT, srcname, first_start):
                                if nb == 1:
                                    S.op("pe", "matmul", pt2[:, 0:64], srcT[hp, :], PW[hp, 0, :], start=first_start, stop=True,
                                         reads=[srcname, "PW"], writes=[pn2])
                                else:
                                    S.op("dve", "memset", KQM_flat[:, 0:2048], 0.0, writes=["KQM"])
                                    S.op("dve", "tensor_copy", KQD[hp], srcT[hp, :].rearrange("p (b t) -> p b t", b=16), reads=[srcname], writes=["KQM"])
                                    for b_ in range(nb):
                                        S.op("pe", "matmul", pt2[:, 0:64], KQM[hp, b_, :], PW[hp, b_, :], start=(first_start and b_ == 0),
                                             stop=(b_ == nb - 1), reads=["KQM", "PW"], writes=[pn2])
                            pr_, prn = PS_()
                            S.op("pe", "matmul", pr_[:, 0:64], AKT, Vt[:, hv], start=True, stop=False, reads=["AKT", "Vt"], writes=[prn])
                            state_acc(pr_, prn, AT, "AT", False)
                            S.op("act", "activation", y0[:, 0:64], pr_[:, 0:64], AF.Copy, reads=[prn], writes=["y0"])
                            pt, pn_ = PS_()
                            S.op("pe", "matmul", pt[:, 0:64], LTm, y0[:, 0:64], start=True, stop=True, reads=["LTm", "y0"], writes=[pn_])
                            S.op("dve", "tensor_tensor", y1[:, 0:64], y0[:, 0:64], pt[:, 0:64], ALU.add, reads=[pn_, "y0"], writes=["y1"])
                            ycur, yn_, yoth, yon_ = y1, "y1", y0, "y0"
                            Pc, PcT, Pcn, PcTn = Lm, LTm, "Lm", "LTm"
                            pw = [(P0, P0T, nP0, nP0T), (P1, P1T, nP1, "P1T")]
                            for lev in range(levels):
                                Pn, PnT, Pnn, PnTn = pw[lev % 2]
                                pt, pn_ = PS_()
                                S.op("pe", "matmul", pt[:, 0:128], Pc, PcT, start=True, stop=True, reads=[Pcn, PcTn], writes=[pn_])
                                if lev < levels - 1:
                                    S.op("pe", "matmul", pt[:, 128:256], PcT, Pc, start=True, stop=True, reads=[Pcn, PcTn], writes=[pn_])
                                S.op("act", "activation", PnT, pt[:, 0:128], AF.Copy, reads=[pn_], writes=[PnTn])
                                if lev < levels - 1:
                                    S.op("act", "activation", Pn, pt[:, 128:256], AF.Copy, reads=[pn_], writes=[Pnn])
                                pt2, pn2 = PS_()
                                S.op("pe", "matmul", pt2[:, 0:64], PnT, ycur[:, 0:64], start=True, stop=True, reads=[PnTn, yn_], writes=[pn2])
                                S.op("dve", "tensor_tensor", yoth[:, 0:64], ycur[:, 0:64], pt2[:, 0:64], ALU.add, reads=[pn2, yn_], writes=[yon_])
                                ycur, yn_, yoth, yon_ = yoth, yon_, ycur, yn_
                                Pc, PcT, Pcn, PcTn = Pn, PnT, Pnn, PnTn
                            Um, Un_ = ycur, yn_
                            py, pyn = PS_()
                            S.op("pe", "matmul", py[:, 0:64], RBT, Um[:, 0:64], start=True, stop=False, reads=["RBT", Un_], writes=[pyn])
                            S.op("pe", "matmul", py[:, 0:64], RKT, Vt[:, hv], start=False, stop=False, reads=["RKT", "Vt"], writes=[pyn])
                            state_acc(py, pyn, RT, "RT", False)
                            st6 = AKT[:, 0:6]
                            mv = AKT[:, 8:10]
                            S.op("act", "activation", RBT[:, 0:64], py[:, 0:64], AF.Copy, reads=[pyn], writes=["RBT"])
                            S.op("dve", "bn_stats", st6, RBT[:, 0:64], reads=["RBT"], writes=["AKT"])
                            S.op("dve", "bn_aggr", mv, st6, reads=["AKT"], writes=["AKT"])
                            S.op("act", "activation", mv[:, 1:2], mv[:, 1:2], AF.Sqrt, bias=64e-5, scale=1.0, reads=["AKT"], writes=["AKT"])
                            S.op("dve", "reciprocal", mv[:, 1:2], mv[:, 1:2], reads=["AKT"], writes=["AKT"])
                            S.op("dve", "tensor_scalar", YN[:, hv], RBT[:, 0:64], mv[:, 0:1], mv[:, 1:2], ALU.subtract, ALU.mult,
                                 reads=["RBT", "AKT"], writes=["YN"])
                            S.op("dve", "tensor_copy", UU[:, hv], Um[:, 0:64], reads=[Un_], writes=["UU"])
                        if nb == 1:
                            ths = []
                            for hh in range(2):
                                S.sfx = "@%d" % hh
                                S.thread_begin()
                                head_chain(hh, True)
                                ths.append(S.thread_end())
                            S.sfx = ""
                            S.replay(ths)
                        else:
                            for hh in range(2):
                                head_chain(hh, False)
                        for b_ in range(nb):
                            if nb > 1:
                                S.op("dve", "tensor_scalar", KM1, BTt, rowmask[:, b_:b_ + 1], None, ALU.mult, reads=["BTt", "CS"], writes=["KM1"])
                                S.op("dve", "tensor_scalar", KM2, KTt, rowmask[:, b_:b_ + 1], None, ALU.mult, reads=["KTt", "CS"], writes=["KM2"])
                                lb, lbn = (KM1, KM2), ("KM1", "KM2")
                            else:
                                lb, lbn = (BTt, KTt), ("BTt", "KTt")
                            lastc = (b_ + 1) * L - 1 if nb > 1 else 127
                            for hh in range(2):
                                hp = slice(hh * 64, (hh + 1) * 64)
                                hv = hp
                                pt, pn_ = PS_()
                                S.op("pe", "matmul", pt[hp, 0:64], lb[0][:, hv], UU[:, hv], start=True, stop=False, reads=[lbn[0], "UU"], writes=[pn_])
                                S.op("pe", "matmul", pt[hp, 0:64], lb[1][:, hv], Vt[:, hv], start=False, stop=True, reads=[lbn[1], "Vt"], writes=[pn_])
                                S.op("dve", "tensor_tensor", PW[hp, b_, :], PW[hp, b_, :], pt[hp, 0:64], ALU.add, reads=[pn_, "PW"], writes=["PW"])
                                S.op("dve", "tensor_scalar", PW[hp, b_, :], PW[hp, b_, :], GAM[hp, lastc:lastc + 1], None, ALU.mult,
                                     reads=["GAM", "PW"], writes=["PW"])
                        pt, pn_ = PS_()
                        S.op("pe", "transpose", pt[:, 0:128], YN, ident, reads=["YN", "CS"], writes=[pn_])
                        S.op("dve", "tensor_scalar", RT, pt[:, 0:128], pcol_("lnw"), pcol_("lnb"), ALU.mult, ALU.add, reads=[pn_, "PC"], writes=["RT"])
                        S.op("dve", "tensor_tensor", RT, RT, BON[:, cs_], ALU.add, reads=["RT", "BON"], writes=["RT"])
                        S.op("dve", "tensor_tensor", YBR[:, 4 + pr, cs_], RT, Gb[:, cs_], ALU.mult, reads=["RT", "Gb"], writes=["YBR"])
                    S.dma("act", out=o_wkv[kind][l, pr], in_=PW[:, :, :], reads=["PW"], writes=["o_wkv"])

                for pg in range(0, 4, NP):
                    for k_ in range(NP):
                        S.sfx2 = "#%d" % k_
                        pair_p(pg + k_, pslots[k_])
                    pths = []
                    for k_ in range(NP):
                        S.sfx2 = "#%d" % k_
                        S.thread_begin()
                        pair_c(pg + k_, pslots[k_])
                        pths.append(S.thread_end())
                    S.sfx2 = ""
                    S.replay(pths)
                if NP == 2:
                    S.barrier()
                    ws_busy[0] = False
                S.private2 = set()
                S.dma("act", out=o_shift[kind][l], in_=rst, reads=["STT"], writes=["o_shift"])
                S.barrier()

                S.barrier()
                MUi, MUs, MLs, NEGU, POSL, SAME = (CK[:, i_, :] for i_ in range(6))
                rowmask = CS[:, CI["rowmask"], 0:NSEQ_S] if kind == "s" else CS[:, CI["ones"], 0:NSEQ_S]
                nchunk = n // 128
                levels = 6 if kind == "p" else 2
                bump = [6976]

                def alloc(k):
                    o_ = bump[0]
                    bump[0] += k
                    assert bump[0] <= NA, bump[0]
                    return AR[:, o_:o_ + k]
                QKV = alloc(3 * n).rearrange("p (c t) -> p c t", c=3)
                SG = alloc(nb * 128).rearrange("p (b v) -> p b v", b=nb)
                TT = [alloc(128) for _ in range(22)]
                BTF = alloc(n)
                GCH = alloc(128)
                L2S = alloc(n)
                KQM_flat = AR[:, 1040:1040 + 2176]
                KQM = KQM_flat[:, 0:2048].rearrange("p (b c) -> p b c", b=16)
                KQD = KQM_flat.rearrange("p (b x) -> p b x", x=136)[:, :, 0:8]
                SC8c = [alloc(64) for _ in range(nchunk)]
                GCFc = [alloc(128) for _ in range(nchunk)]
                EGc = [alloc(nb * 8) for _ in range(nchunk)]
                GM = SB5[1][:, 1, :]
                ZS = MRG
                def cons_z(col, w, pap, pres):
                    S.op("act", "activation", ZS[:, (col - C_Z) // 128, 0:n], pap, AF.Silu, reads=[pres], writes=["ZS"])
                linear(w_in[l], 0, D, [(C_Z + i_ * 256, 256, [(0, 128), (128, 128)]) for i_ in range(4)], HN, n, cons_z, wk="in%d" % l)
                ws_ = SB5[0][:, :, :].rearrange("p a b -> p (a b)")
                S.dma("sp", out=ws_[:, 0:256].rearrange("p (kc n) -> p kc n", kc=16),
                      in_=w_in[l][:, C_B:C_B + 16].rearrange("(kc p) n -> p kc n", p=128), writes=["SB5_0"])
                S.op("dve", "tensor_copy", WBA[:, :, :], ws_[:, 0:256].rearrange("p (kc n) -> p kc n", kc=16),
                     reads=["SB5_0"], writes=["WBA"])
                wv_ = WBA
                pb_, pa_ = PSM[0], PSM[1]
                for (pp, c0, pn_) in ((pb_, 0, "PSM0"), (pa_, 8, "PSM1")):
                    for kc in range(16):
                        S.op("pe", "matmul", pp[0:8, 0:n], wv_[:, kc, c0:c0 + 8], HN[:, kc, 0:n], start=(kc == 0), stop=(kc == 15),
                             reads=["WBA", "HN"], writes=[pn_])
                S.op("act", "activation", BTF[0:8, 0:n], pb_[0:8, 0:n], AF.Sigmoid, reads=["PSM0"], writes=["BTF"])
                alog = PR[:, l, 0:8]
                dtb = PR[:, l, 8:16]
                NEA = alloc(8)
                S.op("act", "activation", NEA, alog, AF.Exp, reads=["PR"], writes=["NEA"])
                S.op("dve", "tensor_scalar", NEA, NEA, -1.0, None, ALU.mult, reads=["NEA"], writes=["NEA"])

                gst = STT[:, 0:24 * nb * 3].rearrange("p (c b h) -> p c b h", c=24, b=nb)
                if first:
                    S.op("dve", "memset", STT[:, 0:24 * nb * 3], 0.0, writes=["STT"])
                else:
                    S.dma("act", out=gst, in_=(o_gconv["p"][l] if kind == "p" else st_gconv[l]), reads=["o_gconv"], writes=["STT"])

                ps_rr = [0]

                def PS_():
                    i_ = ps_rr[0] % 6
                    ps_rr[0] += 1
                    return PSM[i_], "PSM%d" % i_

                import os as _os
                GSTOP = int(_os.environ.get("MK_GSTOP", "99"))
                for c in range(nchunk):
                    cs_ = slice(c * 128, (c + 1) * 128)
                    sc8 = SC8c[c]
                    BET, GTK, GCT, EGC, NBEG, NGC, EDEC, GLT = (sc8[:, i_ * 8:(i_ + 1) * 8] for i_ in range(8))
                    GCF = GCFc[c]
                    EG = EGc[c]
                    pt, pn_ = PS_()
                    for kc in range(16):
                        S.op("pe", "matmul", pt[:, 0:16], HN[:, kc, cs_], wv_[:, kc, 0:16], start=(kc == 0), stop=(kc == 15),
                             reads=["WBA", "HN"], writes=[pn_])
                    S.op("act", "activation", BET, pt[:, 0:8], AF.Sigmoid, reads=[pn_], writes=["sc8"])
                    S.op("dve", "tensor_tensor", GTK, pt[:, 8:16], dtb, ALU.add, reads=[pn_, "PR"], writes=["sc8"])
                    S.op("act", "activation", GTK, GTK, AF.Exp, reads=["sc8"], writes=["sc8"])
                    S.op("act", "activation", GTK, GTK, AF.Ln, bias=1.0, scale=1.0, reads=["sc8"], writes=["sc8"])
                    S.op("dve", "tensor_tensor", GTK, GTK, NEA, ALU.mult, reads=["sc8", "NEA"], writes=["sc8"])
                    pt, pn_ = PS_()
                    S.op("pe", "matmul", pt[:, 0:8], MUi, GTK, start=True, stop=True, reads=["CK", "sc8"], writes=[pn_])
                    S.op("pe", "matmul", pt[:, 8:16], SAME, GTK, start=True, stop=True, reads=["CK", "sc8"], writes=[pn_])
                    S.op("pe", "matmul", pt[0:8, 128:256], GTK, MUi, start=True, stop=True, reads=["CK", "sc8"], writes=[pn_])
                    S.op("dve", "tensor_copy", GCT, pt[:, 0:8], reads=[pn_], writes=["sc8"])
                    S.op("dve", "tensor_copy", GLT, pt[:, 8:16], reads=[pn_], writes=["sc8"])
                    S.op("dve", "tensor_copy", GCF[0:8, :], pt[0:8, 128:256], reads=[pn_], writes=["GCF"])
                    S.op("act", "activation", EGC, GCT, AF.Exp, reads=["sc8"], writes=["sc8"])
                    S.op("dve", "tensor_tensor", NBEG, BET, EGC, ALU.mult, reads=["sc8"], writes=["sc8"])
                    S.op("dve", "tensor_scalar", NBEG, NBEG, -1.0, None, ALU.mult, reads=["sc8"], writes=["sc8"])
                    S.op("dve", "tensor_scalar", NGC, GCT, -1.0, None, ALU.mult, reads=["sc8"], writes=["sc8"])
                    S.op("dve", "tensor_tensor", EDEC, GLT, GCT, ALU.subtract, reads=["sc8"], writes=["sc8"])
                    S.op("act", "activation", EDEC, EDEC, AF.Exp, reads=["sc8"], writes=["sc8"])
                    S.op("dve", "tensor_tensor", GM.rearrange("p (b h) -> p b h", b=16)[:, 0:nb, :],
                         GTK.unsqueeze(1).to_broadcast([128, nb, 8]), rowmask[:, 0:nb].unsqueeze(2).to_broadcast([128, nb, 8]),
                         ALU.mult, reads=["sc8", "CS"], writes=["GM"])
                    pt, pn_ = PS_()
                    S.op("pe", "matmul", pt[:, 0:nb * 8], ones, GM[:, 0:nb * 8], start=True, stop=True, reads=["GM", "CS"], writes=[pn_])
                    S.op("act", "activation", EG[:, 0:nb * 8], pt[:, 0:nb * 8], AF.Exp, reads=[pn_], writes=["EG"])
                if GSTOP <= 1:
                    continue

                S.private = {"QKV", "SG", "Kdec", "Vb", "t1", "DTi", "t2", "Dms", "DTs", "Nm", "NTm", "tB", "inT", "Rm", "y0", "y1",
                             "P0", "P0T", "P1", "P1T", "IVs", "om", "Km", "junk", "GCH", "KQM"}
                G = (4 if ti > 0 else 2) if kind == "p" else 1
                BK = [(PSM[i_], "PSM%d" % i_) for i_ in range(6)] + [(PSD[0], "PSD0"), (PSD[1], "PSD1")]
                slots = [dict(QKV=QKV, SG=SG, TT=TT, GCH=GCH, banks=BK[0:6] if G == 1 else (BK[0:3] if G == 2 else BK[0:2]))]
                if G >= 2:
                    ra, rb = [1040], [4240]

                    def allocA(k):
                        o_ = ra[0]
                        ra[0] += k
                        assert ra[0] <= 4160
                        return AR[:, o_:o_ + k]

                    def allocB(k):
                        o_ = rb[0]
                        rb[0] += k
                        assert rb[0] <= 6976
                        return AR[:, o_:o_ + k]
                    slots.append(dict(QKV=allocB(3 * n).rearrange("p (c t) -> p c t", c=3),
                                      SG=allocB(nb * 128).rearrange("p (b v) -> p b v", b=nb),
                                      GCH=allocB(128), TT=[allocA(128) for _ in range(22)],
                                      banks=BK[3:6] if G == 2 else BK[2:4]))
                if G == 4:
                    S.barrier()
                    ws_busy[0] = True
                    for k_, wsx in enumerate(WS):
                        tt_ = [wsx[:, 1792 + i_ * 128:1792 + (i_ + 1) * 128] for i_ in range(18)]
                        (Kdec, Vb, t1, t2, DTs, Nm, NTm, tB, inT, Rm, y0, y1, P0, P0T, P1, P1T, om, junk) = tt_
                        slots.append(dict(QKV=wsx[:, 0:3 * n].rearrange("p (c t) -> p c t", c=3),
                                          SG=wsx[:, 1536:1664].rearrange("p (b v) -> p b v", b=nb), GCH=wsx[:, 1664:1792],
                                          TT=[Kdec, Vb, t1, t1, t2, t2, DTs, Nm, NTm, tB, inT, Rm, y0, y1, P0, P0T, P1, P1T, t1, om, junk, junk],
                                          alias={"DTi": "t1", "Dms": "t2", "IVs": "t1", "Km": "junk"},
                                          banks=BK[4 + 2 * k_:6 + 2 * k_]))

                def p_stage(h, sl):
                    QKV, SG = sl["QKV"], sl["SG"]
                    def after_qkv(bi, w, ev, eres):
                        chn = (h, 8 + h, 16 + h)[bi]
                        cw = PC[:, l, PO["gcw"]:PO["gcw"] + 96]
                        dst = QKV[:, bi, 0:n].rearrange("p (b t) -> p b t", b=nb)
                        S.op("dve", "tensor_scalar", dst, ev[:, :, 0:L], cw[:, chn:chn + 1], None, ALU.mult,
                             reads=[eres, "PC"], writes=["QKV"])
                        for wi in (1, 2, 3):
                            S.op("dve", "scalar_tensor_tensor", dst, ev[:, :, wi:wi + L], cw[:, wi * 24 + chn:wi * 24 + chn + 1], dst,
                                 ALU.mult, ALU.add, reads=[eres, "PC", "QKV"], writes=["QKV"])
                        S.op("act", "activation", QKV[:, bi, 0:n], QKV[:, bi, 0:n], AF.Silu, reads=["QKV"], writes=["QKV"])
                    halo_proj(w_in[l], C_QKV, [(h * 128, 128), (1024 + h * 128, 128), (2048 + h * 128, 128)], 3, None, None,
                              "o_gconv", after_qkv, nbuf=2, sblk=([h, 8 + h, 16 + h], 24), wk="in%d" % l)
                    for bi, scl in ((0, 128.0 ** -0.5), (1, 1.0)):
                        pt, pn_ = PS_()
                        S.op("act", "activation", L2S[:, 0:n], QKV[:, bi, 0:n], AF.Square, reads=["QKV"], writes=["L2S"])
                        S.op("pe", "matmul", pt[:, 0:n], ones, L2S[:, 0:n], start=True, stop=True, reads=["L2S", "CS"], writes=[pn_])
                        S.op("act", "activation", L2S[:, 0:n], pt[:, 0:n], AF.Sqrt, bias=1e-6, scale=1.0, reads=[pn_], writes=["L2S"])
                        S.op("dve", "reciprocal", L2S[:, 0:n], L2S[:, 0:n], reads=["L2S"], writes=["L2S"])
                        S.op("dve", "scalar_tensor_tensor", QKV[:, bi, 0:n], QKV[:, bi, 0:n], scl, L2S[:, 0:n], ALU.mult, ALU.mult,
                             reads=["L2S", "QKV"], writes=["QKV"])
                    if first:
                        S.op("dve", "memset", SG[:, :, :], 0.0, writes=["SG"])
                    else:
                        S.dma("act", out=SG[:, :, :], in_=(o_gdn["p"][l, h] if kind == "p" else st_gdn[l, h]), reads=["o_gdn"], writes=["SG"])

                def c_stage(h, sl):
                    QKV, SG, TT, GCH = sl["QKV"], sl["SG"], sl["TT"], sl["GCH"]
                    rr_ = [0]

                    def PS_():
                        bk = sl["banks"][rr_[0] % len(sl["banks"])]
                        rr_[0] += 1
                        return bk
                    for c in range(nchunk):
                        cs_ = slice(c * 128, (c + 1) * 128)
                        qT, kT, vT = QKV[:, 0, cs_], QKV[:, 1, cs_], QKV[:, 2, cs_]
                        (Kdec, Vb, t1, DTi, t2, Dms, DTs, Nm, NTm, tB, inT, Rm, y0, y1, P0, P0T, P1, P1T, IVs, om, Km, junk) = TT
                        sc8 = SC8c[c]
                        BET, GTK, GCT, EGC, NBEG, NGC, EDEC, GLT = (sc8[:, i_ * 8:(i_ + 1) * 8] for i_ in range(8))
                        GCF = GCFc[c]
                        EGv = EGc[c].rearrange("p (b h) -> p b h", b=nb)
                        hs_ = slice(h, h + 1)
                        pt, pn_ = PS_()
                        S.op("pe", "transpose", pt[:, 0:128], kT, ident, reads=["QKV", "CS"], writes=[pn_])
                        S.op("pe", "transpose", pt[:, 128:256], vT, ident, reads=["QKV", "CS"], writes=[pn_])
                        S.op("dve", "tensor_scalar", Kdec, pt[:, 0:128], EDEC[:, hs_], None, ALU.mult, reads=[pn_, "sc8"], writes=["Kdec"])
                        S.op("dve", "tensor_scalar", Vb, pt[:, 128:256], BET[:, hs_], None, ALU.mult, reads=[pn_, "sc8"], writes=["Vb"])
                        S.op("dve", "tensor_scalar", GCH[0:8, :], GCF[0:8, :], ident[0:8, hs_], None, ALU.mult, reads=["GCF", "CS"], writes=["GCH"])
                        S.op("dve", "tensor_scalar", junk[0:8, :], BTF[0:8, cs_], ident[0:8, hs_], None, ALU.mult, reads=["BTF", "CS"], writes=["junk"])
                        pg, pgn = PS_()
                        S.op("pe", "matmul", pg[:, 0:128], kT, kT, start=True, stop=True, reads=["QKV"], writes=[pgn])
                        S.op("pe", "matmul", pg[:, 128:256], kT, qT, start=True, stop=True, reads=["QKV"], writes=[pgn])
                        S.op("pe", "matmul", pg[:, 256:384], ones[0:8, :], GCH[0:8, :], start=True, stop=True, reads=["GCH", "CS"], writes=[pgn])
                        S.op("pe", "matmul", pg[:, 384:512], ones[0:8, :], junk[0:8, :], start=True, stop=True, reads=["junk", "CS"], writes=[pgn])
                        KKp, ITp, GRp, BRp = pg[:, 0:128], pg[:, 128:256], pg[:, 256:384], pg[:, 384:512]
                        if GSTOP <= 2:
                            continue
                        S.op("dve", "tensor_tensor", t1, GRp, NEGU, ALU.add, reads=[pgn, "CK"], writes=["t1"])
                        S.op("act", "activation", DTi, t1, AF.Exp, bias=NGC[:, hs_], scale=1.0, reads=["t1", "sc8"], writes=["DTi"])
                        S.op("dve", "tensor_tensor", t2, GRp, POSL, ALU.add, reads=[pgn, "CK"], writes=["t2"])
                        S.op("act", "activation", Dms, t2, AF.Exp, bias=GCT[:, hs_], scale=-1.0, reads=["t2", "sc8"], writes=["Dms"])
                        S.op("dve", "tensor_tensor", Dms, Dms, MLs, ALU.mult, reads=["Dms", "CK"], writes=["Dms"])
                        S.op("dve", "tensor_tensor", DTs, DTi, MUs, ALU.mult, reads=["DTi", "CK"], writes=["DTs"])
                        S.op("dve", "scalar_tensor_tensor", Nm, KKp, BET[:, hs_], Dms, ALU.mult, ALU.mult, reads=[pgn, "sc8", "Dms"], writes=["Nm"])
                        S.op("dve", "tensor_tensor", tB, DTs, BRp, ALU.mult, reads=[pgn, "DTs"], writes=["tB"])
                        S.op("dve", "tensor_tensor", NTm, KKp, tB, ALU.mult, reads=[pgn, "tB"], writes=["NTm"])
                        S.op("dve", "tensor_tensor", inT, ITp, DTi, ALU.mult, reads=[pgn, "DTi"], writes=["inT"])
                        if GSTOP <= 4:
                            continue
                        def state_mm(srcT, srcname):
                            pt2, pn2 = PS_()
                            if nb == 1:
                                S.op("pe", "matmul", pt2[:, 0:128], srcT, SG[:, 0, :], start=True, stop=True, reads=[srcname, "SG"], writes=[pn2])
                            else:
                                S.op("dve", "memset", KQM_flat[:, 0:2048], 0.0, writes=["KQM"])
                                S.op("dve", "tensor_copy", KQD, srcT.rearrange("p (b t) -> p b t", b=16), reads=[srcname], writes=["KQM"])
                                for b_ in range(nb):
                                    S.op("pe", "matmul", pt2[:, 0:128], KQM[:, b_, :], SG[:, b_, :], start=(b_ == 0), stop=(b_ == nb - 1),
                                         reads=["KQM", "SG"], writes=[pn2])
                            return pt2, pn2
                        pk, pkn = state_mm(kT, "QKV")
                        S.op("dve", "scalar_tensor_tensor", Rm, pk[:, 0:128], NBEG[:, hs_], Vb, ALU.mult, ALU.add, reads=[pkn, "sc8", "Vb"], writes=["Rm"])
                        if GSTOP <= 5:
                            continue
                        pt, pn_ = PS_()
                        S.op("pe", "matmul", pt[:, 0:128], NTm, Rm, start=True, stop=True, reads=["NTm", "Rm"], writes=[pn_])
                        S.op("dve", "tensor_tensor", y0, Rm, pt[:, 0:128], ALU.subtract, reads=[pn_, "Rm"], writes=["y0"])
                        ycur, yn_, yoth, yon_ = y0, "y0", y1, "y1"
                        Pc, PcT, Pcn, PcTn = Nm, NTm, "Nm", "NTm"
                        pw = [(P0, P0T, "P0", "P0T"), (P1, P1T, "P1", "P1T")]
                        for lev in range(min(levels, int(_os.environ.get("MK_G6", "99")))):
                            Pn, PnT, Pnn, PnTn = pw[lev % 2]
                            pt, pn_ = PS_()
                            S.op("pe", "matmul", pt[:, 0:128], Pc, PcT, start=True, stop=True, reads=[Pcn, PcTn], writes=[pn_])
                            if lev < levels - 1:
                                S.op("pe", "matmul", pt[:, 128:256], PcT, Pc, start=True, stop=True, reads=[Pcn, PcTn], writes=[pn_])
                            S.op("act", "activation", PnT, pt[:, 0:128], AF.Copy, reads=[pn_], writes=[PnTn])
                            if lev < levels - 1:
                                S.op("act", "activation", Pn, pt[:, 128:256], AF.Copy, reads=[pn_], writes=[Pnn])
                            pt2, pn2 = PS_()
                            S.op("pe", "matmul", pt2[:, 0:128], PnT, ycur, start=True, stop=True, reads=[PnTn, yn_], writes=[pn2])
                            S.op("dve", "tensor_tensor", yoth, ycur, pt2[:, 0:128], ALU.add, reads=[pn2, yn_], writes=[yon_])
                            ycur, yn_, yoth, yon_ = yoth, yon_, ycur, yn_
                            Pc, PcT, Pcn, PcTn = Pn, PnT, Pnn, PnTn
                        vnew, vn_ = ycur, yn_
                        if GSTOP <= 6:
                            continue
                        pq, pqn = state_mm(qT, "QKV")
                        pt, pn_ = PS_()
                        S.op("pe", "matmul", pt[:, 0:128], inT, vnew, start=True, stop=True, reads=["inT", vn_], writes=[pn_])
                        S.op("act", "activation", IVs, pt[:, 0:128], AF.Copy, reads=[pn_], writes=["IVs"])
                        S.op("dve", "scalar_tensor_tensor", om, pq[:, 0:128], EGC[:, hs_], IVs, ALU.mult, ALU.add, reads=[pqn, "sc8", "IVs"], writes=["om"])
                        if GSTOP <= 7:
                            continue
                        ssq = junk[:, 0:1]
                        S.op("act", "activation", t1, om, AF.Square, accum_out=ssq, reads=["om"], writes=["t1", "junk"])
                        S.op("act", "activation", ssq, ssq, AF.Sqrt, bias=1e-6, scale=1.0 / 128, reads=["junk"], writes=["junk"])
                        S.op("dve", "reciprocal", ssq, ssq, reads=["junk"], writes=["junk"])
                        S.op("dve", "tensor_scalar", om, om, ssq, None, ALU.mult, reads=["junk", "om"], writes=["om"])
                        pt, pn_ = PS_()
                        S.op("pe", "transpose", pt[:, 0:128], om, ident, reads=["om", "CS"], writes=[pn_])
                        S.op("dve", "scalar_tensor_tensor", YBR[:, 8 + h, cs_], pt[:, 0:128], PC[:, l, PO["gng"]:PO["gng"] + 1], ZS[:, h, cs_],
                             ALU.mult, ALU.mult, reads=[pn_, "PC", "ZS"], writes=["YBR"])
                        if GSTOP <= 8:
                            continue
                        if nb == 1:
                            pt, pn_ = PS_()
                            S.op("pe", "matmul", pt[:, 0:128], Kdec, vnew, start=True, stop=True, reads=["Kdec", vn_], writes=[pn_])
                            S.op("dve", "scalar_tensor_tensor", SG[:, 0, :], SG[:, 0, :], EGv[:, 0, hs_], pt[:, 0:128], ALU.mult, ALU.add,
                                 reads=[pn_, "EG", "SG"], writes=["SG"])
                        else:
                            S.op("dve", "tensor_tensor", KQM, Kdec.unsqueeze(1).to_broadcast([128, 16, 128]),
                                 rowmask.unsqueeze(2).to_broadcast([128, 16, 128]), ALU.mult, reads=["Kdec", "CS"], writes=["KQM"])
                            for b_ in range(nb):
                                pt, pn_ = PS_()
                                S.op("pe", "matmul", pt[:, 0:128], KQM[:, b_, :], vnew, start=True, stop=True, reads=["KQM", vn_], writes=[pn_])
                                S.op("dve", "scalar_tensor_tensor", SG[:, b_, :], SG[:, b_, :], EGv[:, b_, hs_], pt[:, 0:128], ALU.mult, ALU.add,
                                     reads=[pn_, "EG", "SG"], writes=["SG"])
                    S.dma("act", out=o_gdn[kind][l, h], in_=SG[:, :, :], reads=["SG"], writes=["o_gdn"])

                for hg in range(0, 8, G):
                    for si in range(G):
                        S.sfx = "@%d" % si
                        S.alias = slots[si].get("alias", {})
                        p_stage(hg + si, slots[si])
                    ths = []
                    for si in range(G):
                        S.sfx = "@%d" % si
                        S.alias = slots[si].get("alias", {})
                        S.thread_begin()
                        c_stage(hg + si, slots[si])
                        ths.append(S.thread_end())
                    S.sfx = ""
                    S.alias = {}
                    S.replay(ths)
                if G == 4:
                    S.barrier()
                    ws_busy[0] = False
                S.dma("act", out=o_gconv[kind][l], in_=gst, reads=["STT"], writes=["o_gconv"])
                S.barrier()

                if stage < 2:
                    continue
                BR = [(0, 4), (4, 4), (8, 8)]
                for op_ in range(8):
                    def cons_gate(br):
                        def f(col, w, pap, pres):
                            sub = ((col - C_G) // 128) % 2
                            g = GT[br * 2 + sub]
                            S.op("act", "activation", g[:, 0:n], pap, AF.Sigmoid, reads=[pres], writes=["GT%d" % (br * 2 + sub)])
                        return f
                    for br in range(3):
                        c0 = C_G + br * D + op_ * 256
                        linear(w_in[l], 0, D, [(c0, 256, [(0, 128), (128, 128)])], HN, n, cons_gate(br), wk="in%d" % l)
                    for br in range(3):
                        def cons_br(col, w, pap, pres, br=br):
                            sub = (col // 128) % 2
                            g = GT[br * 2 + sub]
                            if br == 0:
                                S.op("dve", "tensor_tensor", ACC[sub][:, 0:n], g[:, 0:n], pap, ALU.mult,
                                     reads=[pres, "GT%d" % (br * 2 + sub)], writes=["ACC%d" % sub])
                            else:
                                S.op("dve", "tensor_tensor", TMP[sub][:, 0:n], g[:, 0:n], pap, ALU.mult,
                                     reads=[pres, "GT%d" % (br * 2 + sub)], writes=["TMP%d" % sub])
                                dst = ACC[sub][:, 0:n] if br == 1 else MRG[:, op_ * 2 + sub, 0:n]
                                S.op("dve", "tensor_tensor", dst, ACC[sub][:, 0:n], TMP[sub][:, 0:n], ALU.add,
                                     reads=["ACC%d" % sub, "TMP%d" % sub], writes=["ACC%d" % sub] if br == 1 else ["MRG"])
                        kc0, kn = BR[br]
                        linear(w_brs[br][l], 0, kn * 128, [(op_ * 256, 256, [(0, 128), (128, 128)])], YBR, n, cons_br, kc0=kc0, wk="br%d%d" % (br, l))

                def cons_resid(col, w, pap, pres):
                    o = col // 128
                    S.op("dve", "tensor_tensor", X[:, o, 0:n], X[:, o, 0:n], pap, ALU.add, reads=[pres, "X"], writes=["X"])
                linear(w_out[l], 0, D, [(o2 * 256, 256, [(0, 128), (128, 128)]) for o2 in range(8)], MRG, n, cons_resid, wk="out%d" % l)
                if stage < 3:
                    continue
                rmsnorm(PC[:, l, PO["g2"]:PO["g2"] + 16], n, HN, "HN")
                fin_ = o_fconv["p"][l] if kind == "p" else st_fconv[l]
                svf = STT[:, 0:88 * nb * 2].rearrange("p (c b h) -> p c b h", c=88, b=nb)
                if first:
                    S.op("dve", "memset", STT[:, 0:88 * nb * 2], 0.0, writes=["STT"])
                else:
                    S.dma("act", out=svf, in_=fin_, reads=["o_fconv"], writes=["STT"])
                for j in range(11):
                    def after_ffn(bi, w, ev, eres, j=j):
                        ch = (j * 4 + bi) if bi < 4 else (44 + j * 4 + (bi - 4))
                        cw = PC[:, l, PO["fcw"]:PO["fcw"] + 264]
                        cb = PC[:, l, PO["fcb"] + ch:PO["fcb"] + ch + 1]
                        t = GT[bi % 6] if False else (TMP[0] if bi < 4 else TMP[1])
                        tn = "TMP0" if bi < 4 else "TMP1"
                        tv = t[:, 0:n].rearrange("p (b t) -> p b t", b=nb)
                        S.op("dve", "tensor_scalar", tv, ev[:, :, 0:L], cw[:, ch:ch + 1], cb, ALU.mult, ALU.add,
                             reads=[eres, "PC"], writes=[tn])
                        for wi in (1, 2):
                            S.op("dve", "scalar_tensor_tensor", tv, ev[:, :, wi:wi + L], cw[:, wi * 88 + ch:wi * 88 + ch + 1], tv,
                                 ALU.mult, ALU.add, reads=[eres, "PC", tn], writes=[tn])
                        if bi < 4:
                            S.op("act", "activation", GT[bi][:, 0:n], t[:, 0:n], AF.Silu, reads=[tn], writes=["GT%d" % bi])
                        else:
                            S.op("dve", "tensor_tensor", AJ[:, bi - 4, 0:n], GT[bi - 4][:, 0:n], t[:, 0:n], ALU.mult,
                                 reads=[tn, "GT%d" % (bi - 4)], writes=["AJ"])
                    blocks = [(j * 512 + i * 128, 128) for i in range(4)] + [(DFF + j * 512 + i * 128, 128) for i in range(4)]
                    halo_proj(w_up[l], 0, blocks[:4], 2, None, None, "o_fconv", after_ffn, nbuf=8, sblk=(j * 4, 88), wk="up%d" % l)
                    halo_proj(w_up[l], 0, blocks[4:], 2, None, None, "o_fconv",
                              lambda bi, w, ev, eres: after_ffn(bi + 4, w, ev, eres), nbuf=8, sblk=(44 + j * 4, 88), wk="up%d" % l)
                    for half in range(2):
                        linear(w_down[l], j * 512, 512, [(half * 1024, 1024, [(i * 128, 128) for i in range(8)])], AJ, n, cons_resid, aname="AJ", wk="dn%d" % l)
                S.dma("act", out=o_fconv[kind][l], in_=svf, reads=["STT"], writes=["o_fconv"])
            final_out(kind, ti, n)
        S.final_wait("sp")
        block = es.enter_context(nc.Block())

        @block.sync
        def _(e):
            for f in S.prog["sp"]:
                f(e)

        @block.tensor
        def _(e):
            for f in S.prog["pe"]:
                f(e)

        @block.scalar
        def _(e):
            for f in S.prog["act"]:
                f(e)

        @block.vector
        def _(e):
            for f in S.prog["dve"]:
                f(e)

        @block.gpsimd
        def _(e):
            for f in S.prog["pool"]:
                f(e)
    return nc, S


PO = {}
_o = 0
for _name, _w in [("g1", 16), ("g2", 16), ("gf", 16), ("mu", 16), ("fcw", 264), ("fcb", 88), ("lre", 16), ("lim", 16), ("lst", 16), ("s5d", 4), ("gcw", 96), ("gng", 1), ("nw0", 4), ("a0", 4), ("kk", 4), ("ka", 4), ("rk", 4), ("lnw", 4), ("lnb", 4)]:
    PO[_name] = _o
    _o += _w
NPC = _o
CI = {"ident": 0, "ones": 1, "iota_p": 2, "reset_p": 6, "iota_s": 10, "reset_s": 11, "rowmask": 12, "headblk": 13}
NCST = 14


def _rw_blockcols(v):
    out = np.zeros((128, 16), np.float32)
    for i, (o, w) in enumerate(RW_BLOCKS):
        out[:w, i] = v[o:o + w]
    return out


def host_inputs(inputs, core):
    p = core // 2
    sl = slice(NSEQ_S * core, NSEQ_S * (core + 1))
    m = {}
    m["xp"] = np.ascontiguousarray(inputs["x_prompt"][p])
    m["xs"] = np.ascontiguousarray(inputs["x_sample"][sl].reshape(128, D))
    m["w_in"] = inputs["w_in"]
    G, P, GS = 32, 64, 16
    bB = np.zeros((DEPTH, 2, 512, 2048), np.float32)
    bC = np.zeros((DEPTH, 2, 2048, 512), np.float32)
    for g in range(G):
        for ri, (bk, ck) in enumerate((("s5_b_re", "s5_c_re"), ("s5_b_im", "s5_c_im"))):
            bB[:, ri, g * GS:(g + 1) * GS, g * P:(g + 1) * P] = inputs[bk][:, g].transpose(0, 2, 1)
            bC[:, ri, g * P:(g + 1) * P, g * GS:(g + 1) * GS] = inputs[ck][:, g].transpose(0, 2, 1)
    m["s5B"], m["s5C"] = bB, bC
    s5 = np.stack([inputs["state_s5_re"][:, sl], inputs["state_s5_im"][:, sl]], axis=2)
    m["st_s5"] = np.ascontiguousarray(s5.reshape(DEPTH, NSEQ_S, 2, 16, 128).transpose(0, 4, 2, 3, 1))
    m["s5_w_glu"] = inputs["s5_w_glu"]
    for k in ("w_br_s5", "w_br_rwkv", "w_br_gdn", "w_out", "ffn_w_up", "ffn_w_down"):
        m[k] = inputs[k]
    fc = inputs["state_ffn_conv"][:, sl]
    m["st_fconv"] = np.ascontiguousarray(fc.reshape(DEPTH, NSEQ_S, 2, 88, 128).transpose(0, 4, 3, 1, 2))
    pc = np.zeros((DEPTH, 128, NPC), np.float32)
    for l in range(DEPTH):
        pc[l, :, PO["g1"]:PO["g1"] + 16] = _col(inputs["norm1_g"][l])
        pc[l, :, PO["g2"]:PO["g2"] + 16] = _col(inputs["norm2_g"][l])
        pc[l, :, PO["gf"]:PO["gf"] + 16] = _col(inputs["final_norm_g"])
        pc[l, :, PO["mu"]:PO["mu"] + 16] = _rw_blockcols(inputs["rwkv_mu"][l])
        pc[l, :, PO["fcw"]:PO["fcw"] + 264] = _col(inputs["ffn_conv_w"][l]).transpose(1, 0, 2).reshape(128, 264)
        pc[l, :, PO["fcb"]:PO["fcb"] + 88] = _col(inputs["ffn_conv_b"][l])
        pc[l, :, PO["lre"]:PO["lre"] + 16] = _col(inputs["s5_lambda_re"][l].reshape(-1))
        pc[l, :, PO["lim"]:PO["lim"] + 16] = _col(inputs["s5_lambda_im"][l].reshape(-1))
        pc[l, :, PO["lst"]:PO["lst"] + 16] = _col(np.repeat(inputs["s5_log_step"][l], 64))
        pc[l, :, PO["s5d"]:PO["s5d"] + 4] = _col(inputs["s5_d"][l])
        pc[l, :, PO["gcw"]:PO["gcw"] + 96] = _col(inputs["gdn_conv_w"][l]).transpose(1, 0, 2).reshape(128, 96)
        pc[l, :, PO["gng"]] = inputs["gdn_norm_g"][l]
        pc[l, :, PO["nw0"]:PO["nw0"] + 4] = _col(inputs["rwkv_w0"][l])
        pc[l, :, PO["a0"]:PO["a0"] + 4] = _col(inputs["rwkv_a0"][l])
        pc[l, :, PO["kk"]:PO["kk"] + 4] = _col(inputs["rwkv_k_k"][l])
        pc[l, :, PO["ka"]:PO["ka"] + 4] = _col(inputs["rwkv_k_a"][l])
        pc[l, :, PO["rk"]:PO["rk"] + 4] = _col(inputs["rwkv_r_k"][l].reshape(-1))
        pc[l, :, PO["lnw"]:PO["lnw"] + 4] = _col(inputs["rwkv_ln_w"][l])
        pc[l, :, PO["lnb"]:PO["lnb"] + 4] = _col(inputs["rwkv_ln_b"][l])
    m["pcol"] = pc
    cs = np.zeros((NCST, 128, 128), np.float32)
    cs[CI["ident"]] = np.eye(128)
    cs[CI["ones"]] = 1.0
    cs[CI["iota_p"]:CI["iota_p"] + 4] = (np.arange(512, dtype=np.float32) + 1).reshape(4, 1, 128)
    rp = np.ones(512, np.float32); rp[0] = 0
    cs[CI["reset_p"]:CI["reset_p"] + 4] = rp.reshape(4, 1, 128)
    cs[CI["rowmask"]][:, :NSEQ_S] = (np.arange(128)[:, None] // LS == np.arange(NSEQ_S)[None, :])
    hb = np.arange(128) // 64
    cs[CI["headblk"]] = (hb[:, None] == hb[None, :])
    cs[CI["iota_s"]] = (np.arange(128) % LS + 1).astype(np.float32)[None, :]
    cs[CI["reset_s"]] = (np.arange(128) % LS != 0).astype(np.float32)[None, :]
    m["cst"] = cs
    ck = np.zeros((2, 6, 128, 128), np.float32)
    for ki, kd in enumerate(("p", "s")):
        mm = _masks(kd)
        for j, nm in enumerate(("MUi", "MUs", "MLs", "NEGU", "POSL")):
            ck[ki, j] = mm[nm]
        idx = np.arange(128) // LS if kd == "s" else np.zeros(128, np.int64)
        ck[ki, 5] = (idx[:, None] == idx[None, :])
    m["cstk"] = ck
    pr = np.zeros((DEPTH, 128, 16), np.float32)
    pr[:, :, 0:8] = inputs["gdn_a_log"][:, None, :]
    pr[:, :, 8:16] = inputs["gdn_dt_bias"][:, None, :]
    m["prow"] = pr
    for k_ in ("rwkv_w2", "rwkv_a2", "rwkv_g2"):
        m[k_] = inputs[k_]
    wk = inputs["state_rwkv_wkv"][:, sl]
    m["st_wkv"] = np.ascontiguousarray(wk.transpose(0, 2, 4, 1, 3).reshape(DEPTH, 4, 128, NSEQ_S, 64))
    m["st_gdn"] = np.ascontiguousarray(inputs["state_gdn"][:, sl].transpose(0, 2, 3, 1, 4))
    sh = inputs["state_rwkv_shift"][:, sl]
    t = np.zeros((DEPTH, 128, 16, NSEQ_S, 1), np.float32)
    for i, (o, w) in enumerate(RW_BLOCKS):
        t[:, :w, i, :, 0] = sh[:, :, o:o + w].transpose(0, 2, 1)
    m["st_shift"] = t
    gc = inputs["state_gdn_conv"][:, sl]
    m["st_gconv"] = np.ascontiguousarray(gc.reshape(DEPTH, NSEQ_S, 3, 24, 128).transpose(0, 4, 3, 1, 2))
    return m


_CACHE = {}


def _unblock_shift(a):
    nb = a.shape[3]
    out = np.zeros((DEPTH, nb, 1984), np.float32)
    for i, (o, w) in enumerate(RW_BLOCKS):
        out[:, :, o:o + w] = a[:, :w, i, :, 0].transpose(0, 2, 1)
    return out


def _unchunk(a):
    l, p, c, nb, h = a.shape
    return np.ascontiguousarray(a.transpose(0, 3, 4, 2, 1).reshape(l, nb, h, c * p))


def _uns5(a, ri):
    nb = a.shape[4]
    return np.ascontiguousarray(a[:, :, ri].transpose(0, 3, 2, 1).reshape(DEPTH, nb, 32, 64))


def _unwkv(a):
    nb = a.shape[3]
    return np.ascontiguousarray(a.reshape(DEPTH, 4, 2, 64, nb, 64).transpose(0, 4, 1, 2, 5, 3).reshape(DEPTH, nb, 8, 64, 64))


def _ungdn(a):
    return np.ascontiguousarray(a.transpose(0, 3, 1, 2, 4))


def kernel(**inputs):
    inputs = {k: np.asarray(v) for k, v in inputs.items()}
    if "nc" not in _CACHE:
        _CACHE["nc"] = build_program()
    nc, S = _CACHE["nc"]
    in_maps = [host_inputs(inputs, c) for c in range(8)]
    res = run_bass_kernel_spmd(nc, in_maps, core_ids=list(range(8))).results
    B = 4
    P = [res[2 * p] for p in range(B)]
    cat = lambda xs: np.concatenate(xs, axis=1)
    y_p = np.stack([r["y_p"] for r in P])
    y_s = np.concatenate([r["y_s"].reshape(NSEQ_S, LS, D) for r in res])
    outs = [y_p, y_s]
    for grp, key in ((P, "p"), (res, "s")):
        outs += [cat([_uns5(r["o_s5_" + key], 0) for r in grp]),
                 cat([_uns5(r["o_s5_" + key], 1) for r in grp]),
                 cat([_unblock_shift(r["o_shift_" + key]) for r in grp]),
                 cat([_unwkv(r["o_wkv_" + key]) for r in grp]),
                 cat([_unchunk(r["o_gconv_" + key]) for r in grp]),
                 cat([_ungdn(r["o_gdn_" + key]) for r in grp]),
                 cat([_unchunk(r["o_fconv_" + key]) for r in grp])]
    return tuple(outs)
```
